# Optimizing a Trainium2 kernel written in Bass

```python
import jax, jax.numpy as jnp
from jax import lax
import numpy as np

D_MODEL = 4096
BATCH = 2
SEQ = 8192
DEPTH = 1

D_MIX = D_MODEL
HEAD_DIM = 128
ATTN_WIDTH = D_MIX // 2
N_ATTN_HEADS = ATTN_WIDTH // HEAD_DIM
CONV_WIDTH = D_MIX - ATTN_WIDTH
CONV_GROUP = 128
N_CONV_GROUPS = CONV_WIDTH // CONV_GROUP
CONV_K = 3
Q_BLOCK = 128
D_FF = ((8 * D_MODEL // 3 + 255) // 256) * 256
N_MOD = 6
EPS = 1e-6
IN_SPLITS = (ATTN_WIDTH, ATTN_WIDTH, ATTN_WIDTH, N_ATTN_HEADS, CONV_WIDTH, CONV_WIDTH, CONV_WIDTH)
N_IN = sum(IN_SPLITS)

kernel_name = "hymba_conv_fox_sandwich_adaln_block"


def rms_norm(x, g):
    xf = x.astype(jnp.float32)
    y = xf * lax.rsqrt(jnp.mean(xf * xf, axis=-1, keepdims=True) + EPS)
    return (y * g.astype(jnp.float32)).astype(x.dtype)


def modulate(h, shift, scale):
    return h * (1 + scale[:, None, :]) + shift[:, None, :]


def causal_short_conv(u, w):
    s = u.shape[1]
    u_pad = jnp.pad(u, ((0, 0), (CONV_K - 1, 0), (0, 0)))
    return sum(w[k][None, None, :] * u_pad[:, k:k + s, :] for k in range(CONV_K))


def forgetting_attention(q, k, v, f_logit):
    b, s, h, dh = q.shape
    scale = dh ** -0.5
    log_f = jax.nn.log_sigmoid(f_logit.astype(jnp.float32))
    F = jnp.cumsum(log_f, axis=1).transpose(0, 2, 1)
    qh = q.transpose(0, 2, 1, 3)
    kh = k.transpose(0, 2, 1, 3)
    vh = v.transpose(0, 2, 1, 3)
    nb = s // Q_BLOCK
    qb = qh.reshape(b, h, nb, Q_BLOCK, dh).transpose(2, 0, 1, 3, 4)
    Fb = F.reshape(b, h, nb, Q_BLOCK).transpose(2, 0, 1, 3)
    kpos = jnp.arange(s)

    def block(args):
        q_blk, F_blk, i = args
        logits = jnp.einsum('bhqd,bhkd->bhqk', q_blk, kh).astype(jnp.float32) * scale
        logits = logits + F_blk[..., None] - F[:, :, None, :]
        qpos = i * Q_BLOCK + jnp.arange(Q_BLOCK)
        mask = kpos[None, :] <= qpos[:, None]
        logits = jnp.where(mask[None, None], logits, -jnp.inf)
        p = jax.nn.softmax(logits, axis=-1)
        return jnp.einsum('bhqk,bhkd->bhqd', p.astype(vh.dtype), vh)

    out = lax.map(block, (qb, Fb, jnp.arange(nb)))
    return out.transpose(1, 0, 3, 2, 4).reshape(b, s, h * dh)


def hybrid_mixer(h, w_in, b_f, conv_w, attn_out_norm, conv_out_norm, w_out):
    b, s, _ = h.shape
    proj = jnp.einsum('bsd,dn->bsn', h, w_in)
    idx = list(np.cumsum(IN_SPLITS)[:-1])
    q, k, v, f_logit, gate_b, gate_c, u = jnp.split(proj, idx, axis=-1)
    q = q.reshape(b, s, N_ATTN_HEADS, HEAD_DIM)
    k = k.reshape(b, s, N_ATTN_HEADS, HEAD_DIM)
    v = v.reshape(b, s, N_ATTN_HEADS, HEAD_DIM)
    y_attn = forgetting_attention(q, k, v, f_logit + b_f)
    y_conv = gate_b * causal_short_conv(gate_c * u, conv_w)
    y = jnp.concatenate([rms_norm(y_attn, attn_out_norm), rms_norm(y_conv, conv_out_norm)], axis=-1)
    return jnp.einsum('bsm,md->bsd', y, w_out)


def swiglu(h, w_gate, w_up, w_down):
    g = jnp.einsum('bsd,df->bsf', h, w_gate)
    u = jnp.einsum('bsd,df->bsf', h, w_up)
    return jnp.einsum('bsf,fd->bsd', jax.nn.silu(g) * u, w_down)


def setup_inputs(seed: int = 0) -> dict:
    key = jax.random.key(seed)
    ks = jax.random.split(key, 20)
    d = D_MODEL
    L = DEPTH
    nrm = lambda k, shape, s: jax.random.normal(k, shape, jnp.float32) * s
    gain = lambda k, n: 1.0 + nrm(k, (L, n), 0.05)
    return {
        "x": nrm(ks[0], (BATCH, SEQ, d), 1.0),
        "c": nrm(ks[1], (BATCH, d), 1.0),
        "w_ada": nrm(ks[2], (L, d, N_MOD * d), 0.5 * d ** -0.5),
        "b_ada": nrm(ks[3], (L, N_MOD * d), 0.01),
        "pre_norm_mix": gain(ks[4], d),
        "w_in": nrm(ks[5], (L, d, N_IN), d ** -0.5),
        "b_f": jax.random.uniform(ks[6], (L, N_ATTN_HEADS), jnp.float32, 1.0, 4.0),
        "conv_w": nrm(ks[7], (L, CONV_K, CONV_WIDTH), CONV_K ** -0.5),
        "attn_out_norm": gain(ks[8], ATTN_WIDTH),
        "conv_out_norm": gain(ks[9], CONV_WIDTH),
        "w_out": nrm(ks[10], (L, D_MIX, d), D_MIX ** -0.5),
        "post_norm_mix": gain(ks[11], d),
        "pre_norm_ffn": gain(ks[12], d),
        "w_gate": nrm(ks[13], (L, d, D_FF), d ** -0.5),
        "w_up": nrm(ks[14], (L, d, D_FF), d ** -0.5),
        "w_down": nrm(ks[15], (L, D_FF, d), D_FF ** -0.5),
        "post_norm_ffn": gain(ks[16], d),
    }


def reference(x, c, w_ada, b_ada, pre_norm_mix, w_in, b_f, conv_w, attn_out_norm,
              conv_out_norm, w_out, post_norm_mix, pre_norm_ffn, w_gate, w_up, w_down,
              post_norm_ffn):
    c_act = jax.nn.silu(c)
    for l in range(DEPTH):
        mod = jnp.einsum('bd,dm->bm', c_act, w_ada[l]) + b_ada[l]
        sh1, sc1, g1, sh2, sc2, g2 = jnp.split(mod, N_MOD, axis=-1)
        h = modulate(rms_norm(x, pre_norm_mix[l]), sh1, sc1)
        h = hybrid_mixer(h, w_in[l], b_f[l], conv_w[l], attn_out_norm[l], conv_out_norm[l], w_out[l])
        x = x + g1[:, None, :] * rms_norm(h, post_norm_mix[l])
        h = modulate(rms_norm(x, pre_norm_ffn[l]), sh2, sc2)
        h = swiglu(h, w_gate[l], w_up[l], w_down[l])
        x = x + g2[:, None, :] * rms_norm(h, post_norm_ffn[l])
    return x
```

```python
import os
import numpy as np
import concourse.bass as bass
import concourse.mybir as mybir
from concourse.bass_utils import run_bass_kernel_spmd

F32 = mybir.dt.float32
BF16 = mybir.dt.bfloat16
AF = mybir.ActivationFunctionType
ALU = mybir.AluOpType

D = 4096
S = 8192
NCORES = 8
DFF = 11008
NFC = DFF // 128
EPS = 1e-6
P = 128
TT1 = 1024
NT1 = S // TT1
TT2 = 512
TOK2 = 2048
NT2 = TOK2 // TT2
NW1 = 3072 + 4

ENG = ['sync', 'scalar', 'vector', 'gpsimd', 'tensor']
COMPUTE = ['scalar', 'vector', 'gpsimd', 'tensor']


class Tr:
    def __init__(s, nc):
        s.nc = nc
        s.prog = {e: [] for e in ENG}
        s.esem = {e: [nc.alloc_semaphore("prog_" + e), 0] for e in COMPUTE}
        s.dsem = {}
        s.res = {}
        s.waited = {e: {} for e in ENG}

    def _handle(s, semkey):
        kind, k = semkey
        return s.esem[k][0] if kind == 'e' else s.dsem[k][0]

    def _need(s, e, toks):
        best = {}
        for (semkey, v) in toks:
            if semkey == ('e', 'tensor') and e == 'tensor':
                continue
            if v > best.get(semkey, 0):
                best[semkey] = v
        for semkey, v in best.items():
            if s.waited[e].get(semkey, 0) >= v:
                continue
            s.waited[e][semkey] = v
            h = s._handle(semkey)
            s.prog[e].append(lambda eng, h=h, v=v: eng.wait_ge(h, v))

    def op(s, e, fn, reads=(), writes=(), dma=None):
        writes = list(writes) + [r for r in reads if r.startswith('bk') and r not in writes]
        toks = []
        for r in reads:
            st = s.res.get(r)
            if st and st[0]:
                toks.append(st[0])
        for w in writes:
            st = s.res.get(w)
            if st:
                if st[0]:
                    toks.append(st[0])
                toks.extend(st[1].items())
        s._need(e, toks)
        if dma is not None:
            if dma not in s.dsem:
                s.dsem[dma] = [s.nc.alloc_semaphore("d_" + str(dma)), 0]
            d = s.dsem[dma]
            d[1] += 16
            tok = (('d', dma), d[1])
            h, inc = d[0], 16
        else:
            d = s.esem[e]
            d[1] += 1
            tok = (('e', e), d[1])
            h, inc = d[0], 1
        s.prog[e].append(lambda eng, fn=fn, h=h, inc=inc: fn(eng).then_inc(h, inc))
        for w in writes:
            s.res[w] = [tok, {}]
        for r in reads:
            st = s.res.setdefault(r, [None, {}])
            if tok[1] > st[1].get(tok[0], 0):
                st[1][tok[0]] = tok[1]
        return tok

    def barrier(s):
        toks = [(('e', f), s.esem[f][1]) for f in COMPUTE if s.esem[f][1] > 0]
        toks += [(('d', k), v[1]) for k, v in s.dsem.items()]
        for e in ENG:
            for semkey, v in toks:
                if s.waited[e].get(semkey, 0) >= v:
                    continue
                s.waited[e][semkey] = v
                h = s._handle(semkey)
                s.prog[e].append(lambda eng, h=h, v=v: eng.wait_ge(h, v))
        s.res = {}

    def coll(s, fn, reads=(), writes=()):
        e = 'gpsimd'
        toks = []
        for r in reads:
            st = s.res.get(r)
            if st and st[0]:
                toks.append(st[0])
        for w in writes:
            st = s.res.get(w)
            if st:
                if st[0]:
                    toks.append(st[0])
                toks.extend(st[1].items())
        s._need(e, toks)
        if 'cc' not in s.dsem:
            s.dsem['cc'] = [s.nc.alloc_semaphore("cc"), 0]
        d = s.dsem['cc']
        d[1] += 1
        tok = (('d', 'cc'), d[1])
        h = d[0]
        s.prog[e].append(lambda eng: fn(eng).then_inc(h))
        for w in writes:
            s.res[w] = [tok, {}]
        return tok

    def emit(s, block):
        def mk(e):
            def body(eng):
                for fn in s.prog[e]:
                    fn(eng)
            return body
        block.sync(mk('sync'))
        block.scalar(mk('scalar'))
        block.vector(mk('vector'))
        block.gpsimd(mk('gpsimd'))
        block.tensor(mk('tensor'))


def build_nc(stop_after=9, debug=False):
    nc = bass.Bass("TRN2", target_bir_lowering=False)

    def din(name, shape, dt=F32):
        return nc.dram_tensor(name, shape, dt, kind="ExternalInput").ap()

    x_full = din("x_full", [S, D])
    x_chunk = din("x_chunk", [TOK2, D])
    c_col_in = din("c_col", [P, 32])
    w_ada = din("w_ada", [D, 6 * D])
    b_ada = din("b_ada", [1, 6 * D])
    pre1_in = din("pre1_col", [P, 32])
    pre2_in = din("pre2_col", [P, 32])
    post1_in = din("post1_row", [1, D])
    post2_in = din("post2_row", [1, D])
    w1 = din("w1", [D, NW1])
    bf_in = din("bf_row", [1, 4])
    wf_in = din("wf_col", [P, 128])
    convw_in = din("convw_col", [P, 12])
    gy_in = din("gy_col", [P, 32])
    w_out = din("w_out_perm", [D, D])
    w_gate = din("w_gate", [D, DFF])
    w_up = din("w_up", [D, DFF])
    w_down = din("w_down", [DFF, D])
    out = nc.dram_tensor("out", [TOK2, D], F32, kind="ExternalOutput").ap()
    if debug:
        dbg = nc.dram_tensor("dbg", [P, 8192], F32, kind="ExternalOutput").ap()

    Gs = nc.dram_tensor("Gs", [2, D], F32)
    qTs = nc.dram_tensor("qTs", [4 * P, S], BF16)
    kTs = nc.dram_tensor("kTs", [4 * P, S], BF16)
    Vs = nc.dram_tensor("Vs", [S, 512], BF16)
    ybuf = nc.dram_tensor("ybuf", [16384, 512], BF16)
    yslots = nc.dram_tensor("yslots", [16384, 512], BF16)
    ymine = nc.dram_tensor("ymine", [16384, 512], BF16)
    x1s = nc.dram_tensor("x1s", [TOK2, D], F32)
    accs = nc.dram_tensor("accs", [TOK2, D], F32)

    BASE = 16512
    LIMIT = 229344

    class Arena:
        def __init__(self, start):
            self.off = start
            self.n = 0

        def alloc(self, shape, dt, name=None):
            nbytes = int(np.prod(shape[1:])) * (4 if dt == F32 else 2)
            nbytes = (nbytes + 63) // 64 * 64
            self.n += 1
            t = nc.alloc_sbuf_tensor_at(name or ("t%d_%d" % (self.off, self.n)), list(shape), dt, offset=self.off)
            self.off += nbytes
            assert self.off <= LIMIT, (self.off, LIMIT)
            return t

    A0 = Arena(BASE)
    ident_bf = A0.alloc([P, P], BF16, "ident_bf")
    ident_f = A0.alloc([P, P], F32, "ident_f")
    tri_bf = A0.alloc([P, P], BF16, "tri_bf")
    tri_f = A0.alloc([P, P], F32, "tri_f")
    ones_bf = A0.alloc([P, P], BF16, "ones_bf")
    ones_f = A0.alloc([P, P], F32, "ones_f")
    gain1c = A0.alloc([P, 32], F32, "gain1c")
    shift1c = A0.alloc([P, 32], F32, "shift1c")
    gain2c = A0.alloc([P, 32], F32, "gain2c")
    shift2c = A0.alloc([P, 32], F32, "shift2c")
    sc1c = A0.alloc([P, 32], F32, "sc1c")
    sc2c = A0.alloc([P, 32], F32, "sc2c")
    pre1c = A0.alloc([P, 32], F32, "pre1c")
    pre2c = A0.alloc([P, 32], F32, "pre2c")
    gyc = A0.alloc([P, 32], F32, "gyc")
    ccol = A0.alloc([P, 32], F32, "ccol")
    cact = A0.alloc([P, 32], F32, "cact")
    SP = A0.alloc([P, 256], F32, "SP")
    wf_sb = A0.alloc([P, 32, 4], BF16, "wf_sb")
    bfrep = A0.alloc([P, 32], F32, "bfrep")
    convw = A0.alloc([P, 12], F32, "convw")
    halo = A0.alloc([P, 4, 2], F32, "halo")
    stat = A0.alloc([P, 64], F32, "stat")
    junk_bf = A0.alloc([P, 512], BF16, "junk_bf")
    junk_f = A0.alloc([P, P], F32, "junk_f")
    P0 = BASE + 8192
    assert A0.off <= P0, A0.off

    pp = [nc.alloc_psum_tensor("pp%d" % i, [P, 1024], F32) for i in range(4)]
    ps = [pp[i // 2][:, (i % 2) * 512:(i % 2 + 1) * 512] for i in range(8)]
    psn = ['bk%d' % i for i in range(8)]

    T = Tr(nc)
    op = T.op

    def mk_const(t, val, cmp_op=None):
        op('gpsimd', lambda e: e.memset(t[:], val), writes=[t.name])
        if cmp_op is not None:
            op('gpsimd', lambda e: e.affine_select(out=t[:], in_=t[:], pattern=[[1, P]], compare_op=cmp_op,
                                                   fill=0.0, base=0, channel_multiplier=-1),
               reads=[t.name], writes=[t.name])

    mk_const(ident_bf, 1.0, ALU.is_equal)
    mk_const(ident_f, 1.0, ALU.is_equal)
    mk_const(tri_bf, 1.0, ALU.is_ge)
    mk_const(tri_f, 1.0, ALU.is_ge)
    mk_const(ones_bf, 1.0)
    mk_const(ones_f, 1.0)
    op('gpsimd', lambda e: e.memset(halo[:], 0.0), writes=['halo'])

    def small_load(dst, src, key, name):
        op('sync', lambda e: e.dma_start(out=dst, in_=src), writes=[name], dma=key)

    small_load(ccol[:], c_col_in[:, :], 'm0', 'ccol')
    small_load(pre1c[:], pre1_in[:, :], 'm1', 'pre1c')
    small_load(pre2c[:], pre2_in[:, :], 'm2', 'pre2c')
    small_load(gyc[:], gy_in[:, :], 'm3', 'gyc')
    small_load(convw[:], convw_in[:, :], 'm4', 'convw')
    for s8 in range(8):
        small_load(bfrep[:, s8 * 4:(s8 + 1) * 4], bf_in[0:1, :].partition_broadcast(P), 'm5', 'bfrep%d' % s8)
    op('gpsimd', lambda e: e.dma_start(out=wf_sb[:].rearrange("p k n -> p (k n)"), in_=wf_in[:, :]), writes=['wf_sb'], dma='m6')

    def emit_rstd(dst, src, mul, rname, wname):
        op('vector', lambda e: e.tensor_scalar(out=dst, in0=src, scalar1=mul, scalar2=EPS, op0=ALU.mult, op1=ALU.add),
           reads=rname, writes=[wname])
        op('scalar', lambda e: e.activation(out=dst, in_=dst, func=AF.Sqrt), reads=[wname], writes=[wname])
        op('vector', lambda e: e.reciprocal(out=dst, in_=dst), reads=[wname], writes=[wname])

    A = Arena(P0)
    cB = A.alloc([P, 32, P], BF16, "cB")
    ring0 = [A.alloc([P, 16, 512], BF16, "ring0_%d" % i) for i in range(3)]
    bt = [A.alloc([P, 512], F32, "bt%d" % i) for i in range(2)]
    postr = [A.alloc([P, 512], F32, "postr%d" % i) for i in range(2)]
    modblk = [A.alloc([P, 512], F32, "modblk%d" % i) for i in range(2)]
    gblk = [A.alloc([P, 512], F32, "gblk%d" % i) for i in range(2)]

    op('scalar', lambda e: e.activation(out=cact[:], in_=ccol[:], func=AF.Silu), reads=['ccol'], writes=['cact'])
    for k in range(32):
        op('vector', lambda e, k=k: e.tensor_scalar(out=cB[:, k, :], in0=ones_bf[:], scalar1=cact[:, k:k + 1],
                                                    scalar2=None, op0=ALU.mult),
           reads=['cact', 'ones_bf'], writes=['cB%d' % k])
    cBn = ['cB%d' % k for k in range(32)]

    rcount = [0]

    def ring_load(rings, view_fn, src_ap, nslots):
        i = rcount[0] % nslots
        rcount[0] += 1
        rn = 'ring%d' % i
        dst = view_fn(rings[i])
        op('gpsimd', lambda e: e.dma_start(out=dst, in_=src_ap), writes=[rn], dma=rn)
        return rings[i], rn

    seg_cols = {0: shift1c, 1: sc1c, 3: shift2c, 4: sc2c}
    for cb in range(48):
        seg = cb // 8
        c0 = cb * 512
        pb = ps[cb % 2]
        pbn = psn[cb % 2]
        for kg in range(2):
            rt, rn = ring_load(ring0, lambda t: t[:],
                               w_ada[kg * 2048:(kg + 1) * 2048, c0:c0 + 512].rearrange("(k p) n -> p k n", p=P), 3)

            def mm(e, rt=rt, kg=kg, pb=pb):
                ins = None
                for k in range(16):
                    ins = e.matmul(pb, lhsT=cB[:, kg * 16 + k, :], rhs=rt[:, k, :],
                                   start=(kg == 0 and k == 0), stop=(kg == 1 and k == 15))
                return ins
            op('tensor', mm, reads=[rn] + cBn, writes=[pbn])
        b_t = bt[cb % 2]
        m_t = modblk[cb % 2]
        op('sync', lambda e, b_t=b_t, c0=c0: e.dma_start(out=b_t[:], in_=b_ada[0:1, c0:c0 + 512].partition_broadcast(P)),
           writes=[b_t.name], dma=b_t.name)
        op('vector', lambda e, m_t=m_t, pb=pb, b_t=b_t: e.tensor_tensor(out=m_t[:], in0=pb, in1=b_t[:], op=ALU.add),
           reads=[pbn, b_t.name], writes=[m_t.name])
        if seg in seg_cols:
            colt = seg_cols[seg]
            for i in range(4):
                cidx = (cb % 8) * 4 + i
                op('vector', lambda e, m_t=m_t, i=i, colt=colt, cidx=cidx: e.scalar_tensor_tensor(
                    out=junk_f[:], in0=m_t[:, i * P:(i + 1) * P], scalar=1.0, in1=ident_f[:],
                    op0=ALU.mult, op1=ALU.mult, accum_out=colt[:, cidx:cidx + 1]),
                   reads=[m_t.name], writes=['junk_f', colt.name + str(cidx)])
        else:
            which = 0 if seg == 2 else 1
            prow = post1_in if which == 0 else post2_in
            p_t = postr[cb % 2]
            g_t = gblk[cb % 2]
            cc0 = (cb % 8) * 512
            op('sync', lambda e, p_t=p_t, prow=prow, cc0=cc0: e.dma_start(
                out=p_t[:], in_=prow[0:1, cc0:cc0 + 512].partition_broadcast(P)), writes=[p_t.name], dma=p_t.name)
            op('vector', lambda e, g_t=g_t, m_t=m_t, p_t=p_t: e.tensor_tensor(out=g_t[:], in0=m_t[:], in1=p_t[:], op=ALU.mult),
               reads=[m_t.name, p_t.name], writes=[g_t.name])
            op('sync', lambda e, g_t=g_t, which=which, cc0=cc0: e.dma_start(out=Gs[which:which + 1, cc0:cc0 + 512], in_=g_t[0:1, :]),
               reads=[g_t.name], writes=['Gs%d_%d' % (which, cb % 8)], dma='gst%d' % (cb % 2))
    allc = lambda t: [t.name + str(i) for i in range(32)]
    op('vector', lambda e: e.scalar_tensor_tensor(out=gain1c[:], in0=sc1c[:], scalar=1.0, in1=pre1c[:], op0=ALU.add, op1=ALU.mult),
       reads=allc(sc1c) + ['pre1c'], writes=['gain1c'])
    op('vector', lambda e: e.scalar_tensor_tensor(out=gain2c[:], in0=sc2c[:], scalar=1.0, in1=pre2c[:], op0=ALU.add, op1=ALU.mult),
       reads=allc(sc2c) + ['pre2c'], writes=['gain2c'])
    shift1n = allc(shift1c)
    shift2n = allc(shift2c)
    GsN = [['Gs%d_%d' % (w, i) for i in range(8)] for w in range(2)]

    def dbg_dump(items, reads):
        dt_ = nc.alloc_sbuf_tensor_at("dbgt", [P, 8192], F32, offset=LIMIT - 8192 * 4 - 64)
        op('vector', lambda e: e.memset(dt_[:], 0.0), writes=['dbgt'])
        for (off, n, src) in items:
            op('vector', lambda e, off=off, n=n, src=src: e.tensor_copy(out=dt_[:, off:off + n], in_=src),
               reads=list(reads) + ['dbgt'], writes=['dbgt'])
        op('sync', lambda e: e.dma_start(out=dbg[:, :], in_=dt_[:]), reads=['dbgt'], dma='dbgs')

    def finish():
        T.barrier()
        with nc.Block() as block:
            T.emit(block)
        return nc

    if stop_after == 0:
        if debug:
            dbg_dump([(0, 32, gain1c[:]), (32, 32, shift1c[:]), (64, 32, gain2c[:]), (96, 32, shift2c[:])],
                     ['gain1c', 'gain2c'] + shift1n + shift2n)
        return finish()

    T.barrier()
    A = Arena(P0)
    hT = A.alloc([P, 32, TT1], BF16, "hT")
    ring1 = [A.alloc([P, 32, 256], BF16, "ring1_%d" % i) for i in range(3)]
    xs = [A.alloc([P, D], F32, "xs%d" % i) for i in range(2)]
    xn = [A.alloc([P, D], BF16, "xn%d" % i) for i in range(2)]
    qst = [A.alloc([P, TT1], BF16, "qst%d" % i) for i in range(2)]
    vst = A.alloc([P, 8, 256], BF16, "vst")
    gb_sb = A.alloc([P, 2, TT1], F32, "gb_sb")
    gc_sb = A.alloc([P, 2, TT1], F32, "gc_sb")
    vbuf = A.alloc([P, 2, TT1 + 2], F32, "vbuf")
    ysb = [A.alloc([P, TT1], BF16, "ysb%d" % i) for i in range(2)]
    fsm = A0.alloc([P, 32], F32, "fsm")

    hTn = ['hT%d' % k for k in range(32)]
    ybn = []
    tbf = pp[0][:].bitcast(BF16)
    QSCALE = 1.0 / float(np.sqrt(128.0))
    blocks = [('q', 0, 0), ('q', 256, 1), ('k', 512, 0), ('k', 768, 1),
              ('gb', 1024, 0), ('gc', 1536, 0), ('u', 2048, 0),
              ('gb', 1280, 1), ('gc', 1792, 1), ('u', 2304, 1),
              ('v', 2560, 0), ('v', 2816, 1)]
    cnt = {'xs': 0, 'st': 0, 'pair': 0, 'bank': 0, 'qst': 0, 'ysb': 0, 'ev': 0}
    NT1_RUN = NT1 if stop_after != 1 or not debug else 2
    if int(os.environ.get('K_CUT', '99')) == 0:
        NT1_RUN = 0

    for ti in range(NT1_RUN):
        for s8 in range(8):
            i = cnt['xs'] % 2
            cnt['xs'] += 1
            xt, xnt = xs[i], xn[i]
            r0 = ti * TT1 + s8 * P
            op('sync', lambda e, xt=xt, r0=r0: e.dma_start(out=xt[:], in_=x_full[r0:r0 + P, :]), writes=[xt.name], dma=xt.name)
            sc = cnt['st'] % 32
            cnt['st'] += 1
            stc = stat[:, sc:sc + 1]
            stn = 'stat%d' % sc
            op('scalar', lambda e, xt=xt, xnt=xnt, stc=stc: e.activation(out=xnt[:], in_=xt[:], func=AF.Square, accum_out=stc),
               reads=[xt.name], writes=[xnt.name, stn])
            emit_rstd(stc, stc, 1.0 / D, [stn], stn)
            op('vector', lambda e, xt=xt, xnt=xnt, stc=stc: e.tensor_scalar(out=xnt[:], in0=xt[:], scalar1=stc, scalar2=None, op0=ALU.mult),
               reads=[xt.name, stn], writes=[xnt.name])
            for g4 in range(4):
                half = g4 % 2
                tv = tbf[:, half * 1024:(half + 1) * 1024]
                bn = psn[half]

                def tp(e, xnt=xnt, g4=g4, tv=tv):
                    ins = None
                    for kk in range(8):
                        k = g4 * 8 + kk
                        ins = e.transpose(out=tv[:, kk * P:(kk + 1) * P], in_=xnt[:, k * P:(k + 1) * P], identity=ident_bf[:])
                    return ins
                op('tensor', tp, reads=[xnt.name], writes=[bn])

                def ev_act(e, g4=g4, tv=tv, s8=s8):
                    ins = None
                    for kk in range(8):
                        k = g4 * 8 + kk
                        ins = e.activation(out=hT[:, k, s8 * P:(s8 + 1) * P], in_=tv[:, kk * P:(kk + 1) * P], func=AF.Identity,
                                           scale=gain1c[:, k:k + 1], bias=shift1c[:, k:k + 1])
                    return ins

                def ev_dve(e, g4=g4, tv=tv, s8=s8):
                    ins = None
                    for kk in range(8):
                        k = g4 * 8 + kk
                        ins = e.tensor_scalar(out=hT[:, k, s8 * P:(s8 + 1) * P], in0=tv[:, kk * P:(kk + 1) * P],
                                              scalar1=gain1c[:, k:k + 1], scalar2=shift1c[:, k:k + 1], op0=ALU.mult, op1=ALU.add)
                    return ins
                if g4 % 2 == 0:
                    op('scalar', ev_act, reads=[bn, 'gain1c'] + shift1n, writes=[hTn[g4 * 8 + kk] for kk in range(8)])
                else:
                    op('vector', ev_dve, reads=[bn, 'gain1c'] + shift1n, writes=[hTn[g4 * 8 + kk] for kk in range(8)])

        KCUT = int(os.environ.get("K_CUT", "99"))
        if KCUT <= 1:
            continue
        def mmf(e):
            ins = None
            for s8 in range(8):
                for k in range(32):
                    ins = e.matmul(ps[7][:, s8 * 4:(s8 + 1) * 4], lhsT=hT[:, k, s8 * P:(s8 + 1) * P], rhs=wf_sb[:, k, :],
                                   start=(k == 0), stop=(k == 31))
            return ins
        op('tensor', mmf, reads=hTn + ['wf_sb'], writes=[psn[7]])
        op('vector', lambda e: e.tensor_tensor(out=fsm[:], in0=ps[7][:, 0:32], in1=bfrep[:], op=ALU.add),
           reads=[psn[7]] + ['bfrep%d' % i for i in range(8)], writes=['fsm'])
        op('scalar', lambda e: e.activation(out=fsm[:], in_=fsm[:], func=AF.Exp, scale=-1.0), reads=['fsm'], writes=['fsm'])
        op('scalar', lambda e, ti=ti: e.activation(out=SP[:, ti * 32:(ti + 1) * 32], in_=fsm[:], func=AF.Ln, bias=1.0),
           reads=['fsm'], writes=['SP%d' % ti])

        for bidx, (kind, col0, idx) in enumerate(blocks):
            if bidx >= KCUT - 2:
                break
            rt, rn = ring_load(ring1, lambda t: t[:], w1[:, col0:col0 + 256].rearrange("(k p) n -> p k n", p=P), 3)
            if kind == 'v':
                for s8 in range(8):
                    bi = 2 + cnt['bank'] % 6
                    cnt['bank'] += 1

                    def mmv(e, rt=rt, s8=s8, bi=bi):
                        ins = None
                        for k in range(32):
                            ins = e.matmul(ps[bi][:, 0:256], lhsT=hT[:, k, s8 * P:(s8 + 1) * P], rhs=rt[:, k, :],
                                           start=(k == 0), stop=(k == 31))
                        return ins
                    op('tensor', mmv, reads=hTn + [rn], writes=[psn[bi]])
                    eng = 'scalar' if s8 % 2 == 0 else 'vector'
                    if eng == 'scalar':
                        op('scalar', lambda e, s8=s8, bi=bi: e.activation(out=vst[:, s8, :], in_=ps[bi][:, 0:256], func=AF.Copy),
                           reads=[psn[bi]], writes=['vst%d' % s8])
                    else:
                        op('vector', lambda e, s8=s8, bi=bi: e.tensor_copy(out=vst[:, s8, :], in_=ps[bi][:, 0:256]),
                           reads=[psn[bi]], writes=['vst%d' % s8])
                op('sync', lambda e, ti=ti, idx=idx: e.dma_start(
                    out=Vs[ti * TT1:(ti + 1) * TT1, idx * 256:(idx + 1) * 256].rearrange("(s p) c -> p s c", p=P), in_=vst[:]),
                   reads=['vst%d' % i for i in range(8)], writes=['Vs_%d_%d' % (ti, idx)], dma='vst')
                continue
            for cch in range(2):
                pi = 1 + cnt['pair'] % 3
                cnt['pair'] += 1
                pair = pp[pi]
                pn = [psn[2 * pi], psn[2 * pi + 1]]

                def mmq(e, rt=rt, cch=cch, pair=pair):
                    ins = None
                    for k in range(32):
                        for half in range(2):
                            ins = e.matmul(pair[:, half * 512:(half + 1) * 512], lhsT=rt[:, k, cch * P:(cch + 1) * P],
                                           rhs=hT[:, k, half * 512:(half + 1) * 512], start=(k == 0), stop=(k == 31))
                    return ins
                op('tensor', mmq, reads=hTn + [rn], writes=pn)
                if kind in ('q', 'k'):
                    hl = idx * 2 + cch
                    qi = cnt['qst'] % 2
                    cnt['qst'] += 1
                    qt = qst[qi]
                    dst = qTs if kind == 'q' else kTs
                    if kind == 'q':
                        op('scalar', lambda e, qt=qt, pair=pair: e.activation(out=qt[:], in_=pair[:], func=AF.Copy, scale=QSCALE),
                           reads=pn, writes=[qt.name])
                    else:
                        op('vector', lambda e, qt=qt, pair=pair: e.tensor_copy(out=qt[:], in_=pair[:]), reads=pn, writes=[qt.name])
                    op('sync', lambda e, qt=qt, dst=dst, hl=hl, ti=ti: e.dma_start(
                        out=dst[hl * P:(hl + 1) * P, ti * TT1:(ti + 1) * TT1], in_=qt[:]),
                       reads=[qt.name], writes=['%sTs_%d_%d' % (kind, hl, ti)], dma=qt.name)
                elif kind == 'gb':
                    op('scalar', lambda e, cch=cch, pair=pair: e.activation(out=gb_sb[:, cch, :], in_=pair[:], func=AF.Copy),
                       reads=pn, writes=['gb%d' % cch])
                elif kind == 'gc':
                    op('vector', lambda e, cch=cch, pair=pair: e.tensor_copy(out=gc_sb[:, cch, :], in_=pair[:]),
                       reads=pn, writes=['gc%d' % cch])
                else:
                    cc = idx * 2 + cch
                    vn = 'vb%d' % cch
                    hn = 'halo%d' % cc
                    gcn = 'gc%d' % cch
                    yi = cnt['ysb'] % 2
                    cnt['ysb'] += 1
                    yt = ysb[yi]
                    op('vector', lambda e, cch=cch, cc=cc: e.tensor_copy(out=vbuf[:, cch, 0:2], in_=halo[:, cc, :]),
                       reads=['halo', hn], writes=[vn])
                    op('vector', lambda e, cch=cch, pair=pair: e.tensor_tensor(out=vbuf[:, cch, 2:2 + TT1], in0=pair[:], in1=gc_sb[:, cch, :], op=ALU.mult),
                       reads=pn + [gcn, vn], writes=[vn])
                    op('vector', lambda e, cch=cch, cc=cc: e.tensor_copy(out=halo[:, cc, :], in_=vbuf[:, cch, TT1:TT1 + 2]),
                       reads=[vn], writes=[hn])
                    op('vector', lambda e, cch=cch, cc=cc: e.tensor_scalar(out=gc_sb[:, cch, :], in0=vbuf[:, cch, 2:2 + TT1],
                                                                        scalar1=convw[:, cc * 3 + 2:cc * 3 + 3], scalar2=None, op0=ALU.mult),
                       reads=[vn, 'convw'], writes=[gcn])
                    op('vector', lambda e, cch=cch, cc=cc: e.scalar_tensor_tensor(out=gc_sb[:, cch, :], in0=vbuf[:, cch, 1:1 + TT1],
                                                                               scalar=convw[:, cc * 3 + 1:cc * 3 + 2], in1=gc_sb[:, cch, :],
                                                                               op0=ALU.mult, op1=ALU.add),
                       reads=[vn, gcn], writes=[gcn])
                    op('vector', lambda e, cch=cch, cc=cc: e.scalar_tensor_tensor(out=gc_sb[:, cch, :], in0=vbuf[:, cch, 0:TT1],
                                                                               scalar=convw[:, cc * 3:cc * 3 + 1], in1=gc_sb[:, cch, :],
                                                                               op0=ALU.mult, op1=ALU.add),
                       reads=[vn, gcn], writes=[gcn])
                    op('vector', lambda e, cch=cch, yt=yt: e.tensor_tensor(out=yt[:], in0=gc_sb[:, cch, :], in1=gb_sb[:, cch, :], op=ALU.mult),
                       reads=[gcn, 'gb%d' % cch], writes=[yt.name])
                    for hf in range(2):
                        rrow = ((ti * 2 + hf) * 8 + 4 + cc) * P
                        yn_ = 'yb_c_%d_%d_%d' % (ti, cc, hf)
                        ybn.append(yn_)
                        op('sync', lambda e, yt=yt, rrow=rrow, hf=hf: e.dma_start(out=ybuf[rrow:rrow + P, :], in_=yt[:, hf * 512:(hf + 1) * 512]),
                           reads=[yt.name], writes=[yn_], dma=yt.name + '_%d' % hf)

    if stop_after == 1:
        if debug:
            T.barrier()
            dt_ = nc.alloc_sbuf_tensor_at("dbgt", [P, 8192], F32, offset=LIMIT - 8192 * 4 - 64)
            op('vector', lambda e: e.memset(dt_[:], 0.0), writes=['dbgt'])
            op('vector', lambda e: e.tensor_copy(out=dt_[:, 0:256], in_=SP[:]), reads=['dbgt'], writes=['dbgt'])
            op('vector', lambda e: e.tensor_copy(out=dt_[:, 256:384], in_=hT[:, 0, 896:1024]), reads=['dbgt'], writes=['dbgt'])
            op('vector', lambda e: e.tensor_copy(out=dt_[:, 384:512], in_=hT[:, 31, 0:128]), reads=['dbgt'], writes=['dbgt'])
            qd = nc.alloc_sbuf_tensor_at("qd", [P, 4, 1024], BF16, offset=LIMIT - 8192 * 4 - 64 - 8192)
            op('sync', lambda e: e.dma_start(out=qd[:, 0, :], in_=qTs[0:P, 0:1024]), writes=['qd'], dma='dq0')
            op('sync', lambda e: e.dma_start(out=qd[:, 1, :], in_=kTs[P:2 * P, 1024:2048]), writes=['qd1'], dma='dq1')
            op('sync', lambda e: e.dma_start(out=qd[:, 2, 0:512], in_=Vs[1024:1024 + P, 0:512]), writes=['qd2'], dma='dq2')
            op('sync', lambda e: e.dma_start(out=qd[:, 3, :], in_=ybuf[5 * P:6 * P, 0:1024]), writes=['qd3'], dma='dq3')
            op('vector', lambda e: e.tensor_copy(out=dt_[:, 1024:2048], in_=qd[:, 0, :]), reads=['qd', 'dbgt'], writes=['dbgt'])
            op('vector', lambda e: e.tensor_copy(out=dt_[:, 2048:3072], in_=qd[:, 1, :]), reads=['qd1', 'dbgt'], writes=['dbgt'])
            op('vector', lambda e: e.tensor_copy(out=dt_[:, 3072:3584], in_=qd[:, 2, 0:512]), reads=['qd2', 'dbgt'], writes=['dbgt'])
            op('vector', lambda e: e.tensor_copy(out=dt_[:, 4096:5120], in_=qd[:, 3, :]), reads=['qd3', 'dbgt'], writes=['dbgt'])
            op('sync', lambda e: e.dma_start(out=dbg[:, :], in_=dt_[:]), reads=['dbgt'], dma='dbgs')
        return finish()

    T.barrier()
    A = Arena(P0)
    qT = [A.alloc([P, S], BF16, "qT%d" % i) for i in range(2)]
    kT = [A.alloc([P, S], BF16, "kT%d" % i) for i in range(2)]
    Vt = [A.alloc([P, 64, P], BF16, "Vt%d" % i) for i in range(2)]
    Bm = [A.alloc([P, 64, 64], F32, "Bm%d" % i) for i in range(2)]
    PT = [A.alloc([P, 512], BF16, "PT%d" % i) for i in range(3)]
    rl = [A.alloc([P, 512], F32, "rl%d" % i) for i in range(2)]
    ost = [A.alloc([P, 512], BF16, "ost%d" % i) for i in range(2)]
    CumH = A.alloc([P, 4, 64], F32, "CumH")
    TotH = A.alloc([P, 4, 64], F32, "TotH")
    EndH = A.alloc([P, 4, 64], F32, "EndH")
    CumF = A.alloc([P, 4, 64], F32, "CumF")
    SPn = ['SP%d' % i for i in range(NT1)]

    op('tensor', lambda e: e.matmul(ps[6][:, 0:256], lhsT=tri_f[:], rhs=SP[:], start=True, stop=True), reads=SPn + ['tri_f'], writes=[psn[6]])
    op('tensor', lambda e: e.matmul(ps[7][:, 0:256], lhsT=ones_f[:], rhs=SP[:], start=True, stop=True), reads=SPn + ['ones_f'], writes=[psn[7]])
    op('vector', lambda e: e.tensor_copy(out=CumH[:], in_=ps[6][:, 0:256].rearrange("p (kb h) -> p h kb", h=4)), reads=[psn[6]], writes=['CumH'])
    op('vector', lambda e: e.tensor_copy(out=TotH[:], in_=ps[7][:, 0:256].rearrange("p (kb h) -> p h kb", h=4)), reads=[psn[7]], writes=['TotH'])
    for h in range(4):
        op('vector', lambda e, h=h: e.tensor_tensor_scan(out=EndH[:, h, :], data0=ones_f[:, 0:64], data1=TotH[:, h, :], initial=0.0,
                                                         op0=ALU.mult, op1=ALU.add), reads=['TotH', 'ones_f'], writes=['EndH%d' % h])
    EndN = ['EndH%d' % h for h in range(4)]
    op('vector', lambda e: e.tensor_tensor(out=CumF[:], in0=CumH[:], in1=EndH[:], op=ALU.add), reads=['CumH'] + EndN, writes=['CumF'])
    op('vector', lambda e: e.tensor_tensor(out=CumF[:], in0=CumF[:], in1=TotH[:], op=ALU.subtract), reads=['CumF', 'TotH'], writes=['CumF'])

    cntb = {'s': 0, 'pt': 0, 'o': 0, 'ost': 0}
    NH_RUN = 4 if not (debug and stop_after == 2) else 1
    for hl in range(NH_RUN):
        sl = hl % 2
        q_t, k_t, v_t, b_t = qT[sl], kT[sl], Vt[sl], Bm[sl]
        op('sync', lambda e, q_t=q_t, hl=hl: e.dma_start(out=q_t[:], in_=qTs[hl * P:(hl + 1) * P, :]),
           reads=['qTs_%d_%d' % (hl, t) for t in range(NT1)], writes=[q_t.name], dma=q_t.name)
        op('sync', lambda e, k_t=k_t, hl=hl: e.dma_start(out=k_t[:], in_=kTs[hl * P:(hl + 1) * P, :]),
           reads=['kTs_%d_%d' % (hl, t) for t in range(NT1)], writes=[k_t.name], dma=k_t.name)
        op('sync', lambda e, v_t=v_t, hl=hl: e.dma_start(out=v_t[:], in_=Vs[:, hl * P:(hl + 1) * P].rearrange("(kb p) d -> p kb d", p=P)),
           reads=['Vs_%d_%d' % (t, hl // 2) for t in range(NT1)], writes=[v_t.name], dma=v_t.name)

        def mkB(e, b_t=b_t, hl=hl):
            ins = None
            for kb in range(64):
                ins = e.tensor_scalar(out=b_t[:, kb, :], in0=EndH[:, hl, :], scalar1=-1.0, scalar2=CumF[:, hl, kb:kb + 1],
                                      op0=ALU.mult, op1=ALU.add)
            return ins
        op('vector', mkB, reads=EndN + ['CumF'], writes=[b_t.name])

        for qg in range(16):
            nkb = 4 * qg + 4
            oi = cntb['o'] % 2
            cntb['o'] += 1
            bO, bL = ps[2 + oi], ps[4 + oi]
            bOn, bLn = psn[2 + oi], psn[4 + oi]
            for kb in range(nkb):
                j0 = max(0, kb - 4 * qg)
                c0 = j0 * P
                si = cntb['s'] % 2
                cntb['s'] += 1
                bS, bSn = ps[si], psn[si]
                pi = cntb['pt'] % 3
                cntb['pt'] += 1
                p_t = PT[pi]
                op('tensor', lambda e, bS=bS, c0=c0, k_t=k_t, q_t=q_t, kb=kb, qg=qg: e.matmul(
                    bS[:, c0:512], lhsT=k_t[:, kb * P:(kb + 1) * P], rhs=q_t[:, qg * 512 + c0:(qg + 1) * 512], start=True, stop=True),
                   reads=[k_t.name, q_t.name], writes=[bSn])

                def ex(e, bS=bS, p_t=p_t, b_t=b_t, j0=j0, kb=kb, qg=qg):
                    ins = None
                    for j in range(j0, 4):
                        ins = e.activation(out=p_t[:, j * P:(j + 1) * P], in_=bS[:, j * P:(j + 1) * P], func=AF.Exp,
                                           bias=b_t[:, kb, 4 * qg + j:4 * qg + j + 1], scale=1.0)
                    return ins
                op('scalar', ex, reads=[bSn, b_t.name], writes=[p_t.name])
                if kb >= 4 * qg:
                    op('vector', lambda e, p_t=p_t, c0=c0: e.tensor_tensor(out=p_t[:, c0:c0 + P], in0=p_t[:, c0:c0 + P], in1=tri_bf[:], op=ALU.mult),
                       reads=[p_t.name, 'tri_bf'], writes=[p_t.name])

                def pv(e, bO=bO, bL=bL, p_t=p_t, v_t=v_t, kb=kb, c0=c0, nkb=nkb):
                    e.matmul(bO[:, c0:512], lhsT=v_t[:, kb, :], rhs=p_t[:, c0:512], start=(kb == 0), stop=(kb == nkb - 1))
                    return e.matmul(bL[:, c0:512], lhsT=ones_bf[:], rhs=p_t[:, c0:512], start=(kb == 0), stop=(kb == nkb - 1))
                op('tensor', pv, reads=[p_t.name, v_t.name, 'ones_bf'], writes=[bOn, bLn])
            oi2 = cntb['ost'] % 2
            cntb['ost'] += 1
            r_t, o_t = rl[oi2], ost[oi2]
            op('vector', lambda e, r_t=r_t, bL=bL: e.reciprocal(out=r_t[:], in_=bL), reads=[bLn], writes=[r_t.name])
            op('vector', lambda e, r_t=r_t, o_t=o_t, bO=bO: e.tensor_tensor(out=o_t[:], in0=bO, in1=r_t[:], op=ALU.mult),
               reads=[bOn, r_t.name], writes=[o_t.name])
            rrow = (qg * 8 + hl) * P
            yn_ = 'yb_a_%d_%d' % (hl, qg)
            ybn.append(yn_)
            op('sync', lambda e, o_t=o_t, rrow=rrow: e.dma_start(out=ybuf[rrow:rrow + P, :], in_=o_t[:]),
               reads=[o_t.name], writes=[yn_], dma=o_t.name)

    if stop_after == 2:
        if debug:
            dt_ = nc.alloc_sbuf_tensor_at("dbgt", [P, 8192], F32, offset=LIMIT - 8192 * 4 - 64)
            qd = nc.alloc_sbuf_tensor_at("qd", [P, 2, 2048], BF16, offset=LIMIT - 8192 * 4 - 64 - 8192)
            T.barrier()
            op('vector', lambda e: e.memset(dt_[:], 0.0), writes=['dbgt'])
            op('sync', lambda e: e.dma_start(out=qd[:, 0, :], in_=ybuf[0:P, :]), writes=['qd'], dma='dq0')
            op('sync', lambda e: e.dma_start(out=qd[:, 1, :], in_=ybuf[24 * P:25 * P, :]), writes=['qd1'], dma='dq1')
            op('vector', lambda e: e.tensor_copy(out=dt_[:, 0:2048], in_=qd[:, 0, :]), reads=['qd', 'dbgt'], writes=['dbgt'])
            op('vector', lambda e: e.tensor_copy(out=dt_[:, 2048:4096], in_=qd[:, 1, :]), reads=['qd1', 'dbgt'], writes=['dbgt'])
            op('vector', lambda e: e.tensor_copy(out=dt_[:, 4096:4352], in_=CumF[:].rearrange("p h kb -> p (h kb)")), reads=['dbgt'], writes=['dbgt'])
            op('sync', lambda e: e.dma_start(out=dbg[:, :], in_=dt_[:]), reads=['dbgt'], dma='dbgs')
        return finish()

    T.barrier()
    pst = {}

    def gp_j(e):
        if 'j' not in pst:
            pst['j'] = e.partition_id() % 4
        return pst['j']

    for tt in range(NT2):
        for jp in range(4):
            r0 = (jp * 4 + tt) * 1024
            T.coll(lambda e, r0=r0, jp=jp: e.collective_compute(
                "AllGather", ALU.bypass, replica_groups=[[0, 1, 2, 3], [4, 5, 6, 7]],
                ins=[ybuf[r0:r0 + 1024, :]], outs=[yslots[jp * 4096:(jp + 1) * 4096, :]]),
                reads=ybn if (tt == 0 and jp == 0) else [], writes=['ysl%d' % jp, 'ccchain'])

        def cp(e, tt=tt):
            j = gp_j(e)
            return e.dma_start(out=ymine[tt * 4096:(tt + 1) * 4096, :], in_=yslots[bass.ds(j * 4096, 4096), :])
        op('gpsimd', cp, reads=['ysl%d' % i for i in range(4)], writes=['ym%d' % tt], dma='ymc')

    A = Arena(P0)
    aT = A.alloc([P, NFC, TT2], BF16, "aT")
    zs = [nc.alloc_sbuf_tensor_at("zs%d" % i, [P, D], F32, offset=P0 + i * D * 4) for i in range(4)]
    actT = A.alloc([P, 32, TT2], BF16, "actT")
    ring2 = [A.alloc([P, 8192], BF16, "ring2_%d" % i) for i in range(2)]
    v16 = lambda t: t[:].rearrange("p (k n) -> p k n", n=512)
    v32 = lambda t: t[:].rearrange("p (k n) -> p k n", n=256)
    xs2 = A.alloc([P, D], F32, "xs2")
    Grow = A.alloc([P, D], F32, "Grow")
    xn2 = A.alloc([P, D], BF16, "xn2")
    scr = [A.alloc([P, 512], F32, "scr%d" % i) for i in range(2)]
    sq = A.alloc([P, 512], BF16, "sq")
    sg = [A.alloc([P, 512], F32, "sg%d" % i) for i in range(2)]
    ssq = A0.alloc([P, 32], F32, "ssq")
    actn = ['act%d' % k for k in range(32)]
    aTn = ['aT%d' % k for k in range(NFC)]
    c2 = {'ring': 0, 'sg': 0, 'scr': 0}

    def ring2_load(view, src_ap):
        i = c2['ring'] % 2
        c2['ring'] += 1
        rn = 'ring%d' % i
        dst = view(ring2[i])
        op('gpsimd', lambda e: e.dma_start(out=dst, in_=src_ap), writes=[rn], dma=rn)
        return ring2[i], rn

    NT2_RUN = NT2 if not (debug and stop_after == 3) else 1
    for tt in range(NT2_RUN):
        t0 = tt * TT2
        op('sync', lambda e, tt=tt: e.dma_start(out=actT[:], in_=ymine[tt * 4096:(tt + 1) * 4096, :].rearrange("(k p) t -> p k t", p=P)),
           reads=['ym%d' % tt], writes=actn, dma='yl')
        op('sync', lambda e: e.dma_start(out=Grow[:], in_=Gs[0:1, :].partition_broadcast(P)), reads=GsN[0], writes=['Grow'], dma='grow')
        for grp in range(2):
            bnk, bnkn = ps[grp], psn[grp]
            kks = [r * 8 + lb for r in range(4) for lb in range(grp * 4, grp * 4 + 4)]
            for n_, kk in enumerate(kks):
                op('vector', lambda e, kk=kk: e.tensor_tensor(out=sq[:], in0=actT[:, kk, :], in1=actT[:, kk, :], op=ALU.mult),
                   reads=[actn[kk]], writes=['sq'])
                op('tensor', lambda e, bnk=bnk, n_=n_: e.matmul(bnk, lhsT=ones_bf[:], rhs=sq[:], start=(n_ == 0), stop=(n_ == 15)),
                   reads=['sq', 'ones_bf'], writes=[bnkn])
            emit_rstd(scr[grp][:], bnk, 1.0 / 2048, [bnkn], scr[grp].name)
        for kk in range(32):
            grp = 0 if (kk % 8) < 4 else 1
            op('vector', lambda e, kk=kk, grp=grp: e.scalar_tensor_tensor(out=actT[:, kk, :], in0=actT[:, kk, :], scalar=gyc[:, kk:kk + 1],
                                                                         in1=scr[grp][:], op0=ALU.mult, op1=ALU.mult),
               reads=[actn[kk], scr[grp].name, 'gyc'], writes=[actn[kk]])
        for db in range(8):
            bset = (db % 2) * 4
            for kg in range(2):
                rt, rn = ring2_load(v16, w_out[kg * 2048:(kg + 1) * 2048, db * 512:(db + 1) * 512].rearrange("(k p) n -> p k n", p=P))

                def mmo(e, rt=v16(rt), kg=kg, bset=bset):
                    ins = None
                    for s4 in range(4):
                        for k in range(16):
                            ins = e.matmul(ps[bset + s4], lhsT=actT[:, kg * 16 + k, s4 * P:(s4 + 1) * P], rhs=rt[:, k, :],
                                           start=(kg == 0 and k == 0), stop=(kg == 1 and k == 15))
                    return ins
                op('tensor', mmo, reads=actn + [rn], writes=[psn[bset + s4] for s4 in range(4)])
            for s4 in range(4):
                bk, bkn = ps[bset + s4], psn[bset + s4]
                op('scalar', lambda e, bk=bk, s4=s4, db=db: e.activation(out=junk_bf[:], in_=bk, func=AF.Square, accum_out=ssq[:, s4 * 8 + db:s4 * 8 + db + 1]),
                   reads=[bkn], writes=['junk_bf', 'ssq%d_%d' % (s4, db)])
                op('vector', lambda e, bk=bk, s4=s4, db=db: e.tensor_copy(out=zs[s4][:, db * 512:(db + 1) * 512], in_=bk),
                   reads=[bkn], writes=['zs%d_%d' % (s4, db)])
        for s4 in range(4):
            r0 = t0 + s4 * P
            op('sync', lambda e, r0=r0: e.dma_start(out=xs2[:], in_=x_chunk[r0:r0 + P, :]), writes=['xs2'], dma='xs2')
            ssn = ['ssq%d_%d' % (s4, db) for db in range(8)]
            st1 = stat[:, s4:s4 + 1]
            op('vector', lambda e, s4=s4, st1=st1: e.tensor_reduce(out=st1, in_=ssq[:, s4 * 8:(s4 + 1) * 8], axis=mybir.AxisListType.X, op=ALU.add),
               reads=ssn, writes=['st1_%d' % s4])
            emit_rstd(st1, st1, 1.0 / D, ['st1_%d' % s4], 'st1_%d' % s4)
            zn = ['zs%d_%d' % (s4, db) for db in range(8)]
            op('vector', lambda e, s4=s4, st1=st1: e.scalar_tensor_tensor(out=zs[s4][:], in0=zs[s4][:], scalar=st1, in1=Grow[:], op0=ALU.mult, op1=ALU.mult),
               reads=zn + ['st1_%d' % s4, 'Grow'], writes=['zs%d' % s4])
            op('vector', lambda e, s4=s4: e.tensor_tensor(out=xs2[:], in0=xs2[:], in1=zs[s4][:], op=ALU.add),
               reads=['xs2', 'zs%d' % s4], writes=['xs2'])
            op('sync', lambda e, r0=r0: e.dma_start(out=x1s[r0:r0 + P, :], in_=xs2[:]), reads=['xs2'], writes=['x1s_%d' % (tt * 4 + s4)], dma='x1st')
            st2 = stat[:, 8 + s4:9 + s4]
            op('scalar', lambda e, st2=st2: e.activation(out=xn2[:], in_=xs2[:], func=AF.Square, accum_out=st2),
               reads=['xs2'], writes=['xn2', 'st2_%d' % s4])
            emit_rstd(st2, st2, 1.0 / D, ['st2_%d' % s4], 'st2_%d' % s4)
            op('vector', lambda e, st2=st2: e.tensor_scalar(out=xn2[:], in0=xs2[:], scalar1=st2, scalar2=None, op0=ALU.mult),
               reads=['xs2', 'st2_%d' % s4], writes=['xn2'])
            for g4 in range(4):
                half = g4 % 2
                tv = tbf[:, half * 1024:(half + 1) * 1024]
                bn = psn[half]

                def tp2(e, g4=g4, tv=tv):
                    ins = None
                    for kk in range(8):
                        k = g4 * 8 + kk
                        ins = e.transpose(out=tv[:, kk * P:(kk + 1) * P], in_=xn2[:, k * P:(k + 1) * P], identity=ident_bf[:])
                    return ins
                op('tensor', tp2, reads=['xn2'], writes=[bn])

                def ev2a(e, g4=g4, tv=tv, s4=s4):
                    ins = None
                    for kk in range(8):
                        k = g4 * 8 + kk
                        ins = e.activation(out=actT[:, k, s4 * P:(s4 + 1) * P], in_=tv[:, kk * P:(kk + 1) * P], func=AF.Identity,
                                           scale=gain2c[:, k:k + 1], bias=shift2c[:, k:k + 1])
                    return ins

                def ev2v(e, g4=g4, tv=tv, s4=s4):
                    ins = None
                    for kk in range(8):
                        k = g4 * 8 + kk
                        ins = e.tensor_scalar(out=actT[:, k, s4 * P:(s4 + 1) * P], in0=tv[:, kk * P:(kk + 1) * P],
                                              scalar1=gain2c[:, k:k + 1], scalar2=shift2c[:, k:k + 1], op0=ALU.mult, op1=ALU.add)
                    return ins
                if g4 % 2 == 0:
                    op('scalar', ev2a, reads=[bn, 'gain2c'] + shift2n, writes=[actn[g4 * 8 + kk] for kk in range(8)])
                else:
                    op('vector', ev2v, reads=[bn, 'gain2c'] + shift2n, writes=[actn[g4 * 8 + kk] for kk in range(8)])
        op('sync', lambda e: e.dma_start(out=Grow[:], in_=Gs[1:2, :].partition_broadcast(P)), reads=GsN[1], writes=['Grow'], dma='grow')
        for fb in range(NFC // 2):
            f0 = fb * 256
            rg, rgn = ring2_load(v32, w_gate[:, f0:f0 + 256].rearrange("(k p) n -> p k n", p=P))
            ru, run = ring2_load(v32, w_up[:, f0:f0 + 256].rearrange("(k p) n -> p k n", p=P))
            rgv = v32(rg)
            ruv = v32(ru)
            bset = (fb % 2) * 4

            def mmg(e, rv=rgv, bset=bset, o=0):
                ins = None
                for cch in range(2):
                    for k in range(32):
                        ins = e.matmul(ps[bset + o + cch], lhsT=rv[:, k, cch * P:(cch + 1) * P], rhs=actT[:, k, :], start=(k == 0), stop=(k == 31))
                return ins
            op('tensor', mmg, reads=actn + [rgn], writes=[psn[bset], psn[bset + 1]])
            op('tensor', lambda e, rv=ruv, bset=bset: mmg(e, rv, bset, 2), reads=actn + [run], writes=[psn[bset + 2], psn[bset + 3]])
            for cch in range(2):
                fc = fb * 2 + cch
                si = c2['sg'] % 2
                c2['sg'] += 1
                s_t = sg[si]
                bg, bu = ps[bset + cch], ps[bset + 2 + cch]
                op('scalar', lambda e, s_t=s_t, bg=bg: e.activation(out=s_t[:], in_=bg, func=AF.Silu), reads=[psn[bset + cch]], writes=[s_t.name])
                op('vector', lambda e, s_t=s_t, bu=bu, fc=fc: e.tensor_tensor(out=aT[:, fc, :], in0=bu, in1=s_t[:], op=ALU.mult),
                   reads=[psn[bset + 2 + cch], s_t.name], writes=[aTn[fc]])
        for db in range(8):
            bset = (db % 2) * 4
            ngr = (NFC + 15) // 16
            for kg in range(ngr):
                nk = min(16, NFC - kg * 16)
                rt, rn = ring2_load(lambda t, nk=nk: v16(t)[:, 0:nk, :],
                                    w_down[kg * 2048:kg * 2048 + nk * P, db * 512:(db + 1) * 512].rearrange("(k p) n -> p k n", p=P))

                def mmd(e, rt=v16(rt), kg=kg, nk=nk, bset=bset, ngr=ngr):
                    ins = None
                    for s4 in range(4):
                        for k in range(nk):
                            ins = e.matmul(ps[bset + s4], lhsT=aT[:, kg * 16 + k, s4 * P:(s4 + 1) * P], rhs=rt[:, k, :],
                                           start=(kg == 0 and k == 0), stop=(kg == ngr - 1 and k == nk - 1))
                    return ins
                op('tensor', mmd, reads=aTn[kg * 16:kg * 16 + nk] + [rn], writes=[psn[bset + s4] for s4 in range(4)])
            for s4 in range(4):
                bk, bkn = ps[bset + s4], psn[bset + s4]
                op('scalar', lambda e, bk=bk, s4=s4, db=db: e.activation(out=junk_bf[:], in_=bk, func=AF.Square, accum_out=ssq[:, s4 * 8 + db:s4 * 8 + db + 1]),
                   reads=[bkn], writes=['junk_bf', 'ssq%d_%d' % (s4, db)])
                ci = c2['scr'] % 2
                c2['scr'] += 1
                f_t = scr[ci]
                op('vector', lambda e, bk=bk, f_t=f_t, db=db: e.tensor_tensor(out=f_t[:], in0=bk, in1=Grow[:, db * 512:(db + 1) * 512], op=ALU.mult),
                   reads=[bkn, 'Grow'], writes=[f_t.name])
                r0 = t0 + s4 * P
                op('sync', lambda e, f_t=f_t, r0=r0, db=db: e.dma_start(out=accs[r0:r0 + P, db * 512:(db + 1) * 512], in_=f_t[:]),
                   reads=[f_t.name], writes=['acc_%d_%d' % (tt * 4 + s4, db)], dma=f_t.name + 'st')
        for s4 in range(4):
            r0 = t0 + s4 * P
            a_t = zs[s4 % 2]
            an = 'zs%d' % (s4 % 2)
            op('sync', lambda e, a_t=a_t, r0=r0: e.dma_start(out=a_t[:], in_=accs[r0:r0 + P, :]),
               reads=['acc_%d_%d' % (tt * 4 + s4, db) for db in range(8)], writes=[an] + ['zs%d_%d' % (s4 % 2, db) for db in range(8)], dma=an + 'ld')
            op('sync', lambda e, r0=r0: e.dma_start(out=xs2[:], in_=x1s[r0:r0 + P, :]), reads=['x1s_%d' % (tt * 4 + s4)], writes=['xs2'], dma='xs2')
            ssn = ['ssq%d_%d' % (s4, db) for db in range(8)]
            st3 = stat[:, 16 + s4:17 + s4]
            op('vector', lambda e, s4=s4, st3=st3: e.tensor_reduce(out=st3, in_=ssq[:, s4 * 8:(s4 + 1) * 8], axis=mybir.AxisListType.X, op=ALU.add),
               reads=ssn, writes=['st3_%d' % s4])
            emit_rstd(st3, st3, 1.0 / D, ['st3_%d' % s4], 'st3_%d' % s4)
            op('vector', lambda e, a_t=a_t, st3=st3: e.scalar_tensor_tensor(out=xs2[:], in0=a_t[:], scalar=st3, in1=xs2[:], op0=ALU.mult, op1=ALU.add),
               reads=[an, 'xs2', 'st3_%d' % s4] + ['zs%d_%d' % (s4 % 2, db) for db in range(8)], writes=['xs2'])
            op('sync', lambda e, r0=r0: e.dma_start(out=out[r0:r0 + P, :], in_=xs2[:]), reads=['xs2'], writes=['out_%d' % (tt * 4 + s4)], dma='outst')

    if debug and stop_after == 3:
        pass
    return finish()


def col_layout(v):
    return np.ascontiguousarray(np.asarray(v, dtype=np.float32).reshape(-1, P).T)


def make_in_maps(inputs):
    x = np.asarray(inputs["x"], dtype=np.float32)
    c = np.asarray(inputs["c"], dtype=np.float32)
    w_in = np.asarray(inputs["w_in"], dtype=np.float32)[0]
    w_out = np.asarray(inputs["w_out"], dtype=np.float32)[0]
    conv_w = np.asarray(inputs["conv_w"], dtype=np.float32)[0]
    b_f = np.asarray(inputs["b_f"], dtype=np.float32)[0]
    aon = np.asarray(inputs["attn_out_norm"], dtype=np.float32)[0]
    con = np.asarray(inputs["conv_out_norm"], dtype=np.float32)[0]
    shared = {
        "w_ada": np.ascontiguousarray(np.asarray(inputs["w_ada"], dtype=np.float32)[0]),
        "b_ada": np.ascontiguousarray(np.asarray(inputs["b_ada"], dtype=np.float32)[0][None, :]),
        "pre1_col": col_layout(inputs["pre_norm_mix"][0]),
        "pre2_col": col_layout(inputs["pre_norm_ffn"][0]),
        "post1_row": np.ascontiguousarray(np.asarray(inputs["post_norm_mix"], dtype=np.float32)[0][None, :]),
        "post2_row": np.ascontiguousarray(np.asarray(inputs["post_norm_ffn"], dtype=np.float32)[0][None, :]),
        "w_gate": np.ascontiguousarray(np.asarray(inputs["w_gate"], dtype=np.float32)[0]),
        "w_up": np.ascontiguousarray(np.asarray(inputs["w_up"], dtype=np.float32)[0]),
        "w_down": np.ascontiguousarray(np.asarray(inputs["w_down"], dtype=np.float32)[0]),
    }
    mchunks = []
    for r in range(4):
        for lb in range(8):
            mchunks.append(4 * r + lb if lb < 4 else 16 + 4 * r + (lb - 4))
    gy_full = np.concatenate([aon, con])
    gy_perm = np.concatenate([gy_full[m * P:(m + 1) * P] for m in mchunks])
    shared["gy_col"] = col_layout(gy_perm)
    shared["w_out_perm"] = np.ascontiguousarray(np.concatenate([w_out[m * P:(m + 1) * P] for m in mchunks], axis=0))
    maps = []
    for core in range(NCORES):
        b, g = divmod(core, 4)
        m = dict(shared)
        m["x_full"] = np.ascontiguousarray(x[b])
        m["x_chunk"] = np.ascontiguousarray(x[b, g * TOK2:(g + 1) * TOK2])
        m["c_col"] = col_layout(c[b])
        sl = lambda base: w_in[:, base + 512 * g: base + 512 * g + 512]
        wq, wk, wv = sl(0), sl(2048), sl(4096)
        wf = w_in[:, 6144 + 4 * g: 6144 + 4 * g + 4]
        wgb, wgc, wu = sl(6160), sl(6160 + 2048), sl(6160 + 4096)
        m["w1"] = np.ascontiguousarray(np.concatenate([wq, wk, wgb, wgc, wu, wv, wf], axis=1))
        m["wf_col"] = np.ascontiguousarray(wf.reshape(32, P, 4).transpose(1, 0, 2).reshape(P, 128))
        m["bf_row"] = np.ascontiguousarray(b_f[4 * g:4 * g + 4][None, :])
        cw = conv_w[:, 512 * g:512 * g + 512]
        m["convw_col"] = np.ascontiguousarray(cw.reshape(3, 4, P).transpose(2, 1, 0).reshape(P, 12))
        maps.append(m)
    return maps


_NC_CACHE = {}


def kernel(**inputs):
    if "nc" not in _NC_CACHE:
        _NC_CACHE["nc"] = build_nc()
    nc = _NC_CACHE["nc"]
    in_maps = make_in_maps(inputs)
    res = run_bass_kernel_spmd(nc, in_maps, core_ids=list(range(NCORES)))
    outp = np.empty((2, S, D), dtype=np.float32)
    for core in range(NCORES):
        b, g = divmod(core, 4)
        outp[b, g * TOK2:(g + 1) * TOK2] = np.asarray(res.results[core]["out"])
    return outp
```

```python
import os
import numpy as np
import concourse.bass as bass
import concourse.mybir as mybir
from concourse.bass_utils import run_bass_kernel_spmd

F32 = mybir.dt.float32
BF16 = mybir.dt.bfloat16
AF = mybir.ActivationFunctionType
ALU = mybir.AluOpType

D = 4096
S = 8192
NCORES = 8
DFF = 11008
NFC = DFF // 128
EPS = 1e-6
P = 128
TT1 = 1024
NT1 = S // TT1
TT2 = 512
TOK2 = 2048
NT2 = TOK2 // TT2
NW1 = 3072 + 4

ENG = ['sync', 'scalar', 'vector', 'gpsimd', 'tensor']
COMPUTE = ['scalar', 'vector', 'gpsimd', 'tensor']


class Tr:
    def __init__(s, nc):
        s.nc = nc
        s.prog = {e: [] for e in ENG}
        s.esem = {e: [nc.alloc_semaphore("prog_" + e), 0] for e in COMPUTE}
        s.dsem = {}
        s.res = {}
        s.waited = {e: {} for e in ENG}

    def _handle(s, semkey):
        kind, k = semkey
        return s.esem[k][0] if kind == 'e' else s.dsem[k][0]

    def _need(s, e, toks):
        best = {}
        for (semkey, v) in toks:
            if semkey == ('e', 'tensor') and e == 'tensor':
                continue
            if v > best.get(semkey, 0):
                best[semkey] = v
        for semkey, v in best.items():
            if s.waited[e].get(semkey, 0) >= v:
                continue
            s.waited[e][semkey] = v
            h = s._handle(semkey)
            s.prog[e].append(lambda eng, h=h, v=v: eng.wait_ge(h, v))

    def op(s, e, fn, reads=(), writes=(), dma=None):
        writes = list(writes) + [r for r in reads if r.startswith('bk') and r not in writes]
        toks = []
        for r in reads:
            st = s.res.get(r)
            if st and st[0]:
                toks.append(st[0])
        for w in writes:
            st = s.res.get(w)
            if st:
                if st[0]:
                    toks.append(st[0])
                toks.extend(st[1].items())
        s._need(e, toks)
        if dma is not None:
            if dma not in s.dsem:
                s.dsem[dma] = [s.nc.alloc_semaphore("d_" + str(dma)), 0]
            d = s.dsem[dma]
            d[1] += 16
            tok = (('d', dma), d[1])
            h, inc = d[0], 16
        else:
            d = s.esem[e]
            d[1] += 1
            tok = (('e', e), d[1])
            h, inc = d[0], 1
        s.prog[e].append(lambda eng, fn=fn, h=h, inc=inc: fn(eng).then_inc(h, inc))
        for w in writes:
            s.res[w] = [tok, {}]
        for r in reads:
            st = s.res.setdefault(r, [None, {}])
            if tok[1] > st[1].get(tok[0], 0):
                st[1][tok[0]] = tok[1]
        return tok

    def barrier(s):
        toks = [(('e', f), s.esem[f][1]) for f in COMPUTE if s.esem[f][1] > 0]
        toks += [(('d', k), v[1]) for k, v in s.dsem.items()]
        for e in ENG:
            for semkey, v in toks:
                if s.waited[e].get(semkey, 0) >= v:
                    continue
                s.waited[e][semkey] = v
                h = s._handle(semkey)
                s.prog[e].append(lambda eng, h=h, v=v: eng.wait_ge(h, v))
        s.res = {}

    def coll(s, fn, reads=(), writes=()):
        e = 'gpsimd'
        toks = []
        for r in reads:
            st = s.res.get(r)
            if st and st[0]:
                toks.append(st[0])
        for w in writes:
            st = s.res.get(w)
            if st:
                if st[0]:
                    toks.append(st[0])
                toks.extend(st[1].items())
        s._need(e, toks)
        if 'cc' not in s.dsem:
            s.dsem['cc'] = [s.nc.alloc_semaphore("cc"), 0]
        d = s.dsem['cc']
        d[1] += 1
        tok = (('d', 'cc'), d[1])
        h = d[0]
        s.prog[e].append(lambda eng: fn(eng).then_inc(h))
        for w in writes:
            s.res[w] = [tok, {}]
        return tok

    def emit(s, block):
        def mk(e):
            def body(eng):
                for fn in s.prog[e]:
                    fn(eng)
            return body
        block.sync(mk('sync'))
        block.scalar(mk('scalar'))
        block.vector(mk('vector'))
        block.gpsimd(mk('gpsimd'))
        block.tensor(mk('tensor'))


def build_nc(stop_after=9, debug=False):
    nc = bass.Bass("TRN2", target_bir_lowering=False)

    def din(name, shape, dt=F32):
        return nc.dram_tensor(name, shape, dt, kind="ExternalInput").ap()

    x_full = din("x_full", [S, D])
    x_chunk = din("x_chunk", [TOK2, D])
    c_col_in = din("c_col", [P, 32])
    w_ada = din("w_ada", [D, 6 * D])
    b_ada = din("b_ada", [1, 6 * D])
    pre1_in = din("pre1_col", [P, 32])
    pre2_in = din("pre2_col", [P, 32])
    post1_in = din("post1_row", [1, D])
    post2_in = din("post2_row", [1, D])
    w1 = din("w1", [D, NW1])
    bf_in = din("bf_row", [1, 4])
    wf_in = din("wf_col", [P, 128])
    convw_in = din("convw_col", [P, 12])
    gy_in = din("gy_col", [P, 32])
    w_out = din("w_out_perm", [D, D])
    w_gate = din("w_gate", [D, DFF])
    w_up = din("w_up", [D, DFF])
    w_down = din("w_down", [DFF, D])
    out = nc.dram_tensor("out", [TOK2, D], F32, kind="ExternalOutput").ap()
    if debug:
        dbg = nc.dram_tensor("dbg", [P, 8192], F32, kind="ExternalOutput").ap()

    Gs = nc.dram_tensor("Gs", [2, D], F32)
    qTs = nc.dram_tensor("qTs", [4 * P, S], BF16)
    kTs = nc.dram_tensor("kTs", [4 * P, S], BF16)
    Vs = nc.dram_tensor("Vs", [S, 512], BF16)
    ybuf = nc.dram_tensor("ybuf", [16384, 512], BF16)
    yslots = nc.dram_tensor("yslots", [16384, 512], BF16)
    ymine = nc.dram_tensor("ymine", [16384, 512], BF16)
    x1s = nc.dram_tensor("x1s", [TOK2, D], F32)
    accs = nc.dram_tensor("accs", [TOK2, D], F32)

    BASE = 16512
    LIMIT = 229344

    class Arena:
        def __init__(self, start):
            self.off = start
            self.n = 0

        def alloc(self, shape, dt, name=None):
            nbytes = int(np.prod(shape[1:])) * (4 if dt == F32 else 2)
            nbytes = (nbytes + 63) // 64 * 64
            self.n += 1
            t = nc.alloc_sbuf_tensor_at(name or ("t%d_%d" % (self.off, self.n)), list(shape), dt, offset=self.off)
            self.off += nbytes
            assert self.off <= LIMIT, (self.off, LIMIT)
            return t

    A0 = Arena(BASE)
    ident_bf = A0.alloc([P, P], BF16, "ident_bf")
    ident_f = A0.alloc([P, P], F32, "ident_f")
    tri_bf = A0.alloc([P, P], BF16, "tri_bf")
    tri_f = A0.alloc([P, P], F32, "tri_f")
    ones_bf = A0.alloc([P, P], BF16, "ones_bf")
    ones_f = A0.alloc([P, P], F32, "ones_f")
    gain1c = A0.alloc([P, 32], F32, "gain1c")
    shift1c = A0.alloc([P, 32], F32, "shift1c")
    gain2c = A0.alloc([P, 32], F32, "gain2c")
    shift2c = A0.alloc([P, 32], F32, "shift2c")
    sc1c = A0.alloc([P, 32], F32, "sc1c")
    sc2c = A0.alloc([P, 32], F32, "sc2c")
    pre1c = A0.alloc([P, 32], F32, "pre1c")
    pre2c = A0.alloc([P, 32], F32, "pre2c")
    gyc = A0.alloc([P, 32], F32, "gyc")
    ccol = A0.alloc([P, 32], F32, "ccol")
    cact = A0.alloc([P, 32], F32, "cact")
    SP = A0.alloc([P, 256], F32, "SP")
    wf_sb = A0.alloc([P, 32, 4], BF16, "wf_sb")
    bfrep = A0.alloc([P, 32], F32, "bfrep")
    convw = A0.alloc([P, 12], F32, "convw")
    halo = A0.alloc([P, 4, 2], F32, "halo")
    stat = A0.alloc([P, 64], F32, "stat")
    junk_bf = A0.alloc([P, 512], BF16, "junk_bf")
    junk_f = A0.alloc([P, P], F32, "junk_f")
    P0 = BASE + 8192
    assert A0.off <= P0, A0.off

    pp = [nc.alloc_psum_tensor("pp%d" % i, [P, 1024], F32) for i in range(4)]
    ps = [pp[i // 2][:, (i % 2) * 512:(i % 2 + 1) * 512] for i in range(8)]
    psn = ['bk%d' % i for i in range(8)]

    T = Tr(nc)
    op = T.op

    def mk_const(t, val, cmp_op=None):
        op('gpsimd', lambda e: e.memset(t[:], val), writes=[t.name])
        if cmp_op is not None:
            op('gpsimd', lambda e: e.affine_select(out=t[:], in_=t[:], pattern=[[1, P]], compare_op=cmp_op,
                                                   fill=0.0, base=0, channel_multiplier=-1),
               reads=[t.name], writes=[t.name])

    mk_const(ident_bf, 1.0, ALU.is_equal)
    mk_const(ident_f, 1.0, ALU.is_equal)
    mk_const(tri_bf, 1.0, ALU.is_ge)
    mk_const(tri_f, 1.0, ALU.is_ge)
    mk_const(ones_bf, 1.0)
    mk_const(ones_f, 1.0)
    op('gpsimd', lambda e: e.memset(halo[:], 0.0), writes=['halo'])

    def small_load(dst, src, key, name):
        op('sync', lambda e: e.dma_start(out=dst, in_=src), writes=[name], dma=key)

    small_load(ccol[:], c_col_in[:, :], 'm0', 'ccol')
    small_load(pre1c[:], pre1_in[:, :], 'm1', 'pre1c')
    small_load(pre2c[:], pre2_in[:, :], 'm2', 'pre2c')
    small_load(gyc[:], gy_in[:, :], 'm3', 'gyc')
    small_load(convw[:], convw_in[:, :], 'm4', 'convw')
    for s8 in range(8):
        small_load(bfrep[:, s8 * 4:(s8 + 1) * 4], bf_in[0:1, :].partition_broadcast(P), 'm5', 'bfrep%d' % s8)
    op('gpsimd', lambda e: e.dma_start(out=wf_sb[:].rearrange("p k n -> p (k n)"), in_=wf_in[:, :]), writes=['wf_sb'], dma='m6')

    def emit_rstd(dst, src, mul, rname, wname):
        op('vector', lambda e: e.tensor_scalar(out=dst, in0=src, scalar1=mul, scalar2=EPS, op0=ALU.mult, op1=ALU.add),
           reads=rname, writes=[wname])
        op('scalar', lambda e: e.activation(out=dst, in_=dst, func=AF.Sqrt), reads=[wname], writes=[wname])
        op('vector', lambda e: e.reciprocal(out=dst, in_=dst), reads=[wname], writes=[wname])

    A = Arena(P0)
    cB = A.alloc([P, 32, P], BF16, "cB")
    ring0 = [A.alloc([P, 16, 512], BF16, "ring0_%d" % i) for i in range(3)]
    bt = [A.alloc([P, 512], F32, "bt%d" % i) for i in range(2)]
    postr = [A.alloc([P, 512], F32, "postr%d" % i) for i in range(2)]
    modblk = [A.alloc([P, 512], F32, "modblk%d" % i) for i in range(2)]
    gblk = [A.alloc([P, 512], F32, "gblk%d" % i) for i in range(2)]

    op('scalar', lambda e: e.activation(out=cact[:], in_=ccol[:], func=AF.Silu), reads=['ccol'], writes=['cact'])
    for k in range(32):
        op('vector', lambda e, k=k: e.tensor_scalar(out=cB[:, k, :], in0=ones_bf[:], scalar1=cact[:, k:k + 1],
                                                    scalar2=None, op0=ALU.mult),
           reads=['cact', 'ones_bf'], writes=['cB%d' % k])
    cBn = ['cB%d' % k for k in range(32)]

    rcount = [0]

    def ring_load(rings, view_fn, src_ap, nslots):
        i = rcount[0] % nslots
        rcount[0] += 1
        rn = 'ring%d' % i
        dst = view_fn(rings[i])
        op('gpsimd', lambda e: e.dma_start(out=dst, in_=src_ap), writes=[rn], dma=rn)
        return rings[i], rn

    seg_cols = {0: shift1c, 1: sc1c, 3: shift2c, 4: sc2c}
    for cb in range(48):
        seg = cb // 8
        c0 = cb * 512
        pb = ps[cb % 2]
        pbn = psn[cb % 2]
        for kg in range(2):
            rt, rn = ring_load(ring0, lambda t: t[:],
                               w_ada[kg * 2048:(kg + 1) * 2048, c0:c0 + 512].rearrange("(k p) n -> p k n", p=P), 3)

            def mm(e, rt=rt, kg=kg, pb=pb):
                ins = None
                for k in range(16):
                    ins = e.matmul(pb, lhsT=cB[:, kg * 16 + k, :], rhs=rt[:, k, :],
                                   start=(kg == 0 and k == 0), stop=(kg == 1 and k == 15))
                return ins
            op('tensor', mm, reads=[rn] + cBn, writes=[pbn])
        b_t = bt[cb % 2]
        m_t = modblk[cb % 2]
        op('sync', lambda e, b_t=b_t, c0=c0: e.dma_start(out=b_t[:], in_=b_ada[0:1, c0:c0 + 512].partition_broadcast(P)),
           writes=[b_t.name], dma=b_t.name)
        op('vector', lambda e, m_t=m_t, pb=pb, b_t=b_t: e.tensor_tensor(out=m_t[:], in0=pb, in1=b_t[:], op=ALU.add),
           reads=[pbn, b_t.name], writes=[m_t.name])
        if seg in seg_cols:
            colt = seg_cols[seg]
            for i in range(4):
                cidx = (cb % 8) * 4 + i
                op('vector', lambda e, m_t=m_t, i=i, colt=colt, cidx=cidx: e.scalar_tensor_tensor(
                    out=junk_f[:], in0=m_t[:, i * P:(i + 1) * P], scalar=1.0, in1=ident_f[:],
                    op0=ALU.mult, op1=ALU.mult, accum_out=colt[:, cidx:cidx + 1]),
                   reads=[m_t.name], writes=['junk_f', colt.name + str(cidx)])
        else:
            which = 0 if seg == 2 else 1
            prow = post1_in if which == 0 else post2_in
            p_t = postr[cb % 2]
            g_t = gblk[cb % 2]
            cc0 = (cb % 8) * 512
            op('sync', lambda e, p_t=p_t, prow=prow, cc0=cc0: e.dma_start(
                out=p_t[:], in_=prow[0:1, cc0:cc0 + 512].partition_broadcast(P)), writes=[p_t.name], dma=p_t.name)
            op('vector', lambda e, g_t=g_t, m_t=m_t, p_t=p_t: e.tensor_tensor(out=g_t[:], in0=m_t[:], in1=p_t[:], op=ALU.mult),
               reads=[m_t.name, p_t.name], writes=[g_t.name])
            op('sync', lambda e, g_t=g_t, which=which, cc0=cc0: e.dma_start(out=Gs[which:which + 1, cc0:cc0 + 512], in_=g_t[0:1, :]),
               reads=[g_t.name], writes=['Gs%d_%d' % (which, cb % 8)], dma='gst%d' % (cb % 2))
    allc = lambda t: [t.name + str(i) for i in range(32)]
    op('vector', lambda e: e.scalar_tensor_tensor(out=gain1c[:], in0=sc1c[:], scalar=1.0, in1=pre1c[:], op0=ALU.add, op1=ALU.mult),
       reads=allc(sc1c) + ['pre1c'], writes=['gain1c'])
    op('vector', lambda e: e.scalar_tensor_tensor(out=gain2c[:], in0=sc2c[:], scalar=1.0, in1=pre2c[:], op0=ALU.add, op1=ALU.mult),
       reads=allc(sc2c) + ['pre2c'], writes=['gain2c'])
    shift1n = allc(shift1c)
    shift2n = allc(shift2c)
    GsN = [['Gs%d_%d' % (w, i) for i in range(8)] for w in range(2)]

    def dbg_dump(items, reads):
        dt_ = nc.alloc_sbuf_tensor_at("dbgt", [P, 8192], F32, offset=LIMIT - 8192 * 4 - 64)
        op('vector', lambda e: e.memset(dt_[:], 0.0), writes=['dbgt'])
        for (off, n, src) in items:
            op('vector', lambda e, off=off, n=n, src=src: e.tensor_copy(out=dt_[:, off:off + n], in_=src),
               reads=list(reads) + ['dbgt'], writes=['dbgt'])
        op('sync', lambda e: e.dma_start(out=dbg[:, :], in_=dt_[:]), reads=['dbgt'], dma='dbgs')

    def finish():
        T.barrier()
        with nc.Block() as block:
            T.emit(block)
        return nc

    if stop_after == 0:
        if debug:
            dbg_dump([(0, 32, gain1c[:]), (32, 32, shift1c[:]), (64, 32, gain2c[:]), (96, 32, shift2c[:])],
                     ['gain1c', 'gain2c'] + shift1n + shift2n)
        return finish()

    T.barrier()
    A = Arena(P0)
    hT = A.alloc([P, 32, TT1], BF16, "hT")
    ring1 = [A.alloc([P, 32, 256], BF16, "ring1_%d" % i) for i in range(3)]
    xs = [A.alloc([P, D], F32, "xs%d" % i) for i in range(2)]
    xn = [A.alloc([P, D], BF16, "xn%d" % i) for i in range(2)]
    qst = [A.alloc([P, TT1], BF16, "qst%d" % i) for i in range(2)]
    vst = A.alloc([P, 8, 256], BF16, "vst")
    gb_sb = A.alloc([P, 2, TT1], F32, "gb_sb")
    gc_sb = A.alloc([P, 2, TT1], F32, "gc_sb")
    vbuf = A.alloc([P, 2, TT1 + 2], F32, "vbuf")
    ysb = [A.alloc([P, TT1], BF16, "ysb%d" % i) for i in range(2)]
    fsm = A0.alloc([P, 32], F32, "fsm")

    hTn = ['hT%d' % k for k in range(32)]
    ybn = []
    tbf = pp[0][:].bitcast(BF16)
    QSCALE = 1.0 / float(np.sqrt(128.0))
    blocks = [('q', 0, 0), ('q', 256, 1), ('k', 512, 0), ('k', 768, 1),
              ('gb', 1024, 0), ('gc', 1536, 0), ('u', 2048, 0),
              ('gb', 1280, 1), ('gc', 1792, 1), ('u', 2304, 1),
              ('v', 2560, 0), ('v', 2816, 1)]
    cnt = {'xs': 0, 'st': 0, 'pair': 0, 'bank': 0, 'qst': 0, 'ysb': 0, 'ev': 0}
    NT1_RUN = NT1 if stop_after != 1 or not debug else 2
    if int(os.environ.get('K_CUT', '99')) == 0:
        NT1_RUN = 0

    for ti in range(NT1_RUN):
        for s8 in range(8):
            i = cnt['xs'] % 2
            cnt['xs'] += 1
            xt, xnt = xs[i], xn[i]
            r0 = ti * TT1 + s8 * P
            op('sync', lambda e, xt=xt, r0=r0: e.dma_start(out=xt[:], in_=x_full[r0:r0 + P, :]), writes=[xt.name], dma=xt.name)
            sc = cnt['st'] % 32
            cnt['st'] += 1
            stc = stat[:, sc:sc + 1]
            stn = 'stat%d' % sc
            op('scalar', lambda e, xt=xt, xnt=xnt, stc=stc: e.activation(out=xnt[:], in_=xt[:], func=AF.Square, accum_out=stc),
               reads=[xt.name], writes=[xnt.name, stn])
            emit_rstd(stc, stc, 1.0 / D, [stn], stn)
            op('vector', lambda e, xt=xt, xnt=xnt, stc=stc: e.tensor_scalar(out=xnt[:], in0=xt[:], scalar1=stc, scalar2=None, op0=ALU.mult),
               reads=[xt.name, stn], writes=[xnt.name])
            for g4 in range(4):
                half = g4 % 2
                tv = tbf[:, half * 1024:(half + 1) * 1024]
                bn = psn[half]

                def tp(e, xnt=xnt, g4=g4, tv=tv):
                    ins = None
                    for kk in range(8):
                        k = g4 * 8 + kk
                        ins = e.transpose(out=tv[:, kk * P:(kk + 1) * P], in_=xnt[:, k * P:(k + 1) * P], identity=ident_bf[:])
                    return ins
                op('tensor', tp, reads=[xnt.name], writes=[bn])

                def ev_act(e, g4=g4, tv=tv, s8=s8):
                    ins = None
                    for kk in range(8):
                        k = g4 * 8 + kk
                        ins = e.activation(out=hT[:, k, s8 * P:(s8 + 1) * P], in_=tv[:, kk * P:(kk + 1) * P], func=AF.Identity,
                                           scale=gain1c[:, k:k + 1], bias=shift1c[:, k:k + 1])
                    return ins

                def ev_dve(e, g4=g4, tv=tv, s8=s8):
                    ins = None
                    for kk in range(8):
                        k = g4 * 8 + kk
                        ins = e.tensor_scalar(out=hT[:, k, s8 * P:(s8 + 1) * P], in0=tv[:, kk * P:(kk + 1) * P],
                                              scalar1=gain1c[:, k:k + 1], scalar2=shift1c[:, k:k + 1], op0=ALU.mult, op1=ALU.add)
                    return ins
                if g4 % 2 == 0:
                    op('scalar', ev_act, reads=[bn, 'gain1c'] + shift1n, writes=[hTn[g4 * 8 + kk] for kk in range(8)])
                else:
                    op('vector', ev_dve, reads=[bn, 'gain1c'] + shift1n, writes=[hTn[g4 * 8 + kk] for kk in range(8)])

        KCUT = int(os.environ.get("K_CUT", "99"))
        if KCUT <= 1:
            continue
        def mmf(e):
            ins = None
            for s8 in range(8):
                for k in range(32):
                    ins = e.matmul(ps[7][:, s8 * 4:(s8 + 1) * 4], lhsT=hT[:, k, s8 * P:(s8 + 1) * P], rhs=wf_sb[:, k, :],
                                   start=(k == 0), stop=(k == 31))
            return ins
        op('tensor', mmf, reads=hTn + ['wf_sb'], writes=[psn[7]])
        op('vector', lambda e: e.tensor_tensor(out=fsm[:], in0=ps[7][:, 0:32], in1=bfrep[:], op=ALU.add),
           reads=[psn[7]] + ['bfrep%d' % i for i in range(8)], writes=['fsm'])
        op('scalar', lambda e: e.activation(out=fsm[:], in_=fsm[:], func=AF.Exp, scale=-1.0), reads=['fsm'], writes=['fsm'])
        op('scalar', lambda e, ti=ti: e.activation(out=SP[:, ti * 32:(ti + 1) * 32], in_=fsm[:], func=AF.Ln, bias=1.0),
           reads=['fsm'], writes=['SP%d' % ti])

        for bidx, (kind, col0, idx) in enumerate(blocks):
            if bidx >= KCUT - 2:
                break
            rt, rn = ring_load(ring1, lambda t: t[:], w1[:, col0:col0 + 256].rearrange("(k p) n -> p k n", p=P), 3)
            if kind == 'v':
                for s8 in range(8):
                    bi = 2 + cnt['bank'] % 6
                    cnt['bank'] += 1

                    def mmv(e, rt=rt, s8=s8, bi=bi):
                        ins = None
                        for k in range(32):
                            ins = e.matmul(ps[bi][:, 0:256], lhsT=hT[:, k, s8 * P:(s8 + 1) * P], rhs=rt[:, k, :],
                                           start=(k == 0), stop=(k == 31))
                        return ins
                    op('tensor', mmv, reads=hTn + [rn], writes=[psn[bi]])
                    eng = 'scalar' if s8 % 2 == 0 else 'vector'
                    if eng == 'scalar':
                        op('scalar', lambda e, s8=s8, bi=bi: e.activation(out=vst[:, s8, :], in_=ps[bi][:, 0:256], func=AF.Copy),
                           reads=[psn[bi]], writes=['vst%d' % s8])
                    else:
                        op('vector', lambda e, s8=s8, bi=bi: e.tensor_copy(out=vst[:, s8, :], in_=ps[bi][:, 0:256]),
                           reads=[psn[bi]], writes=['vst%d' % s8])
                op('sync', lambda e, ti=ti, idx=idx: e.dma_start(
                    out=Vs[ti * TT1:(ti + 1) * TT1, idx * 256:(idx + 1) * 256].rearrange("(s p) c -> p s c", p=P), in_=vst[:]),
                   reads=['vst%d' % i for i in range(8)], writes=['Vs_%d_%d' % (ti, idx)], dma='vst')
                continue
            for cch in range(2):
                pi = 1 + cnt['pair'] % 3
                cnt['pair'] += 1
                pair = pp[pi]
                pn = [psn[2 * pi], psn[2 * pi + 1]]

                def mmq(e, rt=rt, cch=cch, pair=pair):
                    ins = None
                    for k in range(32):
                        for half in range(2):
                            ins = e.matmul(pair[:, half * 512:(half + 1) * 512], lhsT=rt[:, k, cch * P:(cch + 1) * P],
                                           rhs=hT[:, k, half * 512:(half + 1) * 512], start=(k == 0), stop=(k == 31))
                    return ins
                op('tensor', mmq, reads=hTn + [rn], writes=pn)
                if kind in ('q', 'k'):
                    hl = idx * 2 + cch
                    qi = cnt['qst'] % 2
                    cnt['qst'] += 1
                    qt = qst[qi]
                    dst = qTs if kind == 'q' else kTs
                    if kind == 'q':
                        op('scalar', lambda e, qt=qt, pair=pair: e.activation(out=qt[:], in_=pair[:], func=AF.Copy, scale=QSCALE),
                           reads=pn, writes=[qt.name])
                    else:
                        op('vector', lambda e, qt=qt, pair=pair: e.tensor_copy(out=qt[:], in_=pair[:]), reads=pn, writes=[qt.name])
                    op('sync', lambda e, qt=qt, dst=dst, hl=hl, ti=ti: e.dma_start(
                        out=dst[hl * P:(hl + 1) * P, ti * TT1:(ti + 1) * TT1], in_=qt[:]),
                       reads=[qt.name], writes=['%sTs_%d_%d' % (kind, hl, ti)], dma=qt.name)
                elif kind == 'gb':
                    op('scalar', lambda e, cch=cch, pair=pair: e.activation(out=gb_sb[:, cch, :], in_=pair[:], func=AF.Copy),
                       reads=pn, writes=['gb%d' % cch])
                elif kind == 'gc':
                    op('vector', lambda e, cch=cch, pair=pair: e.tensor_copy(out=gc_sb[:, cch, :], in_=pair[:]),
                       reads=pn, writes=['gc%d' % cch])
                else:
                    cc = idx * 2 + cch
                    vn = 'vb%d' % cch
                    hn = 'halo%d' % cc
                    gcn = 'gc%d' % cch
                    yi = cnt['ysb'] % 2
                    cnt['ysb'] += 1
                    yt = ysb[yi]
                    op('vector', lambda e, cch=cch, cc=cc: e.tensor_copy(out=vbuf[:, cch, 0:2], in_=halo[:, cc, :]),
                       reads=['halo', hn], writes=[vn])
                    op('vector', lambda e, cch=cch, pair=pair: e.tensor_tensor(out=vbuf[:, cch, 2:2 + TT1], in0=pair[:], in1=gc_sb[:, cch, :], op=ALU.mult),
                       reads=pn + [gcn, vn], writes=[vn])
                    op('vector', lambda e, cch=cch, cc=cc: e.tensor_copy(out=halo[:, cc, :], in_=vbuf[:, cch, TT1:TT1 + 2]),
                       reads=[vn], writes=[hn])
                    op('vector', lambda e, cch=cch, cc=cc: e.tensor_scalar(out=gc_sb[:, cch, :], in0=vbuf[:, cch, 2:2 + TT1],
                                                                        scalar1=convw[:, cc * 3 + 2:cc * 3 + 3], scalar2=None, op0=ALU.mult),
                       reads=[vn, 'convw'], writes=[gcn])
                    op('vector', lambda e, cch=cch, cc=cc: e.scalar_tensor_tensor(out=gc_sb[:, cch, :], in0=vbuf[:, cch, 1:1 + TT1],
                                                                               scalar=convw[:, cc * 3 + 1:cc * 3 + 2], in1=gc_sb[:, cch, :],
                                                                               op0=ALU.mult, op1=ALU.add),
                       reads=[vn, gcn], writes=[gcn])
                    op('vector', lambda e, cch=cch, cc=cc: e.scalar_tensor_tensor(out=gc_sb[:, cch, :], in0=vbuf[:, cch, 0:TT1],
                                                                               scalar=convw[:, cc * 3:cc * 3 + 1], in1=gc_sb[:, cch, :],
                                                                               op0=ALU.mult, op1=ALU.add),
                       reads=[vn, gcn], writes=[gcn])
                    op('vector', lambda e, cch=cch, yt=yt: e.tensor_tensor(out=yt[:], in0=gc_sb[:, cch, :], in1=gb_sb[:, cch, :], op=ALU.mult),
                       reads=[gcn, 'gb%d' % cch], writes=[yt.name])
                    for hf in range(2):
                        rrow = ((ti * 2 + hf) * 8 + 4 + cc) * P
                        yn_ = 'yb_c_%d_%d_%d' % (ti, cc, hf)
                        ybn.append(yn_)
                        op('sync', lambda e, yt=yt, rrow=rrow, hf=hf: e.dma_start(out=ybuf[rrow:rrow + P, :], in_=yt[:, hf * 512:(hf + 1) * 512]),
                           reads=[yt.name], writes=[yn_], dma=yt.name + '_%d' % hf)

    if stop_after == 1:
        if debug:
            T.barrier()
            dt_ = nc.alloc_sbuf_tensor_at("dbgt", [P, 8192], F32, offset=LIMIT - 8192 * 4 - 64)
            op('vector', lambda e: e.memset(dt_[:], 0.0), writes=['dbgt'])
            op('vector', lambda e: e.tensor_copy(out=dt_[:, 0:256], in_=SP[:]), reads=['dbgt'], writes=['dbgt'])
            op('vector', lambda e: e.tensor_copy(out=dt_[:, 256:384], in_=hT[:, 0, 896:1024]), reads=['dbgt'], writes=['dbgt'])
            op('vector', lambda e: e.tensor_copy(out=dt_[:, 384:512], in_=hT[:, 31, 0:128]), reads=['dbgt'], writes=['dbgt'])
            qd = nc.alloc_sbuf_tensor_at("qd", [P, 4, 1024], BF16, offset=LIMIT - 8192 * 4 - 64 - 8192)
            op('sync', lambda e: e.dma_start(out=qd[:, 0, :], in_=qTs[0:P, 0:1024]), writes=['qd'], dma='dq0')
            op('sync', lambda e: e.dma_start(out=qd[:, 1, :], in_=kTs[P:2 * P, 1024:2048]), writes=['qd1'], dma='dq1')
            op('sync', lambda e: e.dma_start(out=qd[:, 2, 0:512], in_=Vs[1024:1024 + P, 0:512]), writes=['qd2'], dma='dq2')
            op('sync', lambda e: e.dma_start(out=qd[:, 3, :], in_=ybuf[5 * P:6 * P, 0:1024]), writes=['qd3'], dma='dq3')
            op('vector', lambda e: e.tensor_copy(out=dt_[:, 1024:2048], in_=qd[:, 0, :]), reads=['qd', 'dbgt'], writes=['dbgt'])
            op('vector', lambda e: e.tensor_copy(out=dt_[:, 2048:3072], in_=qd[:, 1, :]), reads=['qd1', 'dbgt'], writes=['dbgt'])
            op('vector', lambda e: e.tensor_copy(out=dt_[:, 3072:3584], in_=qd[:, 2, 0:512]), reads=['qd2', 'dbgt'], writes=['dbgt'])
            op('vector', lambda e: e.tensor_copy(out=dt_[:, 4096:5120], in_=qd[:, 3, :]), reads=['qd3', 'dbgt'], writes=['dbgt'])
            op('sync', lambda e: e.dma_start(out=dbg[:, :], in_=dt_[:]), reads=['dbgt'], dma='dbgs')
        return finish()

    T.barrier()
    A = Arena(P0)
    qT = [A.alloc([P, S], BF16, "qT%d" % i) for i in range(2)]
    kT = [A.alloc([P, S], BF16, "kT%d" % i) for i in range(2)]
    Vt = [A.alloc([P, 64, P], BF16, "Vt%d" % i) for i in range(2)]
    Bm = [A.alloc([P, 64, 64], F32, "Bm%d" % i) for i in range(2)]
    PT = [A.alloc([P, 512], BF16, "PT%d" % i) for i in range(3)]
    rl = [A.alloc([P, 512], F32, "rl%d" % i) for i in range(2)]
    ost = [A.alloc([P, 512], BF16, "ost%d" % i) for i in range(2)]
    CumH = A.alloc([P, 4, 64], F32, "CumH")
    TotH = A.alloc([P, 4, 64], F32, "TotH")
    EndH = A.alloc([P, 4, 64], F32, "EndH")
    CumF = A.alloc([P, 4, 64], F32, "CumF")
    SPn = ['SP%d' % i for i in range(NT1)]

    op('tensor', lambda e: e.matmul(ps[6][:, 0:256], lhsT=tri_f[:], rhs=SP[:], start=True, stop=True), reads=SPn + ['tri_f'], writes=[psn[6]])
    op('tensor', lambda e: e.matmul(ps[7][:, 0:256], lhsT=ones_f[:], rhs=SP[:], start=True, stop=True), reads=SPn + ['ones_f'], writes=[psn[7]])
    op('vector', lambda e: e.tensor_copy(out=CumH[:], in_=ps[6][:, 0:256].rearrange("p (kb h) -> p h kb", h=4)), reads=[psn[6]], writes=['CumH'])
    op('vector', lambda e: e.tensor_copy(out=TotH[:], in_=ps[7][:, 0:256].rearrange("p (kb h) -> p h kb", h=4)), reads=[psn[7]], writes=['TotH'])
    for h in range(4):
        op('vector', lambda e, h=h: e.tensor_tensor_scan(out=EndH[:, h, :], data0=ones_f[:, 0:64], data1=TotH[:, h, :], initial=0.0,
                                                         op0=ALU.mult, op1=ALU.add), reads=['TotH', 'ones_f'], writes=['EndH%d' % h])
    EndN = ['EndH%d' % h for h in range(4)]
    op('vector', lambda e: e.tensor_tensor(out=CumF[:], in0=CumH[:], in1=EndH[:], op=ALU.add), reads=['CumH'] + EndN, writes=['CumF'])
    op('vector', lambda e: e.tensor_tensor(out=CumF[:], in0=CumF[:], in1=TotH[:], op=ALU.subtract), reads=['CumF', 'TotH'], writes=['CumF'])

    cntb = {'s': 0, 'pt': 0, 'o': 0, 'ost': 0}
    NH_RUN = 4 if not (debug and stop_after == 2) else 1
    PTx = PT + [A.alloc([P, 512], BF16, "PT3")]
    SB = [0, 1, 6, 7]
    LOOK = 2

    def head_setup(hl):
        sl = hl % 2
        q_t, k_t, v_t, b_t = qT[sl], kT[sl], Vt[sl], Bm[sl]
        op('sync', lambda e, q_t=q_t, hl=hl: e.dma_start(out=q_t[:], in_=qTs[hl * P:(hl + 1) * P, :]),
           reads=['qTs_%d_%d' % (hl, t) for t in range(NT1)], writes=[q_t.name], dma=q_t.name)
        op('sync', lambda e, k_t=k_t, hl=hl: e.dma_start(out=k_t[:], in_=kTs[hl * P:(hl + 1) * P, :]),
           reads=['kTs_%d_%d' % (hl, t) for t in range(NT1)], writes=[k_t.name], dma=k_t.name)
        op('sync', lambda e, v_t=v_t, hl=hl: e.dma_start(out=v_t[:], in_=Vs[:, hl * P:(hl + 1) * P].rearrange("(kb p) d -> p kb d", p=P)),
           reads=['Vs_%d_%d' % (t, hl // 2) for t in range(NT1)], writes=[v_t.name], dma=v_t.name)

        def mkB(e, b_t=b_t, hl=hl):
            ins = None
            for kb in range(64):
                ins = e.tensor_scalar(out=b_t[:, kb, :], in0=EndH[:, hl, :], scalar1=-1.0, scalar2=CumF[:, hl, kb:kb + 1],
                                      op0=ALU.mult, op1=ALU.add)
            return ins
        op('vector', mkB, reads=EndN + ['CumF'], writes=[b_t.name])

    blks = []
    for hl in range(NH_RUN):
        for qg in range(16):
            for kb in range(4 * qg + 4):
                blks.append((hl, qg, kb))
    info = {}

    def emit_S(i):
        hl, qg, kb = blks[i]
        sl = hl % 2
        q_t, k_t = qT[sl], kT[sl]
        c0 = max(0, kb - 4 * qg) * P
        bi = SB[i % 4]
        bS, bSn = ps[bi], psn[bi]
        op('tensor', lambda e, bS=bS, c0=c0, k_t=k_t, q_t=q_t, kb=kb, qg=qg: e.matmul(
            bS[:, c0:512], lhsT=k_t[:, kb * P:(kb + 1) * P], rhs=q_t[:, qg * 512 + c0:(qg + 1) * 512], start=True, stop=True),
           reads=[k_t.name, q_t.name], writes=[bSn])
        info[i] = (bS, bSn)

    head_setup(0)
    if NH_RUN > 1:
        head_setup(1)
    for i in range(min(LOOK, len(blks))):
        emit_S(i)
    for i, (hl, qg, kb) in enumerate(blks):
        sl = hl % 2
        v_t, b_t = Vt[sl], Bm[sl]
        nkb = 4 * qg + 4
        if qg == 0 and kb == 0 and hl >= 1 and hl + 1 < NH_RUN:
            head_setup(hl + 1)
        if i + LOOK < len(blks):
            emit_S(i + LOOK)
        if kb == 0:
            oi = cntb['o'] % 2
            cntb['o'] += 1
            info['o'] = (ps[2 + oi], ps[4 + oi], psn[2 + oi], psn[4 + oi])
        bO, bL, bOn, bLn = info['o']
        bS, bSn = info.pop(i)
        j0 = max(0, kb - 4 * qg)
        c0 = j0 * P
        p_t = PTx[i % 4]

        def ex(e, bS=bS, p_t=p_t, b_t=b_t, j0=j0, kb=kb, qg=qg):
            ins = None
            for j in range(j0, 4):
                ins = e.activation(out=p_t[:, j * P:(j + 1) * P], in_=bS[:, j * P:(j + 1) * P], func=AF.Exp,
                                   bias=b_t[:, kb, 4 * qg + j:4 * qg + j + 1], scale=1.0)
            return ins
        op('scalar', ex, reads=[bSn, b_t.name], writes=[p_t.name])
        if kb >= 4 * qg:
            op('vector', lambda e, p_t=p_t, c0=c0: e.tensor_tensor(out=p_t[:, c0:c0 + P], in0=p_t[:, c0:c0 + P], in1=tri_bf[:], op=ALU.mult),
               reads=[p_t.name, 'tri_bf'], writes=[p_t.name])

        def pv(e, bO=bO, bL=bL, p_t=p_t, v_t=v_t, kb=kb, c0=c0, nkb=nkb):
            e.matmul(bO[:, c0:512], lhsT=v_t[:, kb, :], rhs=p_t[:, c0:512], start=(kb == 0), stop=(kb == nkb - 1))
            return e.matmul(bL[:, c0:512], lhsT=ones_bf[:], rhs=p_t[:, c0:512], start=(kb == 0), stop=(kb == nkb - 1))
        op('tensor', pv, reads=[p_t.name, v_t.name, 'ones_bf'], writes=[bOn, bLn])
        if kb == nkb - 1:
            oi2 = cntb['ost'] % 2
            cntb['ost'] += 1
            r_t, o_t = rl[oi2], ost[oi2]
            op('vector', lambda e, r_t=r_t, bL=bL: e.reciprocal(out=r_t[:], in_=bL), reads=[bLn], writes=[r_t.name])
            op('vector', lambda e, r_t=r_t, o_t=o_t, bO=bO: e.tensor_tensor(out=o_t[:], in0=bO, in1=r_t[:], op=ALU.mult),
               reads=[bOn, r_t.name], writes=[o_t.name])
            rrow = (qg * 8 + hl) * P
            yn_ = 'yb_a_%d_%d' % (hl, qg)
            ybn.append(yn_)
            op('sync', lambda e, o_t=o_t, rrow=rrow: e.dma_start(out=ybuf[rrow:rrow + P, :], in_=o_t[:]),
               reads=[o_t.name], writes=[yn_], dma=o_t.name)

    if stop_after == 2:
        if debug:
            dt_ = nc.alloc_sbuf_tensor_at("dbgt", [P, 8192], F32, offset=LIMIT - 8192 * 4 - 64)
            qd = nc.alloc_sbuf_tensor_at("qd", [P, 2, 2048], BF16, offset=LIMIT - 8192 * 4 - 64 - 8192)
            T.barrier()
            op('vector', lambda e: e.memset(dt_[:], 0.0), writes=['dbgt'])
            op('sync', lambda e: e.dma_start(out=qd[:, 0, :], in_=ybuf[0:P, :]), writes=['qd'], dma='dq0')
            op('sync', lambda e: e.dma_start(out=qd[:, 1, :], in_=ybuf[24 * P:25 * P, :]), writes=['qd1'], dma='dq1')
            op('vector', lambda e: e.tensor_copy(out=dt_[:, 0:2048], in_=qd[:, 0, :]), reads=['qd', 'dbgt'], writes=['dbgt'])
            op('vector', lambda e: e.tensor_copy(out=dt_[:, 2048:4096], in_=qd[:, 1, :]), reads=['qd1', 'dbgt'], writes=['dbgt'])
            op('vector', lambda e: e.tensor_copy(out=dt_[:, 4096:4352], in_=CumF[:].rearrange("p h kb -> p (h kb)")), reads=['dbgt'], writes=['dbgt'])
            op('sync', lambda e: e.dma_start(out=dbg[:, :], in_=dt_[:]), reads=['dbgt'], dma='dbgs')
        return finish()

    T.barrier()
    pst = {}

    def gp_j(e):
        if 'j' not in pst:
            pst['j'] = e.partition_id() % 4
        return pst['j']

    for tt in range(NT2):
        for jp in range(4):
            r0 = (jp * 4 + tt) * 1024
            T.coll(lambda e, r0=r0, jp=jp: e.collective_compute(
                "AllGather", ALU.bypass, replica_groups=[[0, 1, 2, 3], [4, 5, 6, 7]],
                ins=[ybuf[r0:r0 + 1024, :]], outs=[yslots[jp * 4096:(jp + 1) * 4096, :]]),
                reads=ybn if (tt == 0 and jp == 0) else [], writes=['ysl%d' % jp, 'ccchain'])

        def cp(e, tt=tt):
            j = gp_j(e)
            return e.dma_start(out=ymine[tt * 4096:(tt + 1) * 4096, :], in_=yslots[bass.ds(j * 4096, 4096), :])
        op('gpsimd', cp, reads=['ysl%d' % i for i in range(4)], writes=['ym%d' % tt], dma='ymc')

    A = Arena(P0)
    aT = A.alloc([P, NFC, TT2], BF16, "aT")
    zs = [nc.alloc_sbuf_tensor_at("zs%d" % i, [P, D], F32, offset=P0 + i * D * 4) for i in range(4)]
    actT = A.alloc([P, 32, TT2], BF16, "actT")
    ring2 = [A.alloc([P, 8192], BF16, "ring2_%d" % i) for i in range(2)]
    v16 = lambda t: t[:].rearrange("p (k n) -> p k n", n=512)
    v32 = lambda t: t[:].rearrange("p (k n) -> p k n", n=256)
    xs2 = A.alloc([P, D], F32, "xs2")
    Grow = A.alloc([P, D], F32, "Grow")
    xn2 = A.alloc([P, D], BF16, "xn2")
    scr = [A.alloc([P, 512], F32, "scr%d" % i) for i in range(2)]
    sq = A.alloc([P, 512], BF16, "sq")
    sg = [A.alloc([P, 512], F32, "sg%d" % i) for i in range(2)]
    ssq = A0.alloc([P, 32], F32, "ssq")
    actn = ['act%d' % k for k in range(32)]
    aTn = ['aT%d' % k for k in range(NFC)]
    c2 = {'ring': 0, 'sg': 0, 'scr': 0}

    def ring2_load(view, src_ap):
        i = c2['ring'] % 2
        c2['ring'] += 1
        rn = 'ring%d' % i
        dst = view(ring2[i])
        op('gpsimd', lambda e: e.dma_start(out=dst, in_=src_ap), writes=[rn], dma=rn)
        return ring2[i], rn

    NT2_RUN = NT2 if not (debug and stop_after == 3) else 1
    for tt in range(NT2_RUN):
        t0 = tt * TT2
        op('sync', lambda e, tt=tt: e.dma_start(out=actT[:], in_=ymine[tt * 4096:(tt + 1) * 4096, :].rearrange("(k p) t -> p k t", p=P)),
           reads=['ym%d' % tt], writes=actn, dma='yl')
        op('sync', lambda e: e.dma_start(out=Grow[:], in_=Gs[0:1, :].partition_broadcast(P)), reads=GsN[0], writes=['Grow'], dma='grow')
        for grp in range(2):
            bnk, bnkn = ps[grp], psn[grp]
            kks = [r * 8 + lb for r in range(4) for lb in range(grp * 4, grp * 4 + 4)]
            for n_, kk in enumerate(kks):
                op('vector', lambda e, kk=kk: e.tensor_tensor(out=sq[:], in0=actT[:, kk, :], in1=actT[:, kk, :], op=ALU.mult),
                   reads=[actn[kk]], writes=['sq'])
                op('tensor', lambda e, bnk=bnk, n_=n_: e.matmul(bnk, lhsT=ones_bf[:], rhs=sq[:], start=(n_ == 0), stop=(n_ == 15)),
                   reads=['sq', 'ones_bf'], writes=[bnkn])
            emit_rstd(scr[grp][:], bnk, 1.0 / 2048, [bnkn], scr[grp].name)
        for kk in range(32):
            grp = 0 if (kk % 8) < 4 else 1
            op('vector', lambda e, kk=kk, grp=grp: e.scalar_tensor_tensor(out=actT[:, kk, :], in0=actT[:, kk, :], scalar=gyc[:, kk:kk + 1],
                                                                         in1=scr[grp][:], op0=ALU.mult, op1=ALU.mult),
               reads=[actn[kk], scr[grp].name, 'gyc'], writes=[actn[kk]])
        for db in range(8):
            bset = (db % 2) * 4
            for kg in range(2):
                rt, rn = ring2_load(v16, w_out[kg * 2048:(kg + 1) * 2048, db * 512:(db + 1) * 512].rearrange("(k p) n -> p k n", p=P))

                def mmo(e, rt=v16(rt), kg=kg, bset=bset):
                    ins = None
                    for s4 in range(4):
                        for k in range(16):
                            ins = e.matmul(ps[bset + s4], lhsT=actT[:, kg * 16 + k, s4 * P:(s4 + 1) * P], rhs=rt[:, k, :],
                                           start=(kg == 0 and k == 0), stop=(kg == 1 and k == 15))
                    return ins
                op('tensor', mmo, reads=actn + [rn], writes=[psn[bset + s4] for s4 in range(4)])
            for s4 in range(4):
                bk, bkn = ps[bset + s4], psn[bset + s4]
                op('scalar', lambda e, bk=bk, s4=s4, db=db: e.activation(out=junk_bf[:], in_=bk, func=AF.Square, accum_out=ssq[:, s4 * 8 + db:s4 * 8 + db + 1]),
                   reads=[bkn], writes=['junk_bf', 'ssq%d_%d' % (s4, db)])
                op('vector', lambda e, bk=bk, s4=s4, db=db: e.tensor_copy(out=zs[s4][:, db * 512:(db + 1) * 512], in_=bk),
                   reads=[bkn], writes=['zs%d_%d' % (s4, db)])
        for s4 in range(4):
            r0 = t0 + s4 * P
            op('sync', lambda e, r0=r0: e.dma_start(out=xs2[:], in_=x_chunk[r0:r0 + P, :]), writes=['xs2'], dma='xs2')
            ssn = ['ssq%d_%d' % (s4, db) for db in range(8)]
            st1 = stat[:, s4:s4 + 1]
            op('vector', lambda e, s4=s4, st1=st1: e.tensor_reduce(out=st1, in_=ssq[:, s4 * 8:(s4 + 1) * 8], axis=mybir.AxisListType.X, op=ALU.add),
               reads=ssn, writes=['st1_%d' % s4])
            emit_rstd(st1, st1, 1.0 / D, ['st1_%d' % s4], 'st1_%d' % s4)
            zn = ['zs%d_%d' % (s4, db) for db in range(8)]
            op('vector', lambda e, s4=s4, st1=st1: e.scalar_tensor_tensor(out=zs[s4][:], in0=zs[s4][:], scalar=st1, in1=Grow[:], op0=ALU.mult, op1=ALU.mult),
               reads=zn + ['st1_%d' % s4, 'Grow'], writes=['zs%d' % s4])
            op('vector', lambda e, s4=s4: e.tensor_tensor(out=xs2[:], in0=xs2[:], in1=zs[s4][:], op=ALU.add),
               reads=['xs2', 'zs%d' % s4], writes=['xs2'])
            op('sync', lambda e, r0=r0: e.dma_start(out=x1s[r0:r0 + P, :], in_=xs2[:]), reads=['xs2'], writes=['x1s_%d' % (tt * 4 + s4)], dma='x1st')
            st2 = stat[:, 8 + s4:9 + s4]
            op('scalar', lambda e, st2=st2: e.activation(out=xn2[:], in_=xs2[:], func=AF.Square, accum_out=st2),
               reads=['xs2'], writes=['xn2', 'st2_%d' % s4])
            emit_rstd(st2, st2, 1.0 / D, ['st2_%d' % s4], 'st2_%d' % s4)
            op('vector', lambda e, st2=st2: e.tensor_scalar(out=xn2[:], in0=xs2[:], scalar1=st2, scalar2=None, op0=ALU.mult),
               reads=['xs2', 'st2_%d' % s4], writes=['xn2'])
            for g4 in range(4):
                half = g4 % 2
                tv = tbf[:, half * 1024:(half + 1) * 1024]
                bn = psn[half]

                def tp2(e, g4=g4, tv=tv):
                    ins = None
                    for kk in range(8):
                        k = g4 * 8 + kk
                        ins = e.transpose(out=tv[:, kk * P:(kk + 1) * P], in_=xn2[:, k * P:(k + 1) * P], identity=ident_bf[:])
                    return ins
                op('tensor', tp2, reads=['xn2'], writes=[bn])

                def ev2a(e, g4=g4, tv=tv, s4=s4):
                    ins = None
                    for kk in range(8):
                        k = g4 * 8 + kk
                        ins = e.activation(out=actT[:, k, s4 * P:(s4 + 1) * P], in_=tv[:, kk * P:(kk + 1) * P], func=AF.Identity,
                                           scale=gain2c[:, k:k + 1], bias=shift2c[:, k:k + 1])
                    return ins

                def ev2v(e, g4=g4, tv=tv, s4=s4):
                    ins = None
                    for kk in range(8):
                        k = g4 * 8 + kk
                        ins = e.tensor_scalar(out=actT[:, k, s4 * P:(s4 + 1) * P], in0=tv[:, kk * P:(kk + 1) * P],
                                              scalar1=gain2c[:, k:k + 1], scalar2=shift2c[:, k:k + 1], op0=ALU.mult, op1=ALU.add)
                    return ins
                if g4 % 2 == 0:
                    op('scalar', ev2a, reads=[bn, 'gain2c'] + shift2n, writes=[actn[g4 * 8 + kk] for kk in range(8)])
                else:
                    op('vector', ev2v, reads=[bn, 'gain2c'] + shift2n, writes=[actn[g4 * 8 + kk] for kk in range(8)])
        op('sync', lambda e: e.dma_start(out=Grow[:], in_=Gs[1:2, :].partition_broadcast(P)), reads=GsN[1], writes=['Grow'], dma='grow')
        for fb in range(NFC // 2):
            f0 = fb * 256
            rg, rgn = ring2_load(v32, w_gate[:, f0:f0 + 256].rearrange("(k p) n -> p k n", p=P))
            ru, run = ring2_load(v32, w_up[:, f0:f0 + 256].rearrange("(k p) n -> p k n", p=P))
            rgv = v32(rg)
            ruv = v32(ru)
            bset = (fb % 2) * 4

            def mmg(e, rv=rgv, bset=bset, o=0):
                ins = None
                for cch in range(2):
                    for k in range(32):
                        ins = e.matmul(ps[bset + o + cch], lhsT=rv[:, k, cch * P:(cch + 1) * P], rhs=actT[:, k, :], start=(k == 0), stop=(k == 31))
                return ins
            op('tensor', mmg, reads=actn + [rgn], writes=[psn[bset], psn[bset + 1]])
            op('tensor', lambda e, rv=ruv, bset=bset: mmg(e, rv, bset, 2), reads=actn + [run], writes=[psn[bset + 2], psn[bset + 3]])
            for cch in range(2):
                fc = fb * 2 + cch
                si = c2['sg'] % 2
                c2['sg'] += 1
                s_t = sg[si]
                bg, bu = ps[bset + cch], ps[bset + 2 + cch]
                op('scalar', lambda e, s_t=s_t, bg=bg: e.activation(out=s_t[:], in_=bg, func=AF.Silu), reads=[psn[bset + cch]], writes=[s_t.name])
                op('vector', lambda e, s_t=s_t, bu=bu, fc=fc: e.tensor_tensor(out=aT[:, fc, :], in0=bu, in1=s_t[:], op=ALU.mult),
                   reads=[psn[bset + 2 + cch], s_t.name], writes=[aTn[fc]])
        for db in range(8):
            bset = (db % 2) * 4
            ngr = (NFC + 15) // 16
            for kg in range(ngr):
                nk = min(16, NFC - kg * 16)
                rt, rn = ring2_load(lambda t, nk=nk: v16(t)[:, 0:nk, :],
                                    w_down[kg * 2048:kg * 2048 + nk * P, db * 512:(db + 1) * 512].rearrange("(k p) n -> p k n", p=P))

                def mmd(e, rt=v16(rt), kg=kg, nk=nk, bset=bset, ngr=ngr):
                    ins = None
                    for s4 in range(4):
                        for k in range(nk):
                            ins = e.matmul(ps[bset + s4], lhsT=aT[:, kg * 16 + k, s4 * P:(s4 + 1) * P], rhs=rt[:, k, :],
                                           start=(kg == 0 and k == 0), stop=(kg == ngr - 1 and k == nk - 1))
                    return ins
                op('tensor', mmd, reads=aTn[kg * 16:kg * 16 + nk] + [rn], writes=[psn[bset + s4] for s4 in range(4)])
            for s4 in range(4):
                bk, bkn = ps[bset + s4], psn[bset + s4]
                op('scalar', lambda e, bk=bk, s4=s4, db=db: e.activation(out=junk_bf[:], in_=bk, func=AF.Square, accum_out=ssq[:, s4 * 8 + db:s4 * 8 + db + 1]),
                   reads=[bkn], writes=['junk_bf', 'ssq%d_%d' % (s4, db)])
                ci = c2['scr'] % 2
                c2['scr'] += 1
                f_t = scr[ci]
                op('vector', lambda e, bk=bk, f_t=f_t, db=db: e.tensor_tensor(out=f_t[:], in0=bk, in1=Grow[:, db * 512:(db + 1) * 512], op=ALU.mult),
                   reads=[bkn, 'Grow'], writes=[f_t.name])
                r0 = t0 + s4 * P
                op('sync', lambda e, f_t=f_t, r0=r0, db=db: e.dma_start(out=accs[r0:r0 + P, db * 512:(db + 1) * 512], in_=f_t[:]),
                   reads=[f_t.name], writes=['acc_%d_%d' % (tt * 4 + s4, db)], dma=f_t.name + 'st')
        for s4 in range(4):
            r0 = t0 + s4 * P
            a_t = zs[s4 % 2]
            an = 'zs%d' % (s4 % 2)
            op('sync', lambda e, a_t=a_t, r0=r0: e.dma_start(out=a_t[:], in_=accs[r0:r0 + P, :]),
               reads=['acc_%d_%d' % (tt * 4 + s4, db) for db in range(8)], writes=[an] + ['zs%d_%d' % (s4 % 2, db) for db in range(8)], dma=an + 'ld')
            op('sync', lambda e, r0=r0: e.dma_start(out=xs2[:], in_=x1s[r0:r0 + P, :]), reads=['x1s_%d' % (tt * 4 + s4)], writes=['xs2'], dma='xs2')
            ssn = ['ssq%d_%d' % (s4, db) for db in range(8)]
            st3 = stat[:, 16 + s4:17 + s4]
            op('vector', lambda e, s4=s4, st3=st3: e.tensor_reduce(out=st3, in_=ssq[:, s4 * 8:(s4 + 1) * 8], axis=mybir.AxisListType.X, op=ALU.add),
               reads=ssn, writes=['st3_%d' % s4])
            emit_rstd(st3, st3, 1.0 / D, ['st3_%d' % s4], 'st3_%d' % s4)
            op('vector', lambda e, a_t=a_t, st3=st3: e.scalar_tensor_tensor(out=xs2[:], in0=a_t[:], scalar=st3, in1=xs2[:], op0=ALU.mult, op1=ALU.add),
               reads=[an, 'xs2', 'st3_%d' % s4] + ['zs%d_%d' % (s4 % 2, db) for db in range(8)], writes=['xs2'])
            op('sync', lambda e, r0=r0: e.dma_start(out=out[r0:r0 + P, :], in_=xs2[:]), reads=['xs2'], writes=['out_%d' % (tt * 4 + s4)], dma='outst')

    if debug and stop_after == 3:
        pass
    return finish()


def col_layout(v):
    return np.ascontiguousarray(np.asarray(v, dtype=np.float32).reshape(-1, P).T)


def make_in_maps(inputs):
    x = np.asarray(inputs["x"], dtype=np.float32)
    c = np.asarray(inputs["c"], dtype=np.float32)
    w_in = np.asarray(inputs["w_in"], dtype=np.float32)[0]
    w_out = np.asarray(inputs["w_out"], dtype=np.float32)[0]
    conv_w = np.asarray(inputs["conv_w"], dtype=np.float32)[0]
    b_f = np.asarray(inputs["b_f"], dtype=np.float32)[0]
    aon = np.asarray(inputs["attn_out_norm"], dtype=np.float32)[0]
    con = np.asarray(inputs["conv_out_norm"], dtype=np.float32)[0]
    shared = {
        "w_ada": np.ascontiguousarray(np.asarray(inputs["w_ada"], dtype=np.float32)[0]),
        "b_ada": np.ascontiguousarray(np.asarray(inputs["b_ada"], dtype=np.float32)[0][None, :]),
        "pre1_col": col_layout(inputs["pre_norm_mix"][0]),
        "pre2_col": col_layout(inputs["pre_norm_ffn"][0]),
        "post1_row": np.ascontiguousarray(np.asarray(inputs["post_norm_mix"], dtype=np.float32)[0][None, :]),
        "post2_row": np.ascontiguousarray(np.asarray(inputs["post_norm_ffn"], dtype=np.float32)[0][None, :]),
        "w_gate": np.ascontiguousarray(np.asarray(inputs["w_gate"], dtype=np.float32)[0]),
        "w_up": np.ascontiguousarray(np.asarray(inputs["w_up"], dtype=np.float32)[0]),
        "w_down": np.ascontiguousarray(np.asarray(inputs["w_down"], dtype=np.float32)[0]),
    }
    mchunks = []
    for r in range(4):
        for lb in range(8):
            mchunks.append(4 * r + lb if lb < 4 else 16 + 4 * r + (lb - 4))
    gy_full = np.concatenate([aon, con])
    gy_perm = np.concatenate([gy_full[m * P:(m + 1) * P] for m in mchunks])
    shared["gy_col"] = col_layout(gy_perm)
    shared["w_out_perm"] = np.ascontiguousarray(np.concatenate([w_out[m * P:(m + 1) * P] for m in mchunks], axis=0))
    maps = []
    for core in range(NCORES):
        b, g = divmod(core, 4)
        m = dict(shared)
        m["x_full"] = np.ascontiguousarray(x[b])
        m["x_chunk"] = np.ascontiguousarray(x[b, g * TOK2:(g + 1) * TOK2])
        m["c_col"] = col_layout(c[b])
        sl = lambda base: w_in[:, base + 512 * g: base + 512 * g + 512]
        wq, wk, wv = sl(0), sl(2048), sl(4096)
        wf = w_in[:, 6144 + 4 * g: 6144 + 4 * g + 4]
        wgb, wgc, wu = sl(6160), sl(6160 + 2048), sl(6160 + 4096)
        m["w1"] = np.ascontiguousarray(np.concatenate([wq, wk, wgb, wgc, wu, wv, wf], axis=1))
        m["wf_col"] = np.ascontiguousarray(wf.reshape(32, P, 4).transpose(1, 0, 2).reshape(P, 128))
        m["bf_row"] = np.ascontiguousarray(b_f[4 * g:4 * g + 4][None, :])
        cw = conv_w[:, 512 * g:512 * g + 512]
        m["convw_col"] = np.ascontiguousarray(cw.reshape(3, 4, P).transpose(2, 1, 0).reshape(P, 12))
        maps.append(m)
    return maps


_NC_CACHE = {}


def kernel(**inputs):
    if "nc" not in _NC_CACHE:
        _NC_CACHE["nc"] = build_nc()
    nc = _NC_CACHE["nc"]
    in_maps = make_in_maps(inputs)
    res = run_bass_kernel_spmd(nc, in_maps, core_ids=list(range(NCORES)))
    outp = np.empty((2, S, D), dtype=np.float32)
    for core in range(NCORES):
        b, g = divmod(core, 4)
        outp[b, g * TOK2:(g + 1) * TOK2] = np.asarray(res.results[core]["out"])
    return outp
```

```python
import os
import numpy as np
import concourse.bass as bass
import concourse.mybir as mybir
from concourse.bass_utils import run_bass_kernel_spmd

F32 = mybir.dt.float32
BF16 = mybir.dt.bfloat16
AF = mybir.ActivationFunctionType
ALU = mybir.AluOpType

D = 4096
S = 8192
NCORES = 8
DFF = 11008
NFC = DFF // 128
EPS = 1e-6
P = 128
TT1 = 1024
NT1 = S // TT1
TT2 = 512
TOK2 = 2048
NT2 = TOK2 // TT2
STAGE_BF16 = False
NW1 = 3072 + 4

ENG = ['sync', 'scalar', 'vector', 'gpsimd', 'tensor']
COMPUTE = ['scalar', 'vector', 'gpsimd', 'tensor']


class Tr:
    def __init__(s, nc):
        s.nc = nc
        s.prog = {e: [] for e in ENG}
        s.esem = {e: [nc.alloc_semaphore("prog_" + e), 0] for e in COMPUTE}
        s.dsem = {}
        s.res = {}
        s.waited = {e: {} for e in ENG}

    def _handle(s, semkey):
        kind, k = semkey
        return s.esem[k][0] if kind == 'e' else s.dsem[k][0]

    def _need(s, e, toks):
        best = {}
        for (semkey, v) in toks:
            if semkey == ('e', 'tensor') and e == 'tensor':
                continue
            if v > best.get(semkey, 0):
                best[semkey] = v
        for semkey, v in best.items():
            if s.waited[e].get(semkey, 0) >= v:
                continue
            s.waited[e][semkey] = v
            h = s._handle(semkey)
            s.prog[e].append(lambda eng, h=h, v=v: eng.wait_ge(h, v))

    def op(s, e, fn, reads=(), writes=(), dma=None):
        writes = list(writes) + [r for r in reads if r.startswith('bk') and r not in writes]
        toks = []
        for r in reads:
            st = s.res.get(r)
            if st and st[0]:
                toks.append(st[0])
        for w in writes:
            st = s.res.get(w)
            if st:
                if st[0]:
                    toks.append(st[0])
                toks.extend(st[1].items())
        s._need(e, toks)
        if dma is not None:
            if dma not in s.dsem:
                s.dsem[dma] = [s.nc.alloc_semaphore("d_" + str(dma)), 0]
            d = s.dsem[dma]
            d[1] += 16
            tok = (('d', dma), d[1])
            h, inc = d[0], 16
        else:
            d = s.esem[e]
            d[1] += 1
            tok = (('e', e), d[1])
            h, inc = d[0], 1
        s.prog[e].append(lambda eng, fn=fn, h=h, inc=inc: fn(eng).then_inc(h, inc))
        for w in writes:
            s.res[w] = [tok, {}]
        for r in reads:
            st = s.res.setdefault(r, [None, {}])
            if tok[1] > st[1].get(tok[0], 0):
                st[1][tok[0]] = tok[1]
        return tok

    def barrier(s):
        toks = [(('e', f), s.esem[f][1]) for f in COMPUTE if s.esem[f][1] > 0]
        toks += [(('d', k), v[1]) for k, v in s.dsem.items()]
        for e in ENG:
            for semkey, v in toks:
                if s.waited[e].get(semkey, 0) >= v:
                    continue
                s.waited[e][semkey] = v
                h = s._handle(semkey)
                s.prog[e].append(lambda eng, h=h, v=v: eng.wait_ge(h, v))
        s.res = {}

    def coll(s, fn, reads=(), writes=()):
        e = 'gpsimd'
        toks = []
        for r in reads:
            st = s.res.get(r)
            if st and st[0]:
                toks.append(st[0])
        for w in writes:
            st = s.res.get(w)
            if st:
                if st[0]:
                    toks.append(st[0])
                toks.extend(st[1].items())
        s._need(e, toks)
        if 'cc' not in s.dsem:
            s.dsem['cc'] = [s.nc.alloc_semaphore("cc"), 0]
        d = s.dsem['cc']
        d[1] += 1
        tok = (('d', 'cc'), d[1])
        h = d[0]
        s.prog[e].append(lambda eng: fn(eng).then_inc(h))
        for w in writes:
            s.res[w] = [tok, {}]
        return tok

    def emit(s, block):
        def mk(e):
            def body(eng):
                for fn in s.prog[e]:
                    fn(eng)
            return body
        block.sync(mk('sync'))
        block.scalar(mk('scalar'))
        block.vector(mk('vector'))
        block.gpsimd(mk('gpsimd'))
        block.tensor(mk('tensor'))


def build_nc(stop_after=9, debug=False):
    nc = bass.Bass("TRN2", target_bir_lowering=False)

    def din(name, shape, dt=F32):
        return nc.dram_tensor(name, shape, dt, kind="ExternalInput").ap()

    x_full = din("x_full", [S, D])
    x_chunk = din("x_chunk", [TOK2, D])
    c_col_in = din("c_col", [P, 32])
    w_ada = din("w_ada", [D, 6 * D])
    b_ada = din("b_ada", [1, 6 * D])
    pre1_in = din("pre1_col", [P, 32])
    pre2_in = din("pre2_col", [P, 32])
    post1_in = din("post1_row", [1, D])
    post2_in = din("post2_row", [1, D])
    w1 = din("w1", [D, NW1])
    bf_in = din("bf_row", [1, 4])
    wf_in = din("wf_col", [P, 128])
    convw_in = din("convw_col", [P, 12])
    gy_in = din("gy_col", [P, 32])
    w_out = din("w_out_perm", [D, D])
    w_gate = din("w_gate", [D, DFF])
    w_up = din("w_up", [D, DFF])
    w_down = din("w_down", [DFF, D])
    out = nc.dram_tensor("out", [TOK2, D], F32, kind="ExternalOutput").ap()
    if debug:
        dbg = nc.dram_tensor("dbg", [P, 8192], F32, kind="ExternalOutput").ap()

    Gs = nc.dram_tensor("Gs", [2, D], F32)
    qTs = nc.dram_tensor("qTs", [4 * P, S], BF16)
    kTs = nc.dram_tensor("kTs", [4 * P, S], BF16)
    Vs = nc.dram_tensor("Vs", [S, 512], BF16)
    ybuf = nc.dram_tensor("ybuf", [16384, 512], BF16)
    yslots = nc.dram_tensor("yslots", [16384, 512], BF16)
    ymine = nc.dram_tensor("ymine", [16384, 512], BF16)
    wo_s = nc.dram_tensor("wo_s", [16 * P, 8192], BF16)
    wgu_s = nc.dram_tensor("wgu_s", [86 * P, 8192], BF16)
    wd_s = nc.dram_tensor("wd_s", [48 * P, 8192], BF16)
    x1s = nc.dram_tensor("x1s", [TOK2, D], F32)
    accs = nc.dram_tensor("accs", [TOK2, D], F32)

    BASE = 16512
    LIMIT = 229344

    class Arena:
        def __init__(self, start):
            self.off = start
            self.n = 0

        def alloc(self, shape, dt, name=None):
            nbytes = int(np.prod(shape[1:])) * (4 if dt == F32 else 2)
            nbytes = (nbytes + 63) // 64 * 64
            self.n += 1
            t = nc.alloc_sbuf_tensor_at(name or ("t%d_%d" % (self.off, self.n)), list(shape), dt, offset=self.off)
            self.off += nbytes
            assert self.off <= LIMIT, (self.off, LIMIT)
            return t

    A0 = Arena(BASE)
    ident_bf = A0.alloc([P, P], BF16, "ident_bf")
    ident_f = A0.alloc([P, P], F32, "ident_f")
    tri_bf = A0.alloc([P, P], BF16, "tri_bf")
    tri_f = A0.alloc([P, P], F32, "tri_f")
    ones_bf = A0.alloc([P, P], BF16, "ones_bf")
    ones_f = A0.alloc([P, P], F32, "ones_f")
    gain1c = A0.alloc([P, 32], F32, "gain1c")
    shift1c = A0.alloc([P, 32], F32, "shift1c")
    gain2c = A0.alloc([P, 32], F32, "gain2c")
    shift2c = A0.alloc([P, 32], F32, "shift2c")
    sc1c = A0.alloc([P, 32], F32, "sc1c")
    sc2c = A0.alloc([P, 32], F32, "sc2c")
    pre1c = A0.alloc([P, 32], F32, "pre1c")
    pre2c = A0.alloc([P, 32], F32, "pre2c")
    gyc = A0.alloc([P, 32], F32, "gyc")
    ccol = A0.alloc([P, 32], F32, "ccol")
    cact = A0.alloc([P, 32], F32, "cact")
    SP = A0.alloc([P, 256], F32, "SP")
    wf_sb = A0.alloc([P, 32, 4], BF16, "wf_sb")
    bfrep = A0.alloc([P, 32], F32, "bfrep")
    convw = A0.alloc([P, 12], F32, "convw")
    halo = A0.alloc([P, 4, 2], F32, "halo")
    stat = A0.alloc([P, 64], F32, "stat")
    junk_bf = A0.alloc([P, 512], BF16, "junk_bf")
    junk_f = A0.alloc([P, P], F32, "junk_f")
    P0 = BASE + 8192
    assert A0.off <= P0, A0.off

    pp = [nc.alloc_psum_tensor("pp%d" % i, [P, 1024], F32) for i in range(4)]
    ps = [pp[i // 2][:, (i % 2) * 512:(i % 2 + 1) * 512] for i in range(8)]
    psn = ['bk%d' % i for i in range(8)]

    T = Tr(nc)
    op = T.op

    def mk_const(t, val, cmp_op=None):
        op('gpsimd', lambda e: e.memset(t[:], val), writes=[t.name])
        if cmp_op is not None:
            op('gpsimd', lambda e: e.affine_select(out=t[:], in_=t[:], pattern=[[1, P]], compare_op=cmp_op,
                                                   fill=0.0, base=0, channel_multiplier=-1),
               reads=[t.name], writes=[t.name])

    mk_const(ident_bf, 1.0, ALU.is_equal)
    mk_const(ident_f, 1.0, ALU.is_equal)
    mk_const(tri_bf, 1.0, ALU.is_ge)
    mk_const(tri_f, 1.0, ALU.is_ge)
    mk_const(ones_bf, 1.0)
    mk_const(ones_f, 1.0)
    op('gpsimd', lambda e: e.memset(halo[:], 0.0), writes=['halo'])

    def small_load(dst, src, key, name):
        op('sync', lambda e: e.dma_start(out=dst, in_=src), writes=[name], dma=key)

    small_load(ccol[:], c_col_in[:, :], 'm0', 'ccol')
    small_load(pre1c[:], pre1_in[:, :], 'm1', 'pre1c')
    small_load(pre2c[:], pre2_in[:, :], 'm2', 'pre2c')
    small_load(gyc[:], gy_in[:, :], 'm3', 'gyc')
    small_load(convw[:], convw_in[:, :], 'm4', 'convw')
    for s8 in range(8):
        small_load(bfrep[:, s8 * 4:(s8 + 1) * 4], bf_in[0:1, :].partition_broadcast(P), 'm5', 'bfrep%d' % s8)
    op('gpsimd', lambda e: e.dma_start(out=wf_sb[:].rearrange("p k n -> p (k n)"), in_=wf_in[:, :]), writes=['wf_sb'], dma='m6')

    def emit_rstd(dst, src, mul, rname, wname):
        op('vector', lambda e: e.tensor_scalar(out=dst, in0=src, scalar1=mul, scalar2=EPS, op0=ALU.mult, op1=ALU.add),
           reads=rname, writes=[wname])
        op('scalar', lambda e: e.activation(out=dst, in_=dst, func=AF.Sqrt), reads=[wname], writes=[wname])
        op('vector', lambda e: e.reciprocal(out=dst, in_=dst), reads=[wname], writes=[wname])

    A = Arena(P0)
    cB = A.alloc([P, 32, P], BF16, "cB")
    ring0 = [A.alloc([P, 16, 512], BF16, "ring0_%d" % i) for i in range(3)]
    bt = [A.alloc([P, 512], F32, "bt%d" % i) for i in range(2)]
    postr = [A.alloc([P, 512], F32, "postr%d" % i) for i in range(2)]
    modblk = [A.alloc([P, 512], F32, "modblk%d" % i) for i in range(2)]
    gblk = [A.alloc([P, 512], F32, "gblk%d" % i) for i in range(2)]

    op('scalar', lambda e: e.activation(out=cact[:], in_=ccol[:], func=AF.Silu), reads=['ccol'], writes=['cact'])
    for k in range(32):
        op('vector', lambda e, k=k: e.tensor_scalar(out=cB[:, k, :], in0=ones_bf[:], scalar1=cact[:, k:k + 1],
                                                    scalar2=None, op0=ALU.mult),
           reads=['cact', 'ones_bf'], writes=['cB%d' % k])
    cBn = ['cB%d' % k for k in range(32)]

    rcount = [0]

    cv_jobs = []
    for db in range(8):
        for kg in range(2):
            blk = db * 2 + kg
            cv_jobs.append((wo_s[blk * P:(blk + 1) * P, :].rearrange("p (k n) -> p k n", n=512),
                            w_out[kg * 2048:(kg + 1) * 2048, db * 512:(db + 1) * 512].rearrange("(k p) n -> p k n", p=P)))
    for fb in range(NFC // 2):
        for wi, wsrc in enumerate((w_gate, w_up)):
            blk = fb * 2 + wi
            cv_jobs.append((wgu_s[blk * P:(blk + 1) * P, :].rearrange("p (k n) -> p k n", n=256),
                            wsrc[:, fb * 256:(fb + 1) * 256].rearrange("(k p) n -> p k n", p=P)))
    for db in range(8):
        for kg in range(6):
            nk = min(16, NFC - kg * 16)
            blk = db * 6 + kg
            cv_jobs.append((wd_s[blk * P:(blk + 1) * P, 0:nk * 512].rearrange("p (k n) -> p k n", n=512),
                            w_down[kg * 2048:kg * 2048 + nk * P, db * 512:(db + 1) * 512].rearrange("(k p) n -> p k n", p=P)))
    cv_pos = [0]

    def cv_issue(n=1):
        if not STAGE_BF16:
            return
        for _ in range(n):
            if cv_pos[0] >= len(cv_jobs):
                return
            dst, src = cv_jobs[cv_pos[0]]
            key = 'cv%d' % (cv_pos[0] % 4)
            cv_pos[0] += 1
            op('gpsimd', lambda e, dst=dst, src=src: e.dma_start(out=dst, in_=src), dma=key)

    def ring_load(rings, view_fn, src_ap, nslots, cv=0):
        i = rcount[0] % nslots
        rcount[0] += 1
        rn = 'ring%d' % i
        dst = view_fn(rings[i])
        op('gpsimd', lambda e: e.dma_start(out=dst, in_=src_ap), writes=[rn], dma=rn)
        cv_issue(cv)
        return rings[i], rn

    seg_cols = {0: shift1c, 1: sc1c, 3: shift2c, 4: sc2c}

    def mod_load(cb, rings, nslots):
        c0 = cb * 512
        return [ring_load(rings, lambda t: t[:], w_ada[kg * 2048:(kg + 1) * 2048, c0:c0 + 512].rearrange("(k p) n -> p k n", p=P), nslots)
                for kg in range(2)]

    def mod_block(cb, rings, nslots, cBt, pb, pbn, b_t, m_t, p_t, g_t, loaded=None):
        seg = cb // 8
        c0 = cb * 512
        cBn_ = ['cB%d' % k for k in range(32)]
        if loaded is None:
            loaded = mod_load(cb, rings, nslots)
        for kg in range(2):
            rt, rn = loaded[kg]

            def mm(e, rt=rt, kg=kg, pb=pb):
                ins = None
                for k in range(16):
                    ins = e.matmul(pb, lhsT=cBt[:, kg * 16 + k, :], rhs=rt[:, k, :],
                                   start=(kg == 0 and k == 0), stop=(kg == 1 and k == 15))
                return ins
            op('tensor', mm, reads=[rn] + cBn_, writes=[pbn])
        op('sync', lambda e, b_t=b_t, c0=c0: e.dma_start(out=b_t[:], in_=b_ada[0:1, c0:c0 + 512].partition_broadcast(P)),
           writes=[b_t.name], dma=b_t.name)
        op('vector', lambda e, m_t=m_t, pb=pb, b_t=b_t: e.tensor_tensor(out=m_t[:], in0=pb, in1=b_t[:], op=ALU.add),
           reads=[pbn, b_t.name], writes=[m_t.name])
        if seg in seg_cols:
            colt = seg_cols[seg]
            for i in range(4):
                cidx = (cb % 8) * 4 + i
                op('vector', lambda e, m_t=m_t, i=i, colt=colt, cidx=cidx: e.scalar_tensor_tensor(
                    out=junk_f[:], in0=m_t[:, i * P:(i + 1) * P], scalar=1.0, in1=ident_f[:],
                    op0=ALU.mult, op1=ALU.mult, accum_out=colt[:, cidx:cidx + 1]),
                   reads=[m_t.name], writes=['junk_f', colt.name + str(cidx)])
        else:
            which = 0 if seg == 2 else 1
            prow = post1_in if which == 0 else post2_in
            cc0 = (cb % 8) * 512
            op('sync', lambda e, p_t=p_t, prow=prow, cc0=cc0: e.dma_start(
                out=p_t[:], in_=prow[0:1, cc0:cc0 + 512].partition_broadcast(P)), writes=[p_t.name], dma=p_t.name)
            op('vector', lambda e, g_t=g_t, m_t=m_t, p_t=p_t: e.tensor_tensor(out=g_t[:], in0=m_t[:], in1=p_t[:], op=ALU.mult),
               reads=[m_t.name, p_t.name], writes=[g_t.name])
            op('sync', lambda e, g_t=g_t, which=which, cc0=cc0: e.dma_start(out=Gs[which:which + 1, cc0:cc0 + 512], in_=g_t[0:1, :]),
               reads=[g_t.name], writes=['Gs%d_%d' % (which, cb % 8)], dma=g_t.name + 'st')

    for cb in range(16):
        mod_block(cb, ring0, 3, cB, ps[cb % 2], psn[cb % 2], bt[cb % 2], modblk[cb % 2], postr[cb % 2], gblk[cb % 2])
    allc = lambda t: [t.name + str(i) for i in range(32)]
    op('vector', lambda e: e.scalar_tensor_tensor(out=gain1c[:], in0=sc1c[:], scalar=1.0, in1=pre1c[:], op0=ALU.add, op1=ALU.mult),
       reads=allc(sc1c) + ['pre1c'], writes=['gain1c'])
    shift1n = allc(shift1c)
    shift2n = allc(shift2c)
    GsN = [['Gs%d_%d' % (w, i) for i in range(8)] for w in range(2)]

    def dbg_dump(items, reads):
        dt_ = nc.alloc_sbuf_tensor_at("dbgt", [P, 8192], F32, offset=LIMIT - 8192 * 4 - 64)
        op('vector', lambda e: e.memset(dt_[:], 0.0), writes=['dbgt'])
        for (off, n, src) in items:
            op('vector', lambda e, off=off, n=n, src=src: e.tensor_copy(out=dt_[:, off:off + n], in_=src),
               reads=list(reads) + ['dbgt'], writes=['dbgt'])
        op('sync', lambda e: e.dma_start(out=dbg[:, :], in_=dt_[:]), reads=['dbgt'], dma='dbgs')

    def finish():
        T.barrier()
        with nc.Block() as block:
            T.emit(block)
        return nc

    if stop_after == 0:
        if debug:
            dbg_dump([(0, 32, gain1c[:]), (32, 32, shift1c[:]), (64, 32, gain2c[:]), (96, 32, shift2c[:])],
                     ['gain1c', 'gain2c'] + shift1n + shift2n)
        return finish()

    T.barrier()
    A = Arena(P0)
    hT = A.alloc([P, 32, TT1], BF16, "hT")
    ring1 = [A.alloc([P, 32, 256], BF16, "ring1_%d" % i) for i in range(3)]
    xs = [A.alloc([P, D], F32, "xs%d" % i) for i in range(2)]
    xn = [A.alloc([P, D], BF16, "xn%d" % i) for i in range(2)]
    qst = [A.alloc([P, TT1], BF16, "qst%d" % i) for i in range(2)]
    vst = A.alloc([P, 8, 256], BF16, "vst")
    gb_sb = A.alloc([P, 2, TT1], F32, "gb_sb")
    gc_sb = A.alloc([P, 2, TT1], F32, "gc_sb")
    vbuf = A.alloc([P, 2, TT1 + 2], F32, "vbuf")
    ysb = [A.alloc([P, TT1], BF16, "ysb%d" % i) for i in range(2)]
    fsm = A0.alloc([P, 32], F32, "fsm")

    hTn = ['hT%d' % k for k in range(32)]
    ybn = []
    tbf = pp[0][:].bitcast(BF16)
    QSCALE = 1.0 / float(np.sqrt(128.0))
    blocks = [('q', 0, 0), ('q', 256, 1), ('k', 512, 0), ('k', 768, 1),
              ('gb', 1024, 0), ('gc', 1536, 0), ('u', 2048, 0),
              ('gb', 1280, 1), ('gc', 1792, 1), ('u', 2304, 1),
              ('v', 2560, 0), ('v', 2816, 1)]
    cnt = {'xs': 0, 'st': 0, 'pair': 0, 'bank': 0, 'qst': 0, 'ysb': 0, 'ev': 0}
    NT1_RUN = NT1 if stop_after != 1 or not debug else 2
    if int(os.environ.get('K_CUT', '99')) == 0:
        NT1_RUN = 0

    for ti in range(NT1_RUN):
        for s8 in range(8):
            i = cnt['xs'] % 2
            cnt['xs'] += 1
            xt, xnt = xs[i], xn[i]
            r0 = ti * TT1 + s8 * P
            op('sync', lambda e, xt=xt, r0=r0: e.dma_start(out=xt[:], in_=x_full[r0:r0 + P, :]), writes=[xt.name], dma=xt.name)
            sc = cnt['st'] % 32
            cnt['st'] += 1
            stc = stat[:, sc:sc + 1]
            stn = 'stat%d' % sc
            op('scalar', lambda e, xt=xt, xnt=xnt, stc=stc: e.activation(out=xnt[:], in_=xt[:], func=AF.Square, accum_out=stc),
               reads=[xt.name], writes=[xnt.name, stn])
            emit_rstd(stc, stc, 1.0 / D, [stn], stn)
            op('vector', lambda e, xt=xt, xnt=xnt, stc=stc: e.tensor_scalar(out=xnt[:], in0=xt[:], scalar1=stc, scalar2=None, op0=ALU.mult),
               reads=[xt.name, stn], writes=[xnt.name])
            for g4 in range(4):
                half = g4 % 2
                tv = tbf[:, half * 1024:(half + 1) * 1024]
                bn = psn[half]

                def tp(e, xnt=xnt, g4=g4, tv=tv):
                    ins = None
                    for kk in range(8):
                        k = g4 * 8 + kk
                        ins = e.transpose(out=tv[:, kk * P:(kk + 1) * P], in_=xnt[:, k * P:(k + 1) * P], identity=ident_bf[:])
                    return ins
                op('tensor', tp, reads=[xnt.name], writes=[bn])

                def ev_act(e, g4=g4, tv=tv, s8=s8):
                    ins = None
                    for kk in range(8):
                        k = g4 * 8 + kk
                        ins = e.activation(out=hT[:, k, s8 * P:(s8 + 1) * P], in_=tv[:, kk * P:(kk + 1) * P], func=AF.Identity,
                                           scale=gain1c[:, k:k + 1], bias=shift1c[:, k:k + 1])
                    return ins

                def ev_dve(e, g4=g4, tv=tv, s8=s8):
                    ins = None
                    for kk in range(8):
                        k = g4 * 8 + kk
                        ins = e.tensor_scalar(out=hT[:, k, s8 * P:(s8 + 1) * P], in0=tv[:, kk * P:(kk + 1) * P],
                                              scalar1=gain1c[:, k:k + 1], scalar2=shift1c[:, k:k + 1], op0=ALU.mult, op1=ALU.add)
                    return ins
                if g4 % 2 == 0:
                    op('scalar', ev_act, reads=[bn, 'gain1c'] + shift1n, writes=[hTn[g4 * 8 + kk] for kk in range(8)])
                else:
                    op('vector', ev_dve, reads=[bn, 'gain1c'] + shift1n, writes=[hTn[g4 * 8 + kk] for kk in range(8)])

        KCUT = int(os.environ.get("K_CUT", "99"))
        if KCUT <= 1:
            continue
        def mmf(e):
            ins = None
            for s8 in range(8):
                for k in range(32):
                    ins = e.matmul(ps[7][:, s8 * 4:(s8 + 1) * 4], lhsT=hT[:, k, s8 * P:(s8 + 1) * P], rhs=wf_sb[:, k, :],
                                   start=(k == 0), stop=(k == 31))
            return ins
        op('tensor', mmf, reads=hTn + ['wf_sb'], writes=[psn[7]])
        op('vector', lambda e: e.tensor_tensor(out=fsm[:], in0=ps[7][:, 0:32], in1=bfrep[:], op=ALU.add),
           reads=[psn[7]] + ['bfrep%d' % i for i in range(8)], writes=['fsm'])
        op('scalar', lambda e: e.activation(out=fsm[:], in_=fsm[:], func=AF.Exp, scale=-1.0), reads=['fsm'], writes=['fsm'])
        op('scalar', lambda e, ti=ti: e.activation(out=SP[:, ti * 32:(ti + 1) * 32], in_=fsm[:], func=AF.Ln, bias=1.0),
           reads=['fsm'], writes=['SP%d' % ti])

        for bidx, (kind, col0, idx) in enumerate(blocks):
            if bidx >= KCUT - 2:
                break
            rt, rn = ring_load(ring1, lambda t: t[:], w1[:, col0:col0 + 256].rearrange("(k p) n -> p k n", p=P), 3, cv=1)
            if kind == 'v':
                for s8 in range(8):
                    bi = 2 + cnt['bank'] % 6
                    cnt['bank'] += 1

                    def mmv(e, rt=rt, s8=s8, bi=bi):
                        ins = None
                        for k in range(32):
                            ins = e.matmul(ps[bi][:, 0:256], lhsT=hT[:, k, s8 * P:(s8 + 1) * P], rhs=rt[:, k, :],
                                           start=(k == 0), stop=(k == 31))
                        return ins
                    op('tensor', mmv, reads=hTn + [rn], writes=[psn[bi]])
                    eng = 'scalar' if s8 % 2 == 0 else 'vector'
                    if eng == 'scalar':
                        op('scalar', lambda e, s8=s8, bi=bi: e.activation(out=vst[:, s8, :], in_=ps[bi][:, 0:256], func=AF.Copy),
                           reads=[psn[bi]], writes=['vst%d' % s8])
                    else:
                        op('vector', lambda e, s8=s8, bi=bi: e.tensor_copy(out=vst[:, s8, :], in_=ps[bi][:, 0:256]),
                           reads=[psn[bi]], writes=['vst%d' % s8])
                op('sync', lambda e, ti=ti, idx=idx: e.dma_start(
                    out=Vs[ti * TT1:(ti + 1) * TT1, idx * 256:(idx + 1) * 256].rearrange("(s p) c -> p s c", p=P), in_=vst[:]),
                   reads=['vst%d' % i for i in range(8)], writes=['Vs_%d_%d' % (ti, idx)], dma='vst')
                continue
            for cch in range(2):
                pi = 1 + cnt['pair'] % 3
                cnt['pair'] += 1
                pair = pp[pi]
                pn = [psn[2 * pi], psn[2 * pi + 1]]

                def mmq(e, rt=rt, cch=cch, pair=pair):
                    ins = None
                    for k in range(32):
                        for half in range(2):
                            ins = e.matmul(pair[:, half * 512:(half + 1) * 512], lhsT=rt[:, k, cch * P:(cch + 1) * P],
                                           rhs=hT[:, k, half * 512:(half + 1) * 512], start=(k == 0), stop=(k == 31))
                    return ins
                op('tensor', mmq, reads=hTn + [rn], writes=pn)
                if kind in ('q', 'k'):
                    hl = idx * 2 + cch
                    qi = cnt['qst'] % 2
                    cnt['qst'] += 1
                    qt = qst[qi]
                    dst = qTs if kind == 'q' else kTs
                    if kind == 'q':
                        op('scalar', lambda e, qt=qt, pair=pair: e.activation(out=qt[:], in_=pair[:], func=AF.Copy, scale=QSCALE),
                           reads=pn, writes=[qt.name])
                    else:
                        op('vector', lambda e, qt=qt, pair=pair: e.tensor_copy(out=qt[:], in_=pair[:]), reads=pn, writes=[qt.name])
                    op('sync', lambda e, qt=qt, dst=dst, hl=hl, ti=ti: e.dma_start(
                        out=dst[hl * P:(hl + 1) * P, ti * TT1:(ti + 1) * TT1], in_=qt[:]),
                       reads=[qt.name], writes=['%sTs_%d_%d' % (kind, hl, ti)], dma=qt.name)
                elif kind == 'gb':
                    op('scalar', lambda e, cch=cch, pair=pair: e.activation(out=gb_sb[:, cch, :], in_=pair[:], func=AF.Copy),
                       reads=pn, writes=['gb%d' % cch])
                elif kind == 'gc':
                    op('vector', lambda e, cch=cch, pair=pair: e.tensor_copy(out=gc_sb[:, cch, :], in_=pair[:]),
                       reads=pn, writes=['gc%d' % cch])
                else:
                    cc = idx * 2 + cch
                    vn = 'vb%d' % cch
                    hn = 'halo%d' % cc
                    gcn = 'gc%d' % cch
                    yi = cnt['ysb'] % 2
                    cnt['ysb'] += 1
                    yt = ysb[yi]
                    op('vector', lambda e, cch=cch, cc=cc: e.tensor_copy(out=vbuf[:, cch, 0:2], in_=halo[:, cc, :]),
                       reads=['halo', hn], writes=[vn])
                    op('vector', lambda e, cch=cch, pair=pair: e.tensor_tensor(out=vbuf[:, cch, 2:2 + TT1], in0=pair[:], in1=gc_sb[:, cch, :], op=ALU.mult),
                       reads=pn + [gcn, vn], writes=[vn])
                    op('vector', lambda e, cch=cch, cc=cc: e.tensor_copy(out=halo[:, cc, :], in_=vbuf[:, cch, TT1:TT1 + 2]),
                       reads=[vn], writes=[hn])
                    op('vector', lambda e, cch=cch, cc=cc: e.tensor_scalar(out=gc_sb[:, cch, :], in0=vbuf[:, cch, 2:2 + TT1],
                                                                        scalar1=convw[:, cc * 3 + 2:cc * 3 + 3], scalar2=None, op0=ALU.mult),
                       reads=[vn, 'convw'], writes=[gcn])
                    op('vector', lambda e, cch=cch, cc=cc: e.scalar_tensor_tensor(out=gc_sb[:, cch, :], in0=vbuf[:, cch, 1:1 + TT1],
                                                                               scalar=convw[:, cc * 3 + 1:cc * 3 + 2], in1=gc_sb[:, cch, :],
                                                                               op0=ALU.mult, op1=ALU.add),
                       reads=[vn, gcn], writes=[gcn])
                    op('vector', lambda e, cch=cch, cc=cc: e.scalar_tensor_tensor(out=gc_sb[:, cch, :], in0=vbuf[:, cch, 0:TT1],
                                                                               scalar=convw[:, cc * 3:cc * 3 + 1], in1=gc_sb[:, cch, :],
                                                                               op0=ALU.mult, op1=ALU.add),
                       reads=[vn, gcn], writes=[gcn])
                    op('vector', lambda e, cch=cch, yt=yt: e.tensor_tensor(out=yt[:], in0=gc_sb[:, cch, :], in1=gb_sb[:, cch, :], op=ALU.mult),
                       reads=[gcn, 'gb%d' % cch], writes=[yt.name])
                    for hf in range(2):
                        rrow = ((((ti // 2) * 4 + cc // 2) * 4 + (ti % 2) * 2 + hf) * 2 + cc % 2) * P
                        yn_ = 'yb_c_%d_%d_%d' % (ti, cc, hf)
                        ybn.append(yn_)
                        op('sync', lambda e, yt=yt, rrow=rrow, hf=hf: e.dma_start(out=ybuf[rrow:rrow + P, :], in_=yt[:, hf * 512:(hf + 1) * 512]),
                           reads=[yt.name], writes=[yn_], dma=yt.name + '_%d' % hf)

    if stop_after == 1:
        if debug:
            T.barrier()
            dt_ = nc.alloc_sbuf_tensor_at("dbgt", [P, 8192], F32, offset=LIMIT - 8192 * 4 - 64)
            op('vector', lambda e: e.memset(dt_[:], 0.0), writes=['dbgt'])
            op('vector', lambda e: e.tensor_copy(out=dt_[:, 0:256], in_=SP[:]), reads=['dbgt'], writes=['dbgt'])
            op('vector', lambda e: e.tensor_copy(out=dt_[:, 256:384], in_=hT[:, 0, 896:1024]), reads=['dbgt'], writes=['dbgt'])
            op('vector', lambda e: e.tensor_copy(out=dt_[:, 384:512], in_=hT[:, 31, 0:128]), reads=['dbgt'], writes=['dbgt'])
            qd = nc.alloc_sbuf_tensor_at("qd", [P, 4, 1024], BF16, offset=LIMIT - 8192 * 4 - 64 - 8192)
            op('sync', lambda e: e.dma_start(out=qd[:, 0, :], in_=qTs[0:P, 0:1024]), writes=['qd'], dma='dq0')
            op('sync', lambda e: e.dma_start(out=qd[:, 1, :], in_=kTs[P:2 * P, 1024:2048]), writes=['qd1'], dma='dq1')
            op('sync', lambda e: e.dma_start(out=qd[:, 2, 0:512], in_=Vs[1024:1024 + P, 0:512]), writes=['qd2'], dma='dq2')
            op('sync', lambda e: e.dma_start(out=qd[:, 3, :], in_=ybuf[5 * P:6 * P, 0:1024]), writes=['qd3'], dma='dq3')
            op('vector', lambda e: e.tensor_copy(out=dt_[:, 1024:2048], in_=qd[:, 0, :]), reads=['qd', 'dbgt'], writes=['dbgt'])
            op('vector', lambda e: e.tensor_copy(out=dt_[:, 2048:3072], in_=qd[:, 1, :]), reads=['qd1', 'dbgt'], writes=['dbgt'])
            op('vector', lambda e: e.tensor_copy(out=dt_[:, 3072:3584], in_=qd[:, 2, 0:512]), reads=['qd2', 'dbgt'], writes=['dbgt'])
            op('vector', lambda e: e.tensor_copy(out=dt_[:, 4096:5120], in_=qd[:, 3, :]), reads=['qd3', 'dbgt'], writes=['dbgt'])
            op('sync', lambda e: e.dma_start(out=dbg[:, :], in_=dt_[:]), reads=['dbgt'], dma='dbgs')
        return finish()

    T.barrier()
    A = Arena(P0)
    qT = [A.alloc([P, S], BF16, "qT%d" % i) for i in range(2)]
    kT = [A.alloc([P, S], BF16, "kT%d" % i) for i in range(2)]
    Vt = [A.alloc([P, 64, P], BF16, "Vt%d" % i) for i in range(2)]
    Bm = [A.alloc([P, 64, 64], F32, "Bm%d" % i) for i in range(2)]
    PT = [A.alloc([P, 512], BF16, "PT%d" % i) for i in range(3)]
    rl = [A.alloc([P, 512], F32, "rl%d" % i) for i in range(2)]
    ost = [A.alloc([P, 512], BF16, "ost%d" % i) for i in range(2)]
    CumH = A.alloc([P, 4, 64], F32, "CumH")
    TotH = A.alloc([P, 4, 64], F32, "TotH")
    EndH = A.alloc([P, 4, 64], F32, "EndH")
    CumF = A.alloc([P, 4, 64], F32, "CumF")
    SPn = ['SP%d' % i for i in range(NT1)]
    cB2 = A.alloc([P, 32, P], BF16, "cB2")
    ringA = [A.alloc([P, 16, 512], BF16, "ringA_%d" % i) for i in range(2)]
    bt1 = A.alloc([P, 512], F32, "bt1")
    postr1 = A.alloc([P, 512], F32, "postr1")
    modblk1 = A.alloc([P, 512], F32, "modblk1")
    gblk1 = A.alloc([P, 512], F32, "gblk1")
    for k in range(32):
        op('vector', lambda e, k=k: e.tensor_scalar(out=cB2[:, k, :], in0=ones_bf[:], scalar1=cact[:, k:k + 1],
                                                    scalar2=None, op0=ALU.mult), writes=['cB%d' % k])

    pst = {}

    def gp_j(e):
        if 'j' not in pst:
            pst['j'] = e.partition_id() % 4
        return pst['j']

    def exchange_group(g, reads):
        for jp in range(4):
            r0 = (jp * 4 + g) * 1024
            T.coll(lambda e, r0=r0, jp=jp: e.collective_compute(
                "AllGather", ALU.bypass, replica_groups=[[0, 1, 2, 3], [4, 5, 6, 7]],
                ins=[ybuf[r0:r0 + 1024, :]], outs=[yslots[jp * 4096:(jp + 1) * 4096, :]]),
                reads=reads if jp == 0 else [], writes=['ysl%d' % jp, 'ccchain'])

        def cp(e, g=g):
            j = gp_j(e)
            return e.dma_start(out=ymine[g * 4096:(g + 1) * 4096, :], in_=yslots[bass.ds(j * 4096, 4096), :])
        op('gpsimd', cp, reads=['ysl%d' % i for i in range(4)], writes=['ym%d' % g], dma='ymc')

    exchange_group(0, [])
    exchange_group(1, [])

    op('tensor', lambda e: e.matmul(ps[6][:, 0:256], lhsT=tri_f[:], rhs=SP[:], start=True, stop=True), reads=SPn + ['tri_f'], writes=[psn[6]])
    op('tensor', lambda e: e.matmul(ps[7][:, 0:256], lhsT=ones_f[:], rhs=SP[:], start=True, stop=True), reads=SPn + ['ones_f'], writes=[psn[7]])
    op('vector', lambda e: e.tensor_copy(out=CumH[:], in_=ps[6][:, 0:256].rearrange("p (kb h) -> p h kb", h=4)), reads=[psn[6]], writes=['CumH'])
    op('vector', lambda e: e.tensor_copy(out=TotH[:], in_=ps[7][:, 0:256].rearrange("p (kb h) -> p h kb", h=4)), reads=[psn[7]], writes=['TotH'])
    for h in range(4):
        op('vector', lambda e, h=h: e.tensor_tensor_scan(out=EndH[:, h, :], data0=ones_f[:, 0:64], data1=TotH[:, h, :], initial=0.0,
                                                         op0=ALU.mult, op1=ALU.add), reads=['TotH', 'ones_f'], writes=['EndH%d' % h])
    EndN = ['EndH%d' % h for h in range(4)]
    op('vector', lambda e: e.tensor_tensor(out=CumF[:], in0=CumH[:], in1=EndH[:], op=ALU.add), reads=['CumH'] + EndN, writes=['CumF'])
    op('vector', lambda e: e.tensor_tensor(out=CumF[:], in0=CumF[:], in1=TotH[:], op=ALU.subtract), reads=['CumF', 'TotH'], writes=['CumF'])

    cntb = {'s': 0, 'pt': 0, 'o': 0, 'ost': 0}
    NH_RUN = 4 if not (debug and stop_after == 2) else 1
    PTx = PT + [A.alloc([P, 512], BF16, "PT3")]
    SB = [0, 1, 6]
    LOOK = 2

    def head_setup(hl):
        sl = hl % 2
        q_t, k_t, v_t, b_t = qT[sl], kT[sl], Vt[sl], Bm[sl]
        op('sync', lambda e, q_t=q_t, hl=hl: e.dma_start(out=q_t[:], in_=qTs[hl * P:(hl + 1) * P, :]),
           reads=['qTs_%d_%d' % (hl, t) for t in range(NT1)], writes=[q_t.name], dma=q_t.name)
        op('sync', lambda e, k_t=k_t, hl=hl: e.dma_start(out=k_t[:], in_=kTs[hl * P:(hl + 1) * P, :]),
           reads=['kTs_%d_%d' % (hl, t) for t in range(NT1)], writes=[k_t.name], dma=k_t.name)
        op('sync', lambda e, v_t=v_t, hl=hl: e.dma_start(out=v_t[:], in_=Vs[:, hl * P:(hl + 1) * P].rearrange("(kb p) d -> p kb d", p=P)),
           reads=['Vs_%d_%d' % (t, hl // 2) for t in range(NT1)], writes=[v_t.name], dma=v_t.name)

        def mkB(e, b_t=b_t, hl=hl):
            ins = None
            for kb in range(64):
                ins = e.tensor_scalar(out=b_t[:, kb, :], in0=EndH[:, hl, :], scalar1=-1.0, scalar2=CumF[:, hl, kb:kb + 1],
                                      op0=ALU.mult, op1=ALU.add)
            return ins
        op('vector', mkB, reads=EndN + ['CumF'], writes=[b_t.name])

    blks = []
    for hl in range(NH_RUN):
        for qg in range(16):
            for kb in range(4 * qg + 4):
                blks.append((hl, qg, kb))
    info = {}

    def emit_S(i):
        hl, qg, kb = blks[i]
        sl = hl % 2
        q_t, k_t = qT[sl], kT[sl]
        c0 = max(0, kb - 4 * qg) * P
        bi = SB[i % 3]
        bS, bSn = ps[bi], psn[bi]
        op('tensor', lambda e, bS=bS, c0=c0, k_t=k_t, q_t=q_t, kb=kb, qg=qg: e.matmul(
            bS[:, c0:512], lhsT=k_t[:, kb * P:(kb + 1) * P], rhs=q_t[:, qg * 512 + c0:(qg + 1) * 512], start=True, stop=True),
           reads=[k_t.name, q_t.name], writes=[bSn])
        info[i] = (bS, bSn)

    head_setup(0)
    if NH_RUN > 1:
        head_setup(1)
    for i in range(min(LOOK, len(blks))):
        emit_S(i)
    mb_next = [16]
    mb_loaded = [None]
    i_ex2 = 2 * 544
    trig = set()
    if NH_RUN == 4:
        ra = list(range(600, i_ex2 - 20))
        rb = list(range(i_ex2 + 300, len(blks) - 40))
        na = 12
        nb_ = 20
        trig = set(ra[(t * len(ra)) // na] for t in range(na)) | set(rb[(t * len(rb)) // nb_] for t in range(nb_))
    for i, (hl, qg, kb) in enumerate(blks):
        sl = hl % 2
        v_t, b_t = Vt[sl], Bm[sl]
        nkb = 4 * qg + 4
        if i == 560 and NH_RUN == 4:
            mb_loaded[0] = mod_load(mb_next[0], ringA, 2)
        if i in trig and mb_next[0] < 48:
            mod_block(mb_next[0], ringA, 2, cB2, ps[7], psn[7], bt1, modblk1, postr1, gblk1, loaded=mb_loaded[0])
            mb_next[0] += 1
            cv_issue(2)
            mb_loaded[0] = mod_load(mb_next[0], ringA, 2) if mb_next[0] < 48 else None
        if NH_RUN == 4 and hl == 2 and qg == 0 and kb == 0:
            exchange_group(2, [n_ for n_ in ybn if n_.startswith('yb_a_0_') or n_.startswith('yb_a_1_')])
        if qg == 0 and kb == 0 and hl >= 1 and hl + 1 < NH_RUN:
            head_setup(hl + 1)
        if i + LOOK < len(blks):
            emit_S(i + LOOK)
        if kb == 0:
            oi = cntb['o'] % 2
            cntb['o'] += 1
            info['o'] = (ps[2 + oi], ps[4 + oi], psn[2 + oi], psn[4 + oi])
        bO, bL, bOn, bLn = info['o']
        bS, bSn = info.pop(i)
        j0 = max(0, kb - 4 * qg)
        c0 = j0 * P
        p_t = PTx[i % 4]

        def ex(e, bS=bS, p_t=p_t, b_t=b_t, j0=j0, kb=kb, qg=qg):
            ins = None
            for j in range(j0, 4):
                ins = e.activation(out=p_t[:, j * P:(j + 1) * P], in_=bS[:, j * P:(j + 1) * P], func=AF.Exp,
                                   bias=b_t[:, kb, 4 * qg + j:4 * qg + j + 1], scale=1.0)
            return ins
        op('scalar', ex, reads=[bSn, b_t.name], writes=[p_t.name])
        if kb >= 4 * qg:
            op('vector', lambda e, p_t=p_t, c0=c0: e.tensor_tensor(out=p_t[:, c0:c0 + P], in0=p_t[:, c0:c0 + P], in1=tri_bf[:], op=ALU.mult),
               reads=[p_t.name, 'tri_bf'], writes=[p_t.name])

        def pv(e, bO=bO, bL=bL, p_t=p_t, v_t=v_t, kb=kb, c0=c0, nkb=nkb):
            e.matmul(bO[:, c0:512], lhsT=v_t[:, kb, :], rhs=p_t[:, c0:512], start=(kb == 0), stop=(kb == nkb - 1))
            return e.matmul(bL[:, c0:512], lhsT=ones_bf[:], rhs=p_t[:, c0:512], start=(kb == 0), stop=(kb == nkb - 1))
        op('tensor', pv, reads=[p_t.name, v_t.name, 'ones_bf'], writes=[bOn, bLn])
        if kb == nkb - 1:
            oi2 = cntb['ost'] % 2
            cntb['ost'] += 1
            r_t, o_t = rl[oi2], ost[oi2]
            op('vector', lambda e, r_t=r_t, bL=bL: e.reciprocal(out=r_t[:], in_=bL), reads=[bLn], writes=[r_t.name])
            op('vector', lambda e, r_t=r_t, o_t=o_t, bO=bO: e.tensor_tensor(out=o_t[:], in0=bO, in1=r_t[:], op=ALU.mult),
               reads=[bOn, r_t.name], writes=[o_t.name])
            rrow = ((((qg // 4) * 4 + 2 + hl // 2) * 4 + qg % 4) * 2 + hl % 2) * P
            yn_ = 'yb_a_%d_%d' % (hl, qg)
            ybn.append(yn_)
            op('sync', lambda e, o_t=o_t, rrow=rrow: e.dma_start(out=ybuf[rrow:rrow + P, :], in_=o_t[:]),
               reads=[o_t.name], writes=[yn_], dma=o_t.name)

    while NH_RUN == 4 and mb_next[0] < 48:
        mod_block(mb_next[0], ringA, 2, cB2, ps[7], psn[7], bt1, modblk1, postr1, gblk1, loaded=mb_loaded[0])
        mb_loaded[0] = None
        mb_next[0] += 1
    cv_issue(len(cv_jobs))
    op('vector', lambda e: e.scalar_tensor_tensor(out=gain2c[:], in0=sc2c[:], scalar=1.0, in1=pre2c[:], op0=ALU.add, op1=ALU.mult),
       reads=allc(sc2c) + ['pre2c'], writes=['gain2c'])

    if stop_after == 2:
        if debug:
            dt_ = nc.alloc_sbuf_tensor_at("dbgt", [P, 8192], F32, offset=LIMIT - 8192 * 4 - 64)
            qd = nc.alloc_sbuf_tensor_at("qd", [P, 2, 2048], BF16, offset=LIMIT - 8192 * 4 - 64 - 8192)
            T.barrier()
            op('vector', lambda e: e.memset(dt_[:], 0.0), writes=['dbgt'])
            op('sync', lambda e: e.dma_start(out=qd[:, 0, :], in_=ybuf[0:P, :]), writes=['qd'], dma='dq0')
            op('sync', lambda e: e.dma_start(out=qd[:, 1, :], in_=ybuf[24 * P:25 * P, :]), writes=['qd1'], dma='dq1')
            op('vector', lambda e: e.tensor_copy(out=dt_[:, 0:2048], in_=qd[:, 0, :]), reads=['qd', 'dbgt'], writes=['dbgt'])
            op('vector', lambda e: e.tensor_copy(out=dt_[:, 2048:4096], in_=qd[:, 1, :]), reads=['qd1', 'dbgt'], writes=['dbgt'])
            op('vector', lambda e: e.tensor_copy(out=dt_[:, 4096:4352], in_=CumF[:].rearrange("p h kb -> p (h kb)")), reads=['dbgt'], writes=['dbgt'])
            op('sync', lambda e: e.dma_start(out=dbg[:, :], in_=dt_[:]), reads=['dbgt'], dma='dbgs')
        return finish()

    exchange_group(3, [n_ for n_ in ybn if n_.startswith('yb_a_2_') or n_.startswith('yb_a_3_')])
    T.barrier()

    A = Arena(P0)
    aT = A.alloc([P, NFC, TT2], BF16, "aT")
    zs = [nc.alloc_sbuf_tensor_at("zs%d" % i, [P, D], F32, offset=P0 + i * D * 4) for i in range(4)]
    actT = A.alloc([P, 32, TT2], BF16, "actT")
    ring2 = [A.alloc([P, 8192], BF16, "ring2_%d" % i) for i in range(2)]
    v16 = lambda t: t[:].rearrange("p (k n) -> p k n", n=512)
    v32 = lambda t: t[:].rearrange("p (k n) -> p k n", n=256)
    xs2 = A.alloc([P, D], F32, "xs2")
    Grow = A.alloc([P, D], F32, "Grow")
    xn2 = A.alloc([P, D], BF16, "xn2")
    scr = [A.alloc([P, 512], F32, "scr%d" % i) for i in range(2)]
    sq = A.alloc([P, 512], BF16, "sq")
    sg = [A.alloc([P, 512], F32, "sg%d" % i) for i in range(2)]
    ssq = A0.alloc([P, 32], F32, "ssq")
    actn = ['act%d' % k for k in range(32)]
    aTn = ['aT%d' % k for k in range(NFC)]
    c2 = {'ring': 0, 'sg': 0, 'scr': 0}

    def ring2_load(view, src_ap):
        i = c2['ring'] % 2
        c2['ring'] += 1
        rn = 'ring%d' % i
        dst = view(ring2[i])
        op('gpsimd', lambda e: e.dma_start(out=dst, in_=src_ap), writes=[rn], dma=rn)
        return ring2[i], rn

    NT2_RUN = NT2 if not (debug and stop_after == 3) else 1
    for tt in range(NT2_RUN):
        t0 = tt * TT2
        for gr in range(16):
            rr0 = gr * 1024 + tt * 256
            op('sync', lambda e, gr=gr, rr0=rr0: e.dma_start(out=actT[:, gr * 2:gr * 2 + 2, :],
                                                           in_=ymine[rr0:rr0 + 256, :].rearrange("(l p) t -> p l t", p=P)),
               writes=actn[gr * 2:gr * 2 + 2], dma='yl%d' % (gr % 4))
        op('sync', lambda e: e.dma_start(out=Grow[:], in_=Gs[0:1, :].partition_broadcast(P)), reads=GsN[0], writes=['Grow'], dma='grow')
        for grp in range(2):
            bnk, bnkn = ps[grp], psn[grp]
            kks = list(range(16, 32)) if grp == 0 else list(range(0, 16))
            for n_, kk in enumerate(kks):
                op('vector', lambda e, kk=kk: e.tensor_tensor(out=sq[:], in0=actT[:, kk, :], in1=actT[:, kk, :], op=ALU.mult),
                   reads=[actn[kk]], writes=['sq'])
                op('tensor', lambda e, bnk=bnk, n_=n_: e.matmul(bnk, lhsT=ones_bf[:], rhs=sq[:], start=(n_ == 0), stop=(n_ == 15)),
                   reads=['sq', 'ones_bf'], writes=[bnkn])
            emit_rstd(scr[grp][:], bnk, 1.0 / 2048, [bnkn], scr[grp].name)
        for kk in range(32):
            grp = 0 if kk >= 16 else 1
            op('vector', lambda e, kk=kk, grp=grp: e.scalar_tensor_tensor(out=actT[:, kk, :], in0=actT[:, kk, :], scalar=gyc[:, kk:kk + 1],
                                                                         in1=scr[grp][:], op0=ALU.mult, op1=ALU.mult),
               reads=[actn[kk], scr[grp].name, 'gyc'], writes=[actn[kk]])
        for db in range(8):
            bset = (db % 2) * 4
            for kg in range(2):
                if STAGE_BF16:
                    rt, rn = ring2_load(lambda t: t[:], wo_s[(db * 2 + kg) * P:(db * 2 + kg + 1) * P, :])
                else:
                    rt, rn = ring2_load(v16, w_out[kg * 2048:(kg + 1) * 2048, db * 512:(db + 1) * 512].rearrange("(k p) n -> p k n", p=P))

                def mmo(e, rt=v16(rt), kg=kg, bset=bset):
                    ins = None
                    for s4 in range(4):
                        for k in range(16):
                            ins = e.matmul(ps[bset + s4], lhsT=actT[:, kg * 16 + k, s4 * P:(s4 + 1) * P], rhs=rt[:, k, :],
                                           start=(kg == 0 and k == 0), stop=(kg == 1 and k == 15))
                    return ins
                op('tensor', mmo, reads=actn + [rn], writes=[psn[bset + s4] for s4 in range(4)])
            for s4 in range(4):
                bk, bkn = ps[bset + s4], psn[bset + s4]
                op('scalar', lambda e, bk=bk, s4=s4, db=db: e.activation(out=junk_bf[:], in_=bk, func=AF.Square, accum_out=ssq[:, s4 * 8 + db:s4 * 8 + db + 1]),
                   reads=[bkn], writes=['junk_bf', 'ssq%d_%d' % (s4, db)])
                op('vector', lambda e, bk=bk, s4=s4, db=db: e.tensor_copy(out=zs[s4][:, db * 512:(db + 1) * 512], in_=bk),
                   reads=[bkn], writes=['zs%d_%d' % (s4, db)])
        for s4 in range(4):
            r0 = t0 + s4 * P
            op('sync', lambda e, r0=r0: e.dma_start(out=xs2[:], in_=x_chunk[r0:r0 + P, :]), writes=['xs2'], dma='xs2')
            ssn = ['ssq%d_%d' % (s4, db) for db in range(8)]
            st1 = stat[:, s4:s4 + 1]
            op('vector', lambda e, s4=s4, st1=st1: e.tensor_reduce(out=st1, in_=ssq[:, s4 * 8:(s4 + 1) * 8], axis=mybir.AxisListType.X, op=ALU.add),
               reads=ssn, writes=['st1_%d' % s4])
            emit_rstd(st1, st1, 1.0 / D, ['st1_%d' % s4], 'st1_%d' % s4)
            zn = ['zs%d_%d' % (s4, db) for db in range(8)]
            op('vector', lambda e, s4=s4, st1=st1: e.scalar_tensor_tensor(out=zs[s4][:], in0=zs[s4][:], scalar=st1, in1=Grow[:], op0=ALU.mult, op1=ALU.mult),
               reads=zn + ['st1_%d' % s4, 'Grow'], writes=['zs%d' % s4])
            op('vector', lambda e, s4=s4: e.tensor_tensor(out=xs2[:], in0=xs2[:], in1=zs[s4][:], op=ALU.add),
               reads=['xs2', 'zs%d' % s4], writes=['xs2'])
            op('sync', lambda e, r0=r0: e.dma_start(out=x1s[r0:r0 + P, :], in_=xs2[:]), reads=['xs2'], writes=['x1s_%d' % (tt * 4 + s4)], dma='x1st')
            st2 = stat[:, 8 + s4:9 + s4]
            op('scalar', lambda e, st2=st2: e.activation(out=xn2[:], in_=xs2[:], func=AF.Square, accum_out=st2),
               reads=['xs2'], writes=['xn2', 'st2_%d' % s4])
            emit_rstd(st2, st2, 1.0 / D, ['st2_%d' % s4], 'st2_%d' % s4)
            op('vector', lambda e, st2=st2: e.tensor_scalar(out=xn2[:], in0=xs2[:], scalar1=st2, scalar2=None, op0=ALU.mult),
               reads=['xs2', 'st2_%d' % s4], writes=['xn2'])
            for g4 in range(4):
                half = g4 % 2
                tv = tbf[:, half * 1024:(half + 1) * 1024]
                bn = psn[half]

                def tp2(e, g4=g4, tv=tv):
                    ins = None
                    for kk in range(8):
                        k = g4 * 8 + kk
                        ins = e.transpose(out=tv[:, kk * P:(kk + 1) * P], in_=xn2[:, k * P:(k + 1) * P], identity=ident_bf[:])
                    return ins
                op('tensor', tp2, reads=['xn2'], writes=[bn])

                def ev2a(e, g4=g4, tv=tv, s4=s4):
                    ins = None
                    for kk in range(8):
                        k = g4 * 8 + kk
                        ins = e.activation(out=actT[:, k, s4 * P:(s4 + 1) * P], in_=tv[:, kk * P:(kk + 1) * P], func=AF.Identity,
                                           scale=gain2c[:, k:k + 1], bias=shift2c[:, k:k + 1])
                    return ins

                def ev2v(e, g4=g4, tv=tv, s4=s4):
                    ins = None
                    for kk in range(8):
                        k = g4 * 8 + kk
                        ins = e.tensor_scalar(out=actT[:, k, s4 * P:(s4 + 1) * P], in0=tv[:, kk * P:(kk + 1) * P],
                                              scalar1=gain2c[:, k:k + 1], scalar2=shift2c[:, k:k + 1], op0=ALU.mult, op1=ALU.add)
                    return ins
                if g4 % 2 == 0:
                    op('scalar', ev2a, reads=[bn, 'gain2c'] + shift2n, writes=[actn[g4 * 8 + kk] for kk in range(8)])
                else:
                    op('vector', ev2v, reads=[bn, 'gain2c'] + shift2n, writes=[actn[g4 * 8 + kk] for kk in range(8)])
        op('sync', lambda e: e.dma_start(out=Grow[:], in_=Gs[1:2, :].partition_broadcast(P)), reads=GsN[1], writes=['Grow'], dma='grow')
        for fb in range(NFC // 2):
            f0 = fb * 256
            if STAGE_BF16:
                rg, rgn = ring2_load(lambda t: t[:], wgu_s[(fb * 2) * P:(fb * 2 + 1) * P, :])
                ru, run = ring2_load(lambda t: t[:], wgu_s[(fb * 2 + 1) * P:(fb * 2 + 2) * P, :])
            else:
                rg, rgn = ring2_load(v32, w_gate[:, f0:f0 + 256].rearrange("(k p) n -> p k n", p=P))
                ru, run = ring2_load(v32, w_up[:, f0:f0 + 256].rearrange("(k p) n -> p k n", p=P))
            rgv = v32(rg)
            ruv = v32(ru)
            bset = (fb % 2) * 4

            def mmg(e, rv=rgv, bset=bset, o=0):
                ins = None
                for cch in range(2):
                    for k in range(32):
                        ins = e.matmul(ps[bset + o + cch], lhsT=rv[:, k, cch * P:(cch + 1) * P], rhs=actT[:, k, :], start=(k == 0), stop=(k == 31))
                return ins
            op('tensor', mmg, reads=actn + [rgn], writes=[psn[bset], psn[bset + 1]])
            op('tensor', lambda e, rv=ruv, bset=bset: mmg(e, rv, bset, 2), reads=actn + [run], writes=[psn[bset + 2], psn[bset + 3]])
            for cch in range(2):
                fc = fb * 2 + cch
                si = c2['sg'] % 2
                c2['sg'] += 1
                s_t = sg[si]
                bg, bu = ps[bset + cch], ps[bset + 2 + cch]
                op('scalar', lambda e, s_t=s_t, bg=bg: e.activation(out=s_t[:], in_=bg, func=AF.Silu), reads=[psn[bset + cch]], writes=[s_t.name])
                op('vector', lambda e, s_t=s_t, bu=bu, fc=fc: e.tensor_tensor(out=aT[:, fc, :], in0=bu, in1=s_t[:], op=ALU.mult),
                   reads=[psn[bset + 2 + cch], s_t.name], writes=[aTn[fc]])
        for db in range(8):
            bset = (db % 2) * 4
            ngr = (NFC + 15) // 16
            for kg in range(ngr):
                nk = min(16, NFC - kg * 16)
                if STAGE_BF16:
                    rt, rn = ring2_load(lambda t, nk=nk: t[:, 0:nk * 512], wd_s[(db * 6 + kg) * P:(db * 6 + kg + 1) * P, 0:nk * 512])
                else:
                    rt, rn = ring2_load(lambda t, nk=nk: v16(t)[:, 0:nk, :],
                                        w_down[kg * 2048:kg * 2048 + nk * P, db * 512:(db + 1) * 512].rearrange("(k p) n -> p k n", p=P))

                def mmd(e, rt=v16(rt), kg=kg, nk=nk, bset=bset, ngr=ngr):
                    ins = None
                    for s4 in range(4):
                        for k in range(nk):
                            ins = e.matmul(ps[bset + s4], lhsT=aT[:, kg * 16 + k, s4 * P:(s4 + 1) * P], rhs=rt[:, k, :],
                                           start=(kg == 0 and k == 0), stop=(kg == ngr - 1 and k == nk - 1))
                    return ins
                op('tensor', mmd, reads=aTn[kg * 16:kg * 16 + nk] + [rn], writes=[psn[bset + s4] for s4 in range(4)])
            for s4 in range(4):
                bk, bkn = ps[bset + s4], psn[bset + s4]
                op('scalar', lambda e, bk=bk, s4=s4, db=db: e.activation(out=junk_bf[:], in_=bk, func=AF.Square, accum_out=ssq[:, s4 * 8 + db:s4 * 8 + db + 1]),
                   reads=[bkn], writes=['junk_bf', 'ssq%d_%d' % (s4, db)])
                ci = c2['scr'] % 2
                c2['scr'] += 1
                f_t = scr[ci]
                op('vector', lambda e, bk=bk, f_t=f_t, db=db: e.tensor_tensor(out=f_t[:], in0=bk, in1=Grow[:, db * 512:(db + 1) * 512], op=ALU.mult),
                   reads=[bkn, 'Grow'], writes=[f_t.name])
                r0 = t0 + s4 * P
                op('sync', lambda e, f_t=f_t, r0=r0, db=db: e.dma_start(out=accs[r0:r0 + P, db * 512:(db + 1) * 512], in_=f_t[:]),
                   reads=[f_t.name], writes=['acc_%d_%d' % (tt * 4 + s4, db)], dma=f_t.name + 'st')
        for s4 in range(4):
            r0 = t0 + s4 * P
            a_t = zs[s4 % 2]
            an = 'zs%d' % (s4 % 2)
            op('sync', lambda e, a_t=a_t, r0=r0: e.dma_start(out=a_t[:], in_=accs[r0:r0 + P, :]),
               reads=['acc_%d_%d' % (tt * 4 + s4, db) for db in range(8)], writes=[an] + ['zs%d_%d' % (s4 % 2, db) for db in range(8)], dma=an + 'ld')
            op('sync', lambda e, r0=r0: e.dma_start(out=xs2[:], in_=x1s[r0:r0 + P, :]), reads=['x1s_%d' % (tt * 4 + s4)], writes=['xs2'], dma='xs2')
            ssn = ['ssq%d_%d' % (s4, db) for db in range(8)]
            st3 = stat[:, 16 + s4:17 + s4]
            op('vector', lambda e, s4=s4, st3=st3: e.tensor_reduce(out=st3, in_=ssq[:, s4 * 8:(s4 + 1) * 8], axis=mybir.AxisListType.X, op=ALU.add),
               reads=ssn, writes=['st3_%d' % s4])
            emit_rstd(st3, st3, 1.0 / D, ['st3_%d' % s4], 'st3_%d' % s4)
            op('vector', lambda e, a_t=a_t, st3=st3: e.scalar_tensor_tensor(out=xs2[:], in0=a_t[:], scalar=st3, in1=xs2[:], op0=ALU.mult, op1=ALU.add),
               reads=[an, 'xs2', 'st3_%d' % s4] + ['zs%d_%d' % (s4 % 2, db) for db in range(8)], writes=['xs2'])
            op('sync', lambda e, r0=r0: e.dma_start(out=out[r0:r0 + P, :], in_=xs2[:]), reads=['xs2'], writes=['out_%d' % (tt * 4 + s4)], dma='outst')

    if debug and stop_after == 3:
        pass
    return finish()


def col_layout(v):
    return np.ascontiguousarray(np.asarray(v, dtype=np.float32).reshape(-1, P).T)


def make_in_maps(inputs):
    x = np.asarray(inputs["x"], dtype=np.float32)
    c = np.asarray(inputs["c"], dtype=np.float32)
    w_in = np.asarray(inputs["w_in"], dtype=np.float32)[0]
    w_out = np.asarray(inputs["w_out"], dtype=np.float32)[0]
    conv_w = np.asarray(inputs["conv_w"], dtype=np.float32)[0]
    b_f = np.asarray(inputs["b_f"], dtype=np.float32)[0]
    aon = np.asarray(inputs["attn_out_norm"], dtype=np.float32)[0]
    con = np.asarray(inputs["conv_out_norm"], dtype=np.float32)[0]
    shared = {
        "w_ada": np.ascontiguousarray(np.asarray(inputs["w_ada"], dtype=np.float32)[0]),
        "b_ada": np.ascontiguousarray(np.asarray(inputs["b_ada"], dtype=np.float32)[0][None, :]),
        "pre1_col": col_layout(inputs["pre_norm_mix"][0]),
        "pre2_col": col_layout(inputs["pre_norm_ffn"][0]),
        "post1_row": np.ascontiguousarray(np.asarray(inputs["post_norm_mix"], dtype=np.float32)[0][None, :]),
        "post2_row": np.ascontiguousarray(np.asarray(inputs["post_norm_ffn"], dtype=np.float32)[0][None, :]),
        "w_gate": np.ascontiguousarray(np.asarray(inputs["w_gate"], dtype=np.float32)[0]),
        "w_up": np.ascontiguousarray(np.asarray(inputs["w_up"], dtype=np.float32)[0]),
        "w_down": np.ascontiguousarray(np.asarray(inputs["w_down"], dtype=np.float32)[0]),
    }
    mchunks = []
    lbmap = {0: (4, 5), 1: (6, 7), 2: (0, 1), 3: (2, 3)}
    for g_ in range(4):
        for r in range(4):
            for lb2 in range(2):
                lb = lbmap[g_][lb2]
                mchunks.append(4 * r + lb if lb < 4 else 16 + 4 * r + (lb - 4))
    gy_full = np.concatenate([aon, con])
    gy_perm = np.concatenate([gy_full[m * P:(m + 1) * P] for m in mchunks])
    shared["gy_col"] = col_layout(gy_perm)
    shared["w_out_perm"] = np.ascontiguousarray(np.concatenate([w_out[m * P:(m + 1) * P] for m in mchunks], axis=0))
    maps = []
    for core in range(NCORES):
        b, g = divmod(core, 4)
        m = dict(shared)
        m["x_full"] = np.ascontiguousarray(x[b])
        m["x_chunk"] = np.ascontiguousarray(x[b, g * TOK2:(g + 1) * TOK2])
        m["c_col"] = col_layout(c[b])
        sl = lambda base: w_in[:, base + 512 * g: base + 512 * g + 512]
        wq, wk, wv = sl(0), sl(2048), sl(4096)
        wf = w_in[:, 6144 + 4 * g: 6144 + 4 * g + 4]
        wgb, wgc, wu = sl(6160), sl(6160 + 2048), sl(6160 + 4096)
        m["w1"] = np.ascontiguousarray(np.concatenate([wq, wk, wgb, wgc, wu, wv, wf], axis=1))
        m["wf_col"] = np.ascontiguousarray(wf.reshape(32, P, 4).transpose(1, 0, 2).reshape(P, 128))
        m["bf_row"] = np.ascontiguousarray(b_f[4 * g:4 * g + 4][None, :])
        cw = conv_w[:, 512 * g:512 * g + 512]
        m["convw_col"] = np.ascontiguousarray(cw.reshape(3, 4, P).transpose(2, 1, 0).reshape(P, 12))
        maps.append(m)
    return maps


_NC_CACHE = {}


def kernel(**inputs):
    if "nc" not in _NC_CACHE:
        _NC_CACHE["nc"] = build_nc()
    nc = _NC_CACHE["nc"]
    in_maps = make_in_maps(inputs)
    res = run_bass_kernel_spmd(nc, in_maps, core_ids=list(range(NCORES)))
    outp = np.empty((2, S, D), dtype=np.float32)
    for core in range(NCORES):
        b, g = divmod(core, 4)
        outp[b, g * TOK2:(g + 1) * TOK2] = np.asarray(res.results[core]["out"])
    return outp
```

```python
import os
import numpy as np
import concourse.bass as bass
import concourse.mybir as mybir
from concourse.bass_utils import run_bass_kernel_spmd

F32 = mybir.dt.float32
BF16 = mybir.dt.bfloat16
AF = mybir.ActivationFunctionType
ALU = mybir.AluOpType

D = 4096
S = 8192
NCORES = 8
DFF = 11008
NFC = DFF // 128
EPS = 1e-6
P = 128
TT1 = 1024
NT1 = S // TT1
TT2 = 512
TOK2 = 2048
NT2 = TOK2 // TT2
STAGE_BF16 = False
NW1 = 3072 + 4

ENG = ['sync', 'scalar', 'vector', 'gpsimd', 'tensor']
COMPUTE = ['scalar', 'vector', 'gpsimd', 'tensor']


class Tr:
    def __init__(s, nc):
        s.nc = nc
        s.prog = {e: [] for e in ENG}
        s.esem = {e: [nc.alloc_semaphore("prog_" + e), 0] for e in COMPUTE}
        s.dsem = {}
        s.res = {}
        s.waited = {e: {} for e in ENG}

    def _handle(s, semkey):
        kind, k = semkey
        return s.esem[k][0] if kind == 'e' else s.dsem[k][0]

    def _need(s, e, toks):
        best = {}
        for (semkey, v) in toks:
            if semkey == ('e', 'tensor') and e == 'tensor':
                continue
            if v > best.get(semkey, 0):
                best[semkey] = v
        for semkey, v in best.items():
            if s.waited[e].get(semkey, 0) >= v:
                continue
            s.waited[e][semkey] = v
            h = s._handle(semkey)
            s.prog[e].append(lambda eng, h=h, v=v: eng.wait_ge(h, v))

    def op(s, e, fn, reads=(), writes=(), dma=None):
        writes = list(writes) + [r for r in reads if r.startswith('bk') and r not in writes]
        toks = []
        for r in reads:
            st = s.res.get(r)
            if st and st[0]:
                toks.append(st[0])
        for w in writes:
            st = s.res.get(w)
            if st:
                if st[0]:
                    toks.append(st[0])
                toks.extend(st[1].items())
        s._need(e, toks)
        if dma is not None:
            if dma not in s.dsem:
                s.dsem[dma] = [s.nc.alloc_semaphore("d_" + str(dma)), 0]
            d = s.dsem[dma]
            d[1] += 16
            tok = (('d', dma), d[1])
            h, inc = d[0], 16
        else:
            d = s.esem[e]
            d[1] += 1
            tok = (('e', e), d[1])
            h, inc = d[0], 1
        s.prog[e].append(lambda eng, fn=fn, h=h, inc=inc: fn(eng).then_inc(h, inc))
        for w in writes:
            s.res[w] = [tok, {}]
        for r in reads:
            st = s.res.setdefault(r, [None, {}])
            if tok[1] > st[1].get(tok[0], 0):
                st[1][tok[0]] = tok[1]
        return tok

    def barrier(s):
        toks = [(('e', f), s.esem[f][1]) for f in COMPUTE if s.esem[f][1] > 0]
        toks += [(('d', k), v[1]) for k, v in s.dsem.items()]
        for e in ENG:
            for semkey, v in toks:
                if s.waited[e].get(semkey, 0) >= v:
                    continue
                s.waited[e][semkey] = v
                h = s._handle(semkey)
                s.prog[e].append(lambda eng, h=h, v=v: eng.wait_ge(h, v))
        s.res = {}

    def coll(s, fn, reads=(), writes=()):
        e = 'gpsimd'
        toks = []
        for r in reads:
            st = s.res.get(r)
            if st and st[0]:
                toks.append(st[0])
        for w in writes:
            st = s.res.get(w)
            if st:
                if st[0]:
                    toks.append(st[0])
                toks.extend(st[1].items())
        s._need(e, toks)
        if 'cc' not in s.dsem:
            s.dsem['cc'] = [s.nc.alloc_semaphore("cc"), 0]
        d = s.dsem['cc']
        d[1] += 1
        tok = (('d', 'cc'), d[1])
        h = d[0]
        s.prog[e].append(lambda eng: fn(eng).then_inc(h))
        for w in writes:
            s.res[w] = [tok, {}]
        return tok

    def emit(s, block):
        def mk(e):
            def body(eng):
                for fn in s.prog[e]:
                    fn(eng)
            return body
        block.sync(mk('sync'))
        block.scalar(mk('scalar'))
        block.vector(mk('vector'))
        block.gpsimd(mk('gpsimd'))
        block.tensor(mk('tensor'))


def build_nc(stop_after=9, debug=False):
    nc = bass.Bass("TRN2", target_bir_lowering=False)

    def din(name, shape, dt=F32):
        return nc.dram_tensor(name, shape, dt, kind="ExternalInput").ap()

    x_full = din("x_full", [S, D])
    x_chunk = din("x_chunk", [TOK2, D])
    c_col_in = din("c_col", [P, 32])
    w_ada = din("w_ada", [D, 6 * D])
    b_ada = din("b_ada", [1, 6 * D])
    pre1_in = din("pre1_col", [P, 32])
    pre2_in = din("pre2_col", [P, 32])
    post1_in = din("post1_row", [1, D])
    post2_in = din("post2_row", [1, D])
    w1 = din("w1", [D, NW1])
    bf_in = din("bf_row", [1, 4])
    wf_in = din("wf_col", [P, 128])
    convw_in = din("convw_col", [P, 12])
    gy_in = din("gy_col", [P, 32])
    w_out = din("w_out_perm", [D, D])
    w_gate = din("w_gate", [D, DFF])
    w_up = din("w_up", [D, DFF])
    w_down = din("w_down", [DFF, D])
    out = nc.dram_tensor("out", [TOK2, D], F32, kind="ExternalOutput").ap()
    if debug:
        dbg = nc.dram_tensor("dbg", [P, 8192], F32, kind="ExternalOutput").ap()

    Gs = nc.dram_tensor("Gs", [2, D], F32)
    qTs = nc.dram_tensor("qTs", [4 * P, S], BF16)
    kTs = nc.dram_tensor("kTs", [4 * P, S], BF16)
    Vs = nc.dram_tensor("Vs", [S, 512], BF16)
    ybuf = nc.dram_tensor("ybuf", [16384, 512], BF16)
    yslots = nc.dram_tensor("yslots", [16384, 512], BF16)
    ymine = nc.dram_tensor("ymine", [16384, 512], BF16)
    wo_s = nc.dram_tensor("wo_s", [16 * P, 8192], BF16)
    wgu_s = nc.dram_tensor("wgu_s", [86 * P, 8192], BF16)
    wd_s = nc.dram_tensor("wd_s", [48 * P, 8192], BF16)
    bar_in = nc.dram_tensor("bar_in", [16, 512], BF16)
    bar_out = nc.dram_tensor("bar_out", [64, 512], BF16)
    x1s = nc.dram_tensor("x1s", [TOK2, D], F32)
    accs = nc.dram_tensor("accs", [TOK2, D], F32)

    BASE = 16512
    LIMIT = 229344

    class Arena:
        def __init__(self, start):
            self.off = start
            self.n = 0

        def alloc(self, shape, dt, name=None):
            nbytes = int(np.prod(shape[1:])) * (4 if dt == F32 else 2)
            nbytes = (nbytes + 63) // 64 * 64
            self.n += 1
            t = nc.alloc_sbuf_tensor_at(name or ("t%d_%d" % (self.off, self.n)), list(shape), dt, offset=self.off)
            self.off += nbytes
            assert self.off <= LIMIT, (self.off, LIMIT)
            return t

    A0 = Arena(BASE)
    ident_bf = A0.alloc([P, P], BF16, "ident_bf")
    ident_f = A0.alloc([P, P], F32, "ident_f")
    tri_bf = A0.alloc([P, P], BF16, "tri_bf")
    tri_f = A0.alloc([P, P], F32, "tri_f")
    ones_bf = A0.alloc([P, P], BF16, "ones_bf")
    ones_f = A0.alloc([P, P], F32, "ones_f")
    gain1c = A0.alloc([P, 32], F32, "gain1c")
    shift1c = A0.alloc([P, 32], F32, "shift1c")
    gain2c = A0.alloc([P, 32], F32, "gain2c")
    shift2c = A0.alloc([P, 32], F32, "shift2c")
    sc1c = A0.alloc([P, 32], F32, "sc1c")
    sc2c = A0.alloc([P, 32], F32, "sc2c")
    pre1c = A0.alloc([P, 32], F32, "pre1c")
    pre2c = A0.alloc([P, 32], F32, "pre2c")
    gyc = A0.alloc([P, 32], F32, "gyc")
    ccol = A0.alloc([P, 32], F32, "ccol")
    cact = A0.alloc([P, 32], F32, "cact")
    SP = A0.alloc([P, 256], F32, "SP")
    wf_sb = A0.alloc([P, 32, 4], BF16, "wf_sb")
    bfrep = A0.alloc([P, 32], F32, "bfrep")
    convw = A0.alloc([P, 12], F32, "convw")
    halo = A0.alloc([P, 4, 2], F32, "halo")
    stat = A0.alloc([P, 64], F32, "stat")
    junk_bf = A0.alloc([P, 512], BF16, "junk_bf")
    junk_f = A0.alloc([P, P], F32, "junk_f")
    P0 = BASE + 8192
    assert A0.off <= P0, A0.off

    pp = [nc.alloc_psum_tensor("pp%d" % i, [P, 1024], F32) for i in range(4)]
    ps = [pp[i // 2][:, (i % 2) * 512:(i % 2 + 1) * 512] for i in range(8)]
    psn = ['bk%d' % i for i in range(8)]

    T = Tr(nc)
    op = T.op

    def mk_const(t, val, cmp_op=None):
        op('gpsimd', lambda e: e.memset(t[:], val), writes=[t.name])
        if cmp_op is not None:
            op('gpsimd', lambda e: e.affine_select(out=t[:], in_=t[:], pattern=[[1, P]], compare_op=cmp_op,
                                                   fill=0.0, base=0, channel_multiplier=-1),
               reads=[t.name], writes=[t.name])

    mk_const(ident_bf, 1.0, ALU.is_equal)
    mk_const(ident_f, 1.0, ALU.is_equal)
    mk_const(tri_bf, 1.0, ALU.is_ge)
    mk_const(tri_f, 1.0, ALU.is_ge)
    mk_const(ones_bf, 1.0)
    mk_const(ones_f, 1.0)
    op('gpsimd', lambda e: e.memset(halo[:], 0.0), writes=['halo'])

    def small_load(dst, src, key, name):
        op('sync', lambda e: e.dma_start(out=dst, in_=src), writes=[name], dma=key)

    small_load(ccol[:], c_col_in[:, :], 'm0', 'ccol')
    small_load(pre1c[:], pre1_in[:, :], 'm1', 'pre1c')
    small_load(pre2c[:], pre2_in[:, :], 'm2', 'pre2c')
    small_load(gyc[:], gy_in[:, :], 'm3', 'gyc')
    small_load(convw[:], convw_in[:, :], 'm4', 'convw')
    for s8 in range(8):
        small_load(bfrep[:, s8 * 4:(s8 + 1) * 4], bf_in[0:1, :].partition_broadcast(P), 'm5', 'bfrep%d' % s8)
    op('gpsimd', lambda e: e.dma_start(out=wf_sb[:].rearrange("p k n -> p (k n)"), in_=wf_in[:, :]), writes=['wf_sb'], dma='m6')

    def emit_rstd(dst, src, mul, rname, wname):
        op('vector', lambda e: e.tensor_scalar(out=dst, in0=src, scalar1=mul, scalar2=EPS, op0=ALU.mult, op1=ALU.add),
           reads=rname, writes=[wname])
        op('scalar', lambda e: e.activation(out=dst, in_=dst, func=AF.Sqrt), reads=[wname], writes=[wname])
        op('vector', lambda e: e.reciprocal(out=dst, in_=dst), reads=[wname], writes=[wname])

    A = Arena(P0)
    cB = A.alloc([P, 32, P], BF16, "cB")
    ring0 = [A.alloc([P, 16, 512], BF16, "ring0_%d" % i) for i in range(3)]
    bt = [A.alloc([P, 512], F32, "bt%d" % i) for i in range(2)]
    postr = [A.alloc([P, 512], F32, "postr%d" % i) for i in range(2)]
    modblk = [A.alloc([P, 512], F32, "modblk%d" % i) for i in range(2)]
    gblk = [A.alloc([P, 512], F32, "gblk%d" % i) for i in range(2)]

    op('scalar', lambda e: e.activation(out=cact[:], in_=ccol[:], func=AF.Silu), reads=['ccol'], writes=['cact'])
    for k in range(32):
        op('vector', lambda e, k=k: e.tensor_scalar(out=cB[:, k, :], in0=ones_bf[:], scalar1=cact[:, k:k + 1],
                                                    scalar2=None, op0=ALU.mult),
           reads=['cact', 'ones_bf'], writes=['cB%d' % k])
    cBn = ['cB%d' % k for k in range(32)]

    rcount = [0]

    cv_jobs = []
    for db in range(8):
        for kg in range(2):
            blk = db * 2 + kg
            cv_jobs.append((wo_s[blk * P:(blk + 1) * P, :].rearrange("p (k n) -> p k n", n=512),
                            w_out[kg * 2048:(kg + 1) * 2048, db * 512:(db + 1) * 512].rearrange("(k p) n -> p k n", p=P)))
    for fb in range(NFC // 2):
        for wi, wsrc in enumerate((w_gate, w_up)):
            blk = fb * 2 + wi
            cv_jobs.append((wgu_s[blk * P:(blk + 1) * P, :].rearrange("p (k n) -> p k n", n=256),
                            wsrc[:, fb * 256:(fb + 1) * 256].rearrange("(k p) n -> p k n", p=P)))
    for db in range(8):
        for kg in range(6):
            nk = min(16, NFC - kg * 16)
            blk = db * 6 + kg
            cv_jobs.append((wd_s[blk * P:(blk + 1) * P, 0:nk * 512].rearrange("p (k n) -> p k n", n=512),
                            w_down[kg * 2048:kg * 2048 + nk * P, db * 512:(db + 1) * 512].rearrange("(k p) n -> p k n", p=P)))
    cv_pos = [0]

    def cv_issue(n=1):
        if not STAGE_BF16:
            return
        for _ in range(n):
            if cv_pos[0] >= len(cv_jobs):
                return
            dst, src = cv_jobs[cv_pos[0]]
            key = 'cv%d' % (cv_pos[0] % 4)
            cv_pos[0] += 1
            op('gpsimd', lambda e, dst=dst, src=src: e.dma_start(out=dst, in_=src), dma=key)

    def ring_load(rings, view_fn, src_ap, nslots, cv=0):
        i = rcount[0] % nslots
        rcount[0] += 1
        rn = 'ring%d' % i
        dst = view_fn(rings[i])
        op('gpsimd', lambda e: e.dma_start(out=dst, in_=src_ap), writes=[rn], dma=rn)
        cv_issue(cv)
        return rings[i], rn

    seg_cols = {0: shift1c, 1: sc1c, 3: shift2c, 4: sc2c}

    def mod_load(cb, rings, nslots):
        c0 = cb * 512
        return [ring_load(rings, lambda t: t[:], w_ada[kg * 2048:(kg + 1) * 2048, c0:c0 + 512].rearrange("(k p) n -> p k n", p=P), nslots)
                for kg in range(2)]

    def mod_block(cb, rings, nslots, cBt, pb, pbn, b_t, m_t, p_t, g_t, loaded=None):
        seg = cb // 8
        c0 = cb * 512
        cBn_ = ['cB%d' % k for k in range(32)]
        if loaded is None:
            loaded = mod_load(cb, rings, nslots)
        for kg in range(2):
            rt, rn = loaded[kg]

            def mm(e, rt=rt, kg=kg, pb=pb):
                ins = None
                for k in range(16):
                    ins = e.matmul(pb, lhsT=cBt[:, kg * 16 + k, :], rhs=rt[:, k, :],
                                   start=(kg == 0 and k == 0), stop=(kg == 1 and k == 15))
                return ins
            op('tensor', mm, reads=[rn] + cBn_, writes=[pbn])
        op('sync', lambda e, b_t=b_t, c0=c0: e.dma_start(out=b_t[:], in_=b_ada[0:1, c0:c0 + 512].partition_broadcast(P)),
           writes=[b_t.name], dma=b_t.name)
        op('vector', lambda e, m_t=m_t, pb=pb, b_t=b_t: e.tensor_tensor(out=m_t[:], in0=pb, in1=b_t[:], op=ALU.add),
           reads=[pbn, b_t.name], writes=[m_t.name])
        if seg in seg_cols:
            colt = seg_cols[seg]
            for i in range(4):
                cidx = (cb % 8) * 4 + i
                op('vector', lambda e, m_t=m_t, i=i, colt=colt, cidx=cidx: e.scalar_tensor_tensor(
                    out=junk_f[:], in0=m_t[:, i * P:(i + 1) * P], scalar=1.0, in1=ident_f[:],
                    op0=ALU.mult, op1=ALU.mult, accum_out=colt[:, cidx:cidx + 1]),
                   reads=[m_t.name], writes=['junk_f', colt.name + str(cidx)])
        else:
            which = 0 if seg == 2 else 1
            prow = post1_in if which == 0 else post2_in
            cc0 = (cb % 8) * 512
            op('sync', lambda e, p_t=p_t, prow=prow, cc0=cc0: e.dma_start(
                out=p_t[:], in_=prow[0:1, cc0:cc0 + 512].partition_broadcast(P)), writes=[p_t.name], dma=p_t.name)
            op('vector', lambda e, g_t=g_t, m_t=m_t, p_t=p_t: e.tensor_tensor(out=g_t[:], in0=m_t[:], in1=p_t[:], op=ALU.mult),
               reads=[m_t.name, p_t.name], writes=[g_t.name])
            op('sync', lambda e, g_t=g_t, which=which, cc0=cc0: e.dma_start(out=Gs[which:which + 1, cc0:cc0 + 512], in_=g_t[0:1, :]),
               reads=[g_t.name], writes=['Gs%d_%d' % (which, cb % 8)], dma=g_t.name + 'st')

    for cb in range(16):
        mod_block(cb, ring0, 3, cB, ps[cb % 2], psn[cb % 2], bt[cb % 2], modblk[cb % 2], postr[cb % 2], gblk[cb % 2])
    allc = lambda t: [t.name + str(i) for i in range(32)]
    op('vector', lambda e: e.scalar_tensor_tensor(out=gain1c[:], in0=sc1c[:], scalar=1.0, in1=pre1c[:], op0=ALU.add, op1=ALU.mult),
       reads=allc(sc1c) + ['pre1c'], writes=['gain1c'])
    shift1n = allc(shift1c)
    shift2n = allc(shift2c)
    GsN = [['Gs%d_%d' % (w, i) for i in range(8)] for w in range(2)]

    def dbg_dump(items, reads):
        dt_ = nc.alloc_sbuf_tensor_at("dbgt", [P, 8192], F32, offset=LIMIT - 8192 * 4 - 64)
        op('vector', lambda e: e.memset(dt_[:], 0.0), writes=['dbgt'])
        for (off, n, src) in items:
            op('vector', lambda e, off=off, n=n, src=src: e.tensor_copy(out=dt_[:, off:off + n], in_=src),
               reads=list(reads) + ['dbgt'], writes=['dbgt'])
        op('sync', lambda e: e.dma_start(out=dbg[:, :], in_=dt_[:]), reads=['dbgt'], dma='dbgs')

    def finish():
        T.barrier()
        with nc.Block() as block:
            T.emit(block)
        return nc

    if stop_after == 0:
        if debug:
            dbg_dump([(0, 32, gain1c[:]), (32, 32, shift1c[:]), (64, 32, gain2c[:]), (96, 32, shift2c[:])],
                     ['gain1c', 'gain2c'] + shift1n + shift2n)
        return finish()

    T.barrier()
    A = Arena(P0)
    hT = A.alloc([P, 32, TT1], BF16, "hT")
    ring1 = [A.alloc([P, 32, 256], BF16, "ring1_%d" % i) for i in range(3)]
    xs = [A.alloc([P, D], F32, "xs%d" % i) for i in range(2)]
    xn = [A.alloc([P, D], BF16, "xn%d" % i) for i in range(2)]
    qst = [A.alloc([P, TT1], BF16, "qst%d" % i) for i in range(2)]
    vst = A.alloc([P, 8, 256], BF16, "vst")
    gb_sb = A.alloc([P, 2, TT1], F32, "gb_sb")
    gc_sb = A.alloc([P, 2, TT1], F32, "gc_sb")
    vbuf = A.alloc([P, 2, TT1 + 2], F32, "vbuf")
    ysb = [A.alloc([P, TT1], BF16, "ysb%d" % i) for i in range(2)]
    fsm = A0.alloc([P, 32], F32, "fsm")

    hTn = ['hT%d' % k for k in range(32)]
    ybn = []
    tbf = pp[0][:].bitcast(BF16)
    QSCALE = 1.0 / float(np.sqrt(128.0))
    blocks = [('q', 0, 0), ('q', 256, 1), ('k', 512, 0), ('k', 768, 1),
              ('gb', 1024, 0), ('gc', 1536, 0), ('u', 2048, 0),
              ('gb', 1280, 1), ('gc', 1792, 1), ('u', 2304, 1),
              ('v', 2560, 0), ('v', 2816, 1)]
    cnt = {'xs': 0, 'st': 0, 'pair': 0, 'bank': 0, 'qst': 0, 'ysb': 0, 'ev': 0}
    NT1_RUN = NT1 if stop_after != 1 or not debug else 2
    if int(os.environ.get('K_CUT', '99')) == 0:
        NT1_RUN = 0

    for ti in range(NT1_RUN):
        for s8 in range(8):
            i = cnt['xs'] % 2
            cnt['xs'] += 1
            xt, xnt = xs[i], xn[i]
            r0 = ti * TT1 + s8 * P
            op('sync', lambda e, xt=xt, r0=r0: e.dma_start(out=xt[:], in_=x_full[r0:r0 + P, :]), writes=[xt.name], dma=xt.name)
            sc = cnt['st'] % 32
            cnt['st'] += 1
            stc = stat[:, sc:sc + 1]
            stn = 'stat%d' % sc
            op('scalar', lambda e, xt=xt, xnt=xnt, stc=stc: e.activation(out=xnt[:], in_=xt[:], func=AF.Square, accum_out=stc),
               reads=[xt.name], writes=[xnt.name, stn])
            emit_rstd(stc, stc, 1.0 / D, [stn], stn)
            op('vector', lambda e, xt=xt, xnt=xnt, stc=stc: e.tensor_scalar(out=xnt[:], in0=xt[:], scalar1=stc, scalar2=None, op0=ALU.mult),
               reads=[xt.name, stn], writes=[xnt.name])
            for g4 in range(4):
                half = g4 % 2
                tv = tbf[:, half * 1024:(half + 1) * 1024]
                bn = psn[half]

                def tp(e, xnt=xnt, g4=g4, tv=tv):
                    ins = None
                    for kk in range(8):
                        k = g4 * 8 + kk
                        ins = e.transpose(out=tv[:, kk * P:(kk + 1) * P], in_=xnt[:, k * P:(k + 1) * P], identity=ident_bf[:])
                    return ins
                op('tensor', tp, reads=[xnt.name], writes=[bn])

                def ev_act(e, g4=g4, tv=tv, s8=s8):
                    ins = None
                    for kk in range(8):
                        k = g4 * 8 + kk
                        ins = e.activation(out=hT[:, k, s8 * P:(s8 + 1) * P], in_=tv[:, kk * P:(kk + 1) * P], func=AF.Identity,
                                           scale=gain1c[:, k:k + 1], bias=shift1c[:, k:k + 1])
                    return ins

                def ev_dve(e, g4=g4, tv=tv, s8=s8):
                    ins = None
                    for kk in range(8):
                        k = g4 * 8 + kk
                        ins = e.tensor_scalar(out=hT[:, k, s8 * P:(s8 + 1) * P], in0=tv[:, kk * P:(kk + 1) * P],
                                              scalar1=gain1c[:, k:k + 1], scalar2=shift1c[:, k:k + 1], op0=ALU.mult, op1=ALU.add)
                    return ins
                if g4 % 2 == 0:
                    op('scalar', ev_act, reads=[bn, 'gain1c'] + shift1n, writes=[hTn[g4 * 8 + kk] for kk in range(8)])
                else:
                    op('vector', ev_dve, reads=[bn, 'gain1c'] + shift1n, writes=[hTn[g4 * 8 + kk] for kk in range(8)])

        KCUT = int(os.environ.get("K_CUT", "99"))
        if KCUT <= 1:
            continue
        def mmf(e):
            ins = None
            for s8 in range(8):
                for k in range(32):
                    ins = e.matmul(ps[7][:, s8 * 4:(s8 + 1) * 4], lhsT=hT[:, k, s8 * P:(s8 + 1) * P], rhs=wf_sb[:, k, :],
                                   start=(k == 0), stop=(k == 31))
            return ins
        op('tensor', mmf, reads=hTn + ['wf_sb'], writes=[psn[7]])
        op('vector', lambda e: e.tensor_tensor(out=fsm[:], in0=ps[7][:, 0:32], in1=bfrep[:], op=ALU.add),
           reads=[psn[7]] + ['bfrep%d' % i for i in range(8)], writes=['fsm'])
        op('scalar', lambda e: e.activation(out=fsm[:], in_=fsm[:], func=AF.Exp, scale=-1.0), reads=['fsm'], writes=['fsm'])
        op('scalar', lambda e, ti=ti: e.activation(out=SP[:, ti * 32:(ti + 1) * 32], in_=fsm[:], func=AF.Ln, bias=1.0),
           reads=['fsm'], writes=['SP%d' % ti])

        for bidx, (kind, col0, idx) in enumerate(blocks):
            if bidx >= KCUT - 2:
                break
            rt, rn = ring_load(ring1, lambda t: t[:], w1[:, col0:col0 + 256].rearrange("(k p) n -> p k n", p=P), 3, cv=1)
            if kind == 'v':
                for s8 in range(8):
                    bi = 2 + cnt['bank'] % 6
                    cnt['bank'] += 1

                    def mmv(e, rt=rt, s8=s8, bi=bi):
                        ins = None
                        for k in range(32):
                            ins = e.matmul(ps[bi][:, 0:256], lhsT=hT[:, k, s8 * P:(s8 + 1) * P], rhs=rt[:, k, :],
                                           start=(k == 0), stop=(k == 31))
                        return ins
                    op('tensor', mmv, reads=hTn + [rn], writes=[psn[bi]])
                    eng = 'scalar' if s8 % 2 == 0 else 'vector'
                    if eng == 'scalar':
                        op('scalar', lambda e, s8=s8, bi=bi: e.activation(out=vst[:, s8, :], in_=ps[bi][:, 0:256], func=AF.Copy),
                           reads=[psn[bi]], writes=['vst%d' % s8])
                    else:
                        op('vector', lambda e, s8=s8, bi=bi: e.tensor_copy(out=vst[:, s8, :], in_=ps[bi][:, 0:256]),
                           reads=[psn[bi]], writes=['vst%d' % s8])
                op('sync', lambda e, ti=ti, idx=idx: e.dma_start(
                    out=Vs[ti * TT1:(ti + 1) * TT1, idx * 256:(idx + 1) * 256].rearrange("(s p) c -> p s c", p=P), in_=vst[:]),
                   reads=['vst%d' % i for i in range(8)], writes=['Vs_%d_%d' % (ti, idx)], dma='vst')
                continue
            for cch in range(2):
                pi = 1 + cnt['pair'] % 3
                cnt['pair'] += 1
                pair = pp[pi]
                pn = [psn[2 * pi], psn[2 * pi + 1]]

                def mmq(e, rt=rt, cch=cch, pair=pair):
                    ins = None
                    for k in range(32):
                        for half in range(2):
                            ins = e.matmul(pair[:, half * 512:(half + 1) * 512], lhsT=rt[:, k, cch * P:(cch + 1) * P],
                                           rhs=hT[:, k, half * 512:(half + 1) * 512], start=(k == 0), stop=(k == 31))
                    return ins
                op('tensor', mmq, reads=hTn + [rn], writes=pn)
                if kind in ('q', 'k'):
                    hl = idx * 2 + cch
                    qi = cnt['qst'] % 2
                    cnt['qst'] += 1
                    qt = qst[qi]
                    dst = qTs if kind == 'q' else kTs
                    if kind == 'q':
                        op('scalar', lambda e, qt=qt, pair=pair: e.activation(out=qt[:], in_=pair[:], func=AF.Copy, scale=QSCALE),
                           reads=pn, writes=[qt.name])
                    else:
                        op('vector', lambda e, qt=qt, pair=pair: e.tensor_copy(out=qt[:], in_=pair[:]), reads=pn, writes=[qt.name])
                    op('sync', lambda e, qt=qt, dst=dst, hl=hl, ti=ti: e.dma_start(
                        out=dst[hl * P:(hl + 1) * P, ti * TT1:(ti + 1) * TT1], in_=qt[:]),
                       reads=[qt.name], writes=['%sTs_%d_%d' % (kind, hl, ti)], dma=qt.name)
                elif kind == 'gb':
                    op('scalar', lambda e, cch=cch, pair=pair: e.activation(out=gb_sb[:, cch, :], in_=pair[:], func=AF.Copy),
                       reads=pn, writes=['gb%d' % cch])
                elif kind == 'gc':
                    op('vector', lambda e, cch=cch, pair=pair: e.tensor_copy(out=gc_sb[:, cch, :], in_=pair[:]),
                       reads=pn, writes=['gc%d' % cch])
                else:
                    cc = idx * 2 + cch
                    vn = 'vb%d' % cch
                    hn = 'halo%d' % cc
                    gcn = 'gc%d' % cch
                    yi = cnt['ysb'] % 2
                    cnt['ysb'] += 1
                    yt = ysb[yi]
                    op('vector', lambda e, cch=cch, cc=cc: e.tensor_copy(out=vbuf[:, cch, 0:2], in_=halo[:, cc, :]),
                       reads=['halo', hn], writes=[vn])
                    op('vector', lambda e, cch=cch, pair=pair: e.tensor_tensor(out=vbuf[:, cch, 2:2 + TT1], in0=pair[:], in1=gc_sb[:, cch, :], op=ALU.mult),
                       reads=pn + [gcn, vn], writes=[vn])
                    op('vector', lambda e, cch=cch, cc=cc: e.tensor_copy(out=halo[:, cc, :], in_=vbuf[:, cch, TT1:TT1 + 2]),
                       reads=[vn], writes=[hn])
                    op('vector', lambda e, cch=cch, cc=cc: e.tensor_scalar(out=gc_sb[:, cch, :], in0=vbuf[:, cch, 2:2 + TT1],
                                                                        scalar1=convw[:, cc * 3 + 2:cc * 3 + 3], scalar2=None, op0=ALU.mult),
                       reads=[vn, 'convw'], writes=[gcn])
                    op('vector', lambda e, cch=cch, cc=cc: e.scalar_tensor_tensor(out=gc_sb[:, cch, :], in0=vbuf[:, cch, 1:1 + TT1],
                                                                               scalar=convw[:, cc * 3 + 1:cc * 3 + 2], in1=gc_sb[:, cch, :],
                                                                               op0=ALU.mult, op1=ALU.add),
                       reads=[vn, gcn], writes=[gcn])
                    op('vector', lambda e, cch=cch, cc=cc: e.scalar_tensor_tensor(out=gc_sb[:, cch, :], in0=vbuf[:, cch, 0:TT1],
                                                                               scalar=convw[:, cc * 3:cc * 3 + 1], in1=gc_sb[:, cch, :],
                                                                               op0=ALU.mult, op1=ALU.add),
                       reads=[vn, gcn], writes=[gcn])
                    op('vector', lambda e, cch=cch, yt=yt: e.tensor_tensor(out=yt[:], in0=gc_sb[:, cch, :], in1=gb_sb[:, cch, :], op=ALU.mult),
                       reads=[gcn, 'gb%d' % cch], writes=[yt.name])
                    for hf in range(2):
                        rrow = ((((ti // 2) * 4 + cc // 2) * 4 + (ti % 2) * 2 + hf) * 2 + cc % 2) * P
                        yn_ = 'yb_c_%d_%d_%d' % (ti, cc, hf)
                        ybn.append(yn_)
                        op('sync', lambda e, yt=yt, rrow=rrow, hf=hf: e.dma_start(out=ybuf[rrow:rrow + P, :], in_=yt[:, hf * 512:(hf + 1) * 512]),
                           reads=[yt.name], writes=[yn_], dma=yt.name + '_%d' % hf)

    if stop_after == 1:
        if debug:
            T.barrier()
            dt_ = nc.alloc_sbuf_tensor_at("dbgt", [P, 8192], F32, offset=LIMIT - 8192 * 4 - 64)
            op('vector', lambda e: e.memset(dt_[:], 0.0), writes=['dbgt'])
            op('vector', lambda e: e.tensor_copy(out=dt_[:, 0:256], in_=SP[:]), reads=['dbgt'], writes=['dbgt'])
            op('vector', lambda e: e.tensor_copy(out=dt_[:, 256:384], in_=hT[:, 0, 896:1024]), reads=['dbgt'], writes=['dbgt'])
            op('vector', lambda e: e.tensor_copy(out=dt_[:, 384:512], in_=hT[:, 31, 0:128]), reads=['dbgt'], writes=['dbgt'])
            qd = nc.alloc_sbuf_tensor_at("qd", [P, 4, 1024], BF16, offset=LIMIT - 8192 * 4 - 64 - 8192)
            op('sync', lambda e: e.dma_start(out=qd[:, 0, :], in_=qTs[0:P, 0:1024]), writes=['qd'], dma='dq0')
            op('sync', lambda e: e.dma_start(out=qd[:, 1, :], in_=kTs[P:2 * P, 1024:2048]), writes=['qd1'], dma='dq1')
            op('sync', lambda e: e.dma_start(out=qd[:, 2, 0:512], in_=Vs[1024:1024 + P, 0:512]), writes=['qd2'], dma='dq2')
            op('sync', lambda e: e.dma_start(out=qd[:, 3, :], in_=ybuf[5 * P:6 * P, 0:1024]), writes=['qd3'], dma='dq3')
            op('vector', lambda e: e.tensor_copy(out=dt_[:, 1024:2048], in_=qd[:, 0, :]), reads=['qd', 'dbgt'], writes=['dbgt'])
            op('vector', lambda e: e.tensor_copy(out=dt_[:, 2048:3072], in_=qd[:, 1, :]), reads=['qd1', 'dbgt'], writes=['dbgt'])
            op('vector', lambda e: e.tensor_copy(out=dt_[:, 3072:3584], in_=qd[:, 2, 0:512]), reads=['qd2', 'dbgt'], writes=['dbgt'])
            op('vector', lambda e: e.tensor_copy(out=dt_[:, 4096:5120], in_=qd[:, 3, :]), reads=['qd3', 'dbgt'], writes=['dbgt'])
            op('sync', lambda e: e.dma_start(out=dbg[:, :], in_=dt_[:]), reads=['dbgt'], dma='dbgs')
        return finish()

    T.barrier()
    A = Arena(P0)
    qT = [A.alloc([P, S], BF16, "qT%d" % i) for i in range(2)]
    kT = [A.alloc([P, S], BF16, "kT%d" % i) for i in range(2)]
    Vt = [A.alloc([P, 64, P], BF16, "Vt%d" % i) for i in range(2)]
    Bm = [A.alloc([P, 64, 64], F32, "Bm%d" % i) for i in range(2)]
    PT = [A.alloc([P, 512], BF16, "PT%d" % i) for i in range(3)]
    rl = [A.alloc([P, 512], F32, "rl%d" % i) for i in range(2)]
    ost = [A.alloc([P, 512], BF16, "ost%d" % i) for i in range(2)]
    CumH = A.alloc([P, 4, 64], F32, "CumH")
    TotH = A.alloc([P, 4, 64], F32, "TotH")
    EndH = A.alloc([P, 4, 64], F32, "EndH")
    CumF = A.alloc([P, 4, 64], F32, "CumF")
    SPn = ['SP%d' % i for i in range(NT1)]
    cB2 = A.alloc([P, 32, P], BF16, "cB2")
    ringA = [A.alloc([P, 16, 512], BF16, "ringA_%d" % i) for i in range(2)]
    bt1 = A.alloc([P, 512], F32, "bt1")
    postr1 = A.alloc([P, 512], F32, "postr1")
    modblk1 = A.alloc([P, 512], F32, "modblk1")
    gblk1 = A.alloc([P, 512], F32, "gblk1")
    for k in range(32):
        op('vector', lambda e, k=k: e.tensor_scalar(out=cB2[:, k, :], in0=ones_bf[:], scalar1=cact[:, k:k + 1],
                                                    scalar2=None, op0=ALU.mult), writes=['cB%d' % k])

    pst = {}

    def gp_j(e):
        if 'j' not in pst:
            pst['j'] = e.partition_id() % 4
        return pst['j']

    def exchange_group(g, reads):
        for jp in range(4):
            r0 = (jp * 4 + g) * 1024
            T.coll(lambda e, r0=r0, jp=jp: e.collective_compute(
                "AllGather", ALU.bypass, replica_groups=[[0, 1, 2, 3], [4, 5, 6, 7]],
                ins=[ybuf[r0:r0 + 1024, :]], outs=[yslots[jp * 4096:(jp + 1) * 4096, :]]),
                reads=reads if jp == 0 else [], writes=['ysl%d' % jp, 'ccchain'])

        def cp(e, g=g):
            j = gp_j(e)
            return e.dma_start(out=ymine[g * 4096:(g + 1) * 4096, :], in_=yslots[bass.ds(j * 4096, 4096), :])
        op('gpsimd', cp, reads=['ysl%d' % i for i in range(4)], writes=['ym%d' % g], dma='ymc')
        T.coll(lambda e: e.collective_compute("AllGather", ALU.bypass, replica_groups=[[0, 1, 2, 3], [4, 5, 6, 7]],
                                              ins=[bar_in[:, :]], outs=[bar_out[:, :]]),
               reads=['ym%d' % g], writes=['ysl%d' % i for i in range(4)] + ['ccchain'])

    exchange_group(0, [])
    exchange_group(1, [])

    op('tensor', lambda e: e.matmul(ps[6][:, 0:256], lhsT=tri_f[:], rhs=SP[:], start=True, stop=True), reads=SPn + ['tri_f'], writes=[psn[6]])
    op('tensor', lambda e: e.matmul(ps[7][:, 0:256], lhsT=ones_f[:], rhs=SP[:], start=True, stop=True), reads=SPn + ['ones_f'], writes=[psn[7]])
    op('vector', lambda e: e.tensor_copy(out=CumH[:], in_=ps[6][:, 0:256].rearrange("p (kb h) -> p h kb", h=4)), reads=[psn[6]], writes=['CumH'])
    op('vector', lambda e: e.tensor_copy(out=TotH[:], in_=ps[7][:, 0:256].rearrange("p (kb h) -> p h kb", h=4)), reads=[psn[7]], writes=['TotH'])
    for h in range(4):
        op('vector', lambda e, h=h: e.tensor_tensor_scan(out=EndH[:, h, :], data0=ones_f[:, 0:64], data1=TotH[:, h, :], initial=0.0,
                                                         op0=ALU.mult, op1=ALU.add), reads=['TotH', 'ones_f'], writes=['EndH%d' % h])
    EndN = ['EndH%d' % h for h in range(4)]
    op('vector', lambda e: e.tensor_tensor(out=CumF[:], in0=CumH[:], in1=EndH[:], op=ALU.add), reads=['CumH'] + EndN, writes=['CumF'])
    op('vector', lambda e: e.tensor_tensor(out=CumF[:], in0=CumF[:], in1=TotH[:], op=ALU.subtract), reads=['CumF', 'TotH'], writes=['CumF'])

    cntb = {'s': 0, 'pt': 0, 'o': 0, 'ost': 0}
    NH_RUN = 4 if not (debug and stop_after == 2) else 1
    PTx = PT + [A.alloc([P, 512], BF16, "PT3")]
    SB = [0, 1, 6]
    LOOK = 2

    def head_setup(hl):
        sl = hl % 2
        q_t, k_t, v_t, b_t = qT[sl], kT[sl], Vt[sl], Bm[sl]
        op('sync', lambda e, q_t=q_t, hl=hl: e.dma_start(out=q_t[:], in_=qTs[hl * P:(hl + 1) * P, :]),
           reads=['qTs_%d_%d' % (hl, t) for t in range(NT1)], writes=[q_t.name], dma=q_t.name)
        op('sync', lambda e, k_t=k_t, hl=hl: e.dma_start(out=k_t[:], in_=kTs[hl * P:(hl + 1) * P, :]),
           reads=['kTs_%d_%d' % (hl, t) for t in range(NT1)], writes=[k_t.name], dma=k_t.name)
        op('sync', lambda e, v_t=v_t, hl=hl: e.dma_start(out=v_t[:], in_=Vs[:, hl * P:(hl + 1) * P].rearrange("(kb p) d -> p kb d", p=P)),
           reads=['Vs_%d_%d' % (t, hl // 2) for t in range(NT1)], writes=[v_t.name], dma=v_t.name)

        def mkB(e, b_t=b_t, hl=hl):
            ins = None
            for kb in range(64):
                ins = e.tensor_scalar(out=b_t[:, kb, :], in0=EndH[:, hl, :], scalar1=-1.0, scalar2=CumF[:, hl, kb:kb + 1],
                                      op0=ALU.mult, op1=ALU.add)
            return ins
        op('vector', mkB, reads=EndN + ['CumF'], writes=[b_t.name])

    blks = []
    for hl in range(NH_RUN):
        for qg in range(16):
            for kb in range(4 * qg + 4):
                blks.append((hl, qg, kb))
    info = {}

    def emit_S(i):
        hl, qg, kb = blks[i]
        sl = hl % 2
        q_t, k_t = qT[sl], kT[sl]
        c0 = max(0, kb - 4 * qg) * P
        bi = SB[i % 3]
        bS, bSn = ps[bi], psn[bi]
        op('tensor', lambda e, bS=bS, c0=c0, k_t=k_t, q_t=q_t, kb=kb, qg=qg: e.matmul(
            bS[:, c0:512], lhsT=k_t[:, kb * P:(kb + 1) * P], rhs=q_t[:, qg * 512 + c0:(qg + 1) * 512], start=True, stop=True),
           reads=[k_t.name, q_t.name], writes=[bSn])
        info[i] = (bS, bSn)

    head_setup(0)
    if NH_RUN > 1:
        head_setup(1)
    for i in range(min(LOOK, len(blks))):
        emit_S(i)
    mb_next = [16]
    mb_loaded = [None]
    i_ex2 = 2 * 544
    trig = set()
    if NH_RUN == 4:
        ra = list(range(600, i_ex2 - 20))
        rb = list(range(i_ex2 + 300, len(blks) - 40))
        na = 12
        nb_ = 20
        trig = set(ra[(t * len(ra)) // na] for t in range(na)) | set(rb[(t * len(rb)) // nb_] for t in range(nb_))
    for i, (hl, qg, kb) in enumerate(blks):
        sl = hl % 2
        v_t, b_t = Vt[sl], Bm[sl]
        nkb = 4 * qg + 4
        if i == 560 and NH_RUN == 4:
            mb_loaded[0] = mod_load(mb_next[0], ringA, 2)
        if i in trig and mb_next[0] < 48:
            mod_block(mb_next[0], ringA, 2, cB2, ps[7], psn[7], bt1, modblk1, postr1, gblk1, loaded=mb_loaded[0])
            mb_next[0] += 1
            cv_issue(2)
            mb_loaded[0] = mod_load(mb_next[0], ringA, 2) if mb_next[0] < 48 else None
        if NH_RUN == 4 and hl == 2 and qg == 0 and kb == 0:
            exchange_group(2, [n_ for n_ in ybn if n_.startswith('yb_a_0_') or n_.startswith('yb_a_1_')])
        if qg == 0 and kb == 0 and hl >= 1 and hl + 1 < NH_RUN:
            head_setup(hl + 1)
        if i + LOOK < len(blks):
            emit_S(i + LOOK)
        if kb == 0:
            oi = cntb['o'] % 2
            cntb['o'] += 1
            info['o'] = (ps[2 + oi], ps[4 + oi], psn[2 + oi], psn[4 + oi])
        bO, bL, bOn, bLn = info['o']
        bS, bSn = info.pop(i)
        j0 = max(0, kb - 4 * qg)
        c0 = j0 * P
        p_t = PTx[i % 4]

        def ex(e, bS=bS, p_t=p_t, b_t=b_t, j0=j0, kb=kb, qg=qg):
            ins = None
            for j in range(j0, 4):
                ins = e.activation(out=p_t[:, j * P:(j + 1) * P], in_=bS[:, j * P:(j + 1) * P], func=AF.Exp,
                                   bias=b_t[:, kb, 4 * qg + j:4 * qg + j + 1], scale=1.0)
            return ins
        op('scalar', ex, reads=[bSn, b_t.name], writes=[p_t.name])
        if kb >= 4 * qg:
            op('vector', lambda e, p_t=p_t, c0=c0: e.tensor_tensor(out=p_t[:, c0:c0 + P], in0=p_t[:, c0:c0 + P], in1=tri_bf[:], op=ALU.mult),
               reads=[p_t.name, 'tri_bf'], writes=[p_t.name])

        def pv(e, bO=bO, bL=bL, p_t=p_t, v_t=v_t, kb=kb, c0=c0, nkb=nkb):
            e.matmul(bO[:, c0:512], lhsT=v_t[:, kb, :], rhs=p_t[:, c0:512], start=(kb == 0), stop=(kb == nkb - 1))
            return e.matmul(bL[:, c0:512], lhsT=ones_bf[:], rhs=p_t[:, c0:512], start=(kb == 0), stop=(kb == nkb - 1))
        op('tensor', pv, reads=[p_t.name, v_t.name, 'ones_bf'], writes=[bOn, bLn])
        if kb == nkb - 1:
            oi2 = cntb['ost'] % 2
            cntb['ost'] += 1
            r_t, o_t = rl[oi2], ost[oi2]
            op('vector', lambda e, r_t=r_t, bL=bL: e.reciprocal(out=r_t[:], in_=bL), reads=[bLn], writes=[r_t.name])
            op('vector', lambda e, r_t=r_t, o_t=o_t, bO=bO: e.tensor_tensor(out=o_t[:], in0=bO, in1=r_t[:], op=ALU.mult),
               reads=[bOn, r_t.name], writes=[o_t.name])
            rrow = ((((qg // 4) * 4 + 2 + hl // 2) * 4 + qg % 4) * 2 + hl % 2) * P
            yn_ = 'yb_a_%d_%d' % (hl, qg)
            ybn.append(yn_)
            op('sync', lambda e, o_t=o_t, rrow=rrow: e.dma_start(out=ybuf[rrow:rrow + P, :], in_=o_t[:]),
               reads=[o_t.name], writes=[yn_], dma=o_t.name)

    while NH_RUN == 4 and mb_next[0] < 48:
        mod_block(mb_next[0], ringA, 2, cB2, ps[7], psn[7], bt1, modblk1, postr1, gblk1, loaded=mb_loaded[0])
        mb_loaded[0] = None
        mb_next[0] += 1
    cv_issue(len(cv_jobs))
    op('vector', lambda e: e.scalar_tensor_tensor(out=gain2c[:], in0=sc2c[:], scalar=1.0, in1=pre2c[:], op0=ALU.add, op1=ALU.mult),
       reads=allc(sc2c) + ['pre2c'], writes=['gain2c'])

    if stop_after == 2:
        if debug:
            dt_ = nc.alloc_sbuf_tensor_at("dbgt", [P, 8192], F32, offset=LIMIT - 8192 * 4 - 64)
            qd = nc.alloc_sbuf_tensor_at("qd", [P, 2, 2048], BF16, offset=LIMIT - 8192 * 4 - 64 - 8192)
            T.barrier()
            op('vector', lambda e: e.memset(dt_[:], 0.0), writes=['dbgt'])
            op('sync', lambda e: e.dma_start(out=qd[:, 0, :], in_=ybuf[0:P, :]), writes=['qd'], dma='dq0')
            op('sync', lambda e: e.dma_start(out=qd[:, 1, :], in_=ybuf[24 * P:25 * P, :]), writes=['qd1'], dma='dq1')
            op('vector', lambda e: e.tensor_copy(out=dt_[:, 0:2048], in_=qd[:, 0, :]), reads=['qd', 'dbgt'], writes=['dbgt'])
            op('vector', lambda e: e.tensor_copy(out=dt_[:, 2048:4096], in_=qd[:, 1, :]), reads=['qd1', 'dbgt'], writes=['dbgt'])
            op('vector', lambda e: e.tensor_copy(out=dt_[:, 4096:4352], in_=CumF[:].rearrange("p h kb -> p (h kb)")), reads=['dbgt'], writes=['dbgt'])
            op('sync', lambda e: e.dma_start(out=dbg[:, :], in_=dt_[:]), reads=['dbgt'], dma='dbgs')
        return finish()

    exchange_group(3, [n_ for n_ in ybn if n_.startswith('yb_a_2_') or n_.startswith('yb_a_3_')])
    T.barrier()

    A = Arena(P0)
    aT = A.alloc([P, NFC, TT2], BF16, "aT")
    zs = [nc.alloc_sbuf_tensor_at("zs%d" % i, [P, D], F32, offset=P0 + i * D * 4) for i in range(4)]
    actT = A.alloc([P, 32, TT2], BF16, "actT")
    ring2 = [A.alloc([P, 8192], BF16, "ring2_%d" % i) for i in range(2)]
    v16 = lambda t: t[:].rearrange("p (k n) -> p k n", n=512)
    v32 = lambda t: t[:].rearrange("p (k n) -> p k n", n=256)
    xs2 = A.alloc([P, D], F32, "xs2")
    Grow = A.alloc([P, D], F32, "Grow")
    xn2 = A.alloc([P, D], BF16, "xn2")
    scr = [A.alloc([P, 512], F32, "scr%d" % i) for i in range(2)]
    sq = A.alloc([P, 512], BF16, "sq")
    sg = [A.alloc([P, 512], F32, "sg%d" % i) for i in range(2)]
    ssq = A0.alloc([P, 32], F32, "ssq")
    actn = ['act%d' % k for k in range(32)]
    aTn = ['aT%d' % k for k in range(NFC)]
    c2 = {'ring': 0, 'sg': 0, 'scr': 0}

    def ring2_load(view, src_ap):
        i = c2['ring'] % 2
        c2['ring'] += 1
        rn = 'ring%d' % i
        dst = view(ring2[i])
        op('gpsimd', lambda e: e.dma_start(out=dst, in_=src_ap), writes=[rn], dma=rn)
        return ring2[i], rn

    NT2_RUN = NT2 if not (debug and stop_after == 3) else 1
    for tt in range(NT2_RUN):
        t0 = tt * TT2
        for gr in range(16):
            rr0 = gr * 1024 + tt * 256
            op('sync', lambda e, gr=gr, rr0=rr0: e.dma_start(out=actT[:, gr * 2:gr * 2 + 2, :],
                                                           in_=ymine[rr0:rr0 + 256, :].rearrange("(l p) t -> p l t", p=P)),
               writes=actn[gr * 2:gr * 2 + 2], dma='yl%d' % (gr % 4))
        op('sync', lambda e: e.dma_start(out=Grow[:], in_=Gs[0:1, :].partition_broadcast(P)), reads=GsN[0], writes=['Grow'], dma='grow')
        for grp in range(2):
            bnk, bnkn = ps[grp], psn[grp]
            kks = list(range(16, 32)) if grp == 0 else list(range(0, 16))
            for n_, kk in enumerate(kks):
                op('vector', lambda e, kk=kk: e.tensor_tensor(out=sq[:], in0=actT[:, kk, :], in1=actT[:, kk, :], op=ALU.mult),
                   reads=[actn[kk]], writes=['sq'])
                op('tensor', lambda e, bnk=bnk, n_=n_: e.matmul(bnk, lhsT=ones_bf[:], rhs=sq[:], start=(n_ == 0), stop=(n_ == 15)),
                   reads=['sq', 'ones_bf'], writes=[bnkn])
            emit_rstd(scr[grp][:], bnk, 1.0 / 2048, [bnkn], scr[grp].name)
        for kk in range(32):
            grp = 0 if kk >= 16 else 1
            op('vector', lambda e, kk=kk, grp=grp: e.scalar_tensor_tensor(out=actT[:, kk, :], in0=actT[:, kk, :], scalar=gyc[:, kk:kk + 1],
                                                                         in1=scr[grp][:], op0=ALU.mult, op1=ALU.mult),
               reads=[actn[kk], scr[grp].name, 'gyc'], writes=[actn[kk]])
        for db in range(8):
            bset = (db % 2) * 4
            for kg in range(2):
                if STAGE_BF16:
                    rt, rn = ring2_load(lambda t: t[:], wo_s[(db * 2 + kg) * P:(db * 2 + kg + 1) * P, :])
                else:
                    rt, rn = ring2_load(v16, w_out[kg * 2048:(kg + 1) * 2048, db * 512:(db + 1) * 512].rearrange("(k p) n -> p k n", p=P))

                def mmo(e, rt=v16(rt), kg=kg, bset=bset):
                    ins = None
                    for s4 in range(4):
                        for k in range(16):
                            ins = e.matmul(ps[bset + s4], lhsT=actT[:, kg * 16 + k, s4 * P:(s4 + 1) * P], rhs=rt[:, k, :],
                                           start=(kg == 0 and k == 0), stop=(kg == 1 and k == 15))
                    return ins
                op('tensor', mmo, reads=actn + [rn], writes=[psn[bset + s4] for s4 in range(4)])
            for s4 in range(4):
                bk, bkn = ps[bset + s4], psn[bset + s4]
                op('scalar', lambda e, bk=bk, s4=s4, db=db: e.activation(out=junk_bf[:], in_=bk, func=AF.Square, accum_out=ssq[:, s4 * 8 + db:s4 * 8 + db + 1]),
                   reads=[bkn], writes=['junk_bf', 'ssq%d_%d' % (s4, db)])
                op('vector', lambda e, bk=bk, s4=s4, db=db: e.tensor_copy(out=zs[s4][:, db * 512:(db + 1) * 512], in_=bk),
                   reads=[bkn], writes=['zs%d_%d' % (s4, db)])
        for s4 in range(4):
            r0 = t0 + s4 * P
            op('sync', lambda e, r0=r0: e.dma_start(out=xs2[:], in_=x_chunk[r0:r0 + P, :]), writes=['xs2'], dma='xs2')
            ssn = ['ssq%d_%d' % (s4, db) for db in range(8)]
            st1 = stat[:, s4:s4 + 1]
            op('vector', lambda e, s4=s4, st1=st1: e.tensor_reduce(out=st1, in_=ssq[:, s4 * 8:(s4 + 1) * 8], axis=mybir.AxisListType.X, op=ALU.add),
               reads=ssn, writes=['st1_%d' % s4])
            emit_rstd(st1, st1, 1.0 / D, ['st1_%d' % s4], 'st1_%d' % s4)
            zn = ['zs%d_%d' % (s4, db) for db in range(8)]
            op('vector', lambda e, s4=s4, st1=st1: e.scalar_tensor_tensor(out=zs[s4][:], in0=zs[s4][:], scalar=st1, in1=Grow[:], op0=ALU.mult, op1=ALU.mult),
               reads=zn + ['st1_%d' % s4, 'Grow'], writes=['zs%d' % s4])
            op('vector', lambda e, s4=s4: e.tensor_tensor(out=xs2[:], in0=xs2[:], in1=zs[s4][:], op=ALU.add),
               reads=['xs2', 'zs%d' % s4], writes=['xs2'])
            op('sync', lambda e, r0=r0: e.dma_start(out=x1s[r0:r0 + P, :], in_=xs2[:]), reads=['xs2'], writes=['x1s_%d' % (tt * 4 + s4)], dma='x1st')
            st2 = stat[:, 8 + s4:9 + s4]
            op('scalar', lambda e, st2=st2: e.activation(out=xn2[:], in_=xs2[:], func=AF.Square, accum_out=st2),
               reads=['xs2'], writes=['xn2', 'st2_%d' % s4])
            emit_rstd(st2, st2, 1.0 / D, ['st2_%d' % s4], 'st2_%d' % s4)
            op('vector', lambda e, st2=st2: e.tensor_scalar(out=xn2[:], in0=xs2[:], scalar1=st2, scalar2=None, op0=ALU.mult),
               reads=['xs2', 'st2_%d' % s4], writes=['xn2'])
            for g4 in range(4):
                half = g4 % 2
                tv = tbf[:, half * 1024:(half + 1) * 1024]
                bn = psn[half]

                def tp2(e, g4=g4, tv=tv):
                    ins = None
                    for kk in range(8):
                        k = g4 * 8 + kk
                        ins = e.transpose(out=tv[:, kk * P:(kk + 1) * P], in_=xn2[:, k * P:(k + 1) * P], identity=ident_bf[:])
                    return ins
                op('tensor', tp2, reads=['xn2'], writes=[bn])

                def ev2a(e, g4=g4, tv=tv, s4=s4):
                    ins = None
                    for kk in range(8):
                        k = g4 * 8 + kk
                        ins = e.activation(out=actT[:, k, s4 * P:(s4 + 1) * P], in_=tv[:, kk * P:(kk + 1) * P], func=AF.Identity,
                                           scale=gain2c[:, k:k + 1], bias=shift2c[:, k:k + 1])
                    return ins

                def ev2v(e, g4=g4, tv=tv, s4=s4):
                    ins = None
                    for kk in range(8):
                        k = g4 * 8 + kk
                        ins = e.tensor_scalar(out=actT[:, k, s4 * P:(s4 + 1) * P], in0=tv[:, kk * P:(kk + 1) * P],
                                              scalar1=gain2c[:, k:k + 1], scalar2=shift2c[:, k:k + 1], op0=ALU.mult, op1=ALU.add)
                    return ins
                if g4 % 2 == 0:
                    op('scalar', ev2a, reads=[bn, 'gain2c'] + shift2n, writes=[actn[g4 * 8 + kk] for kk in range(8)])
                else:
                    op('vector', ev2v, reads=[bn, 'gain2c'] + shift2n, writes=[actn[g4 * 8 + kk] for kk in range(8)])
        op('sync', lambda e: e.dma_start(out=Grow[:], in_=Gs[1:2, :].partition_broadcast(P)), reads=GsN[1], writes=['Grow'], dma='grow')
        for fb in range(NFC // 2):
            f0 = fb * 256
            if STAGE_BF16:
                rg, rgn = ring2_load(lambda t: t[:], wgu_s[(fb * 2) * P:(fb * 2 + 1) * P, :])
                ru, run = ring2_load(lambda t: t[:], wgu_s[(fb * 2 + 1) * P:(fb * 2 + 2) * P, :])
            else:
                rg, rgn = ring2_load(v32, w_gate[:, f0:f0 + 256].rearrange("(k p) n -> p k n", p=P))
                ru, run = ring2_load(v32, w_up[:, f0:f0 + 256].rearrange("(k p) n -> p k n", p=P))
            rgv = v32(rg)
            ruv = v32(ru)
            bset = (fb % 2) * 4

            def mmg(e, rv=rgv, bset=bset, o=0):
                ins = None
                for cch in range(2):
                    for k in range(32):
                        ins = e.matmul(ps[bset + o + cch], lhsT=rv[:, k, cch * P:(cch + 1) * P], rhs=actT[:, k, :], start=(k == 0), stop=(k == 31))
                return ins
            op('tensor', mmg, reads=actn + [rgn], writes=[psn[bset], psn[bset + 1]])
            op('tensor', lambda e, rv=ruv, bset=bset: mmg(e, rv, bset, 2), reads=actn + [run], writes=[psn[bset + 2], psn[bset + 3]])
            for cch in range(2):
                fc = fb * 2 + cch
                si = c2['sg'] % 2
                c2['sg'] += 1
                s_t = sg[si]
                bg, bu = ps[bset + cch], ps[bset + 2 + cch]
                op('scalar', lambda e, s_t=s_t, bg=bg: e.activation(out=s_t[:], in_=bg, func=AF.Silu), reads=[psn[bset + cch]], writes=[s_t.name])
                op('vector', lambda e, s_t=s_t, bu=bu, fc=fc: e.tensor_tensor(out=aT[:, fc, :], in0=bu, in1=s_t[:], op=ALU.mult),
                   reads=[psn[bset + 2 + cch], s_t.name], writes=[aTn[fc]])
        for db in range(8):
            bset = (db % 2) * 4
            ngr = (NFC + 15) // 16
            for kg in range(ngr):
                nk = min(16, NFC - kg * 16)
                if STAGE_BF16:
                    rt, rn = ring2_load(lambda t, nk=nk: t[:, 0:nk * 512], wd_s[(db * 6 + kg) * P:(db * 6 + kg + 1) * P, 0:nk * 512])
                else:
                    rt, rn = ring2_load(lambda t, nk=nk: v16(t)[:, 0:nk, :],
                                        w_down[kg * 2048:kg * 2048 + nk * P, db * 512:(db + 1) * 512].rearrange("(k p) n -> p k n", p=P))

                def mmd(e, rt=v16(rt), kg=kg, nk=nk, bset=bset, ngr=ngr):
                    ins = None
                    for s4 in range(4):
                        for k in range(nk):
                            ins = e.matmul(ps[bset + s4], lhsT=aT[:, kg * 16 + k, s4 * P:(s4 + 1) * P], rhs=rt[:, k, :],
                                           start=(kg == 0 and k == 0), stop=(kg == ngr - 1 and k == nk - 1))
                    return ins
                op('tensor', mmd, reads=aTn[kg * 16:kg * 16 + nk] + [rn], writes=[psn[bset + s4] for s4 in range(4)])
            for s4 in range(4):
                bk, bkn = ps[bset + s4], psn[bset + s4]
                op('scalar', lambda e, bk=bk, s4=s4, db=db: e.activation(out=junk_bf[:], in_=bk, func=AF.Square, accum_out=ssq[:, s4 * 8 + db:s4 * 8 + db + 1]),
                   reads=[bkn], writes=['junk_bf', 'ssq%d_%d' % (s4, db)])
                ci = c2['scr'] % 2
                c2['scr'] += 1
                f_t = scr[ci]
                op('vector', lambda e, bk=bk, f_t=f_t, db=db: e.tensor_tensor(out=f_t[:], in0=bk, in1=Grow[:, db * 512:(db + 1) * 512], op=ALU.mult),
                   reads=[bkn, 'Grow'], writes=[f_t.name])
                r0 = t0 + s4 * P
                op('sync', lambda e, f_t=f_t, r0=r0, db=db: e.dma_start(out=accs[r0:r0 + P, db * 512:(db + 1) * 512], in_=f_t[:]),
                   reads=[f_t.name], writes=['acc_%d_%d' % (tt * 4 + s4, db)], dma=f_t.name + 'st')
        for s4 in range(4):
            r0 = t0 + s4 * P
            a_t = zs[s4 % 2]
            an = 'zs%d' % (s4 % 2)
            op('sync', lambda e, a_t=a_t, r0=r0: e.dma_start(out=a_t[:], in_=accs[r0:r0 + P, :]),
               reads=['acc_%d_%d' % (tt * 4 + s4, db) for db in range(8)], writes=[an] + ['zs%d_%d' % (s4 % 2, db) for db in range(8)], dma=an + 'ld')
            op('sync', lambda e, r0=r0: e.dma_start(out=xs2[:], in_=x1s[r0:r0 + P, :]), reads=['x1s_%d' % (tt * 4 + s4)], writes=['xs2'], dma='xs2')
            ssn = ['ssq%d_%d' % (s4, db) for db in range(8)]
            st3 = stat[:, 16 + s4:17 + s4]
            op('vector', lambda e, s4=s4, st3=st3: e.tensor_reduce(out=st3, in_=ssq[:, s4 * 8:(s4 + 1) * 8], axis=mybir.AxisListType.X, op=ALU.add),
               reads=ssn, writes=['st3_%d' % s4])
            emit_rstd(st3, st3, 1.0 / D, ['st3_%d' % s4], 'st3_%d' % s4)
            op('vector', lambda e, a_t=a_t, st3=st3: e.scalar_tensor_tensor(out=xs2[:], in0=a_t[:], scalar=st3, in1=xs2[:], op0=ALU.mult, op1=ALU.add),
               reads=[an, 'xs2', 'st3_%d' % s4] + ['zs%d_%d' % (s4 % 2, db) for db in range(8)], writes=['xs2'])
            op('sync', lambda e, r0=r0: e.dma_start(out=out[r0:r0 + P, :], in_=xs2[:]), reads=['xs2'], writes=['out_%d' % (tt * 4 + s4)], dma='outst')

    if debug and stop_after == 3:
        pass
    return finish()


def col_layout(v):
    return np.ascontiguousarray(np.asarray(v, dtype=np.float32).reshape(-1, P).T)


def make_in_maps(inputs):
    x = np.asarray(inputs["x"], dtype=np.float32)
    c = np.asarray(inputs["c"], dtype=np.float32)
    w_in = np.asarray(inputs["w_in"], dtype=np.float32)[0]
    w_out = np.asarray(inputs["w_out"], dtype=np.float32)[0]
    conv_w = np.asarray(inputs["conv_w"], dtype=np.float32)[0]
    b_f = np.asarray(inputs["b_f"], dtype=np.float32)[0]
    aon = np.asarray(inputs["attn_out_norm"], dtype=np.float32)[0]
    con = np.asarray(inputs["conv_out_norm"], dtype=np.float32)[0]
    shared = {
        "w_ada": np.ascontiguousarray(np.asarray(inputs["w_ada"], dtype=np.float32)[0]),
        "b_ada": np.ascontiguousarray(np.asarray(inputs["b_ada"], dtype=np.float32)[0][None, :]),
        "pre1_col": col_layout(inputs["pre_norm_mix"][0]),
        "pre2_col": col_layout(inputs["pre_norm_ffn"][0]),
        "post1_row": np.ascontiguousarray(np.asarray(inputs["post_norm_mix"], dtype=np.float32)[0][None, :]),
        "post2_row": np.ascontiguousarray(np.asarray(inputs["post_norm_ffn"], dtype=np.float32)[0][None, :]),
        "w_gate": np.ascontiguousarray(np.asarray(inputs["w_gate"], dtype=np.float32)[0]),
        "w_up": np.ascontiguousarray(np.asarray(inputs["w_up"], dtype=np.float32)[0]),
        "w_down": np.ascontiguousarray(np.asarray(inputs["w_down"], dtype=np.float32)[0]),
    }
    mchunks = []
    lbmap = {0: (4, 5), 1: (6, 7), 2: (0, 1), 3: (2, 3)}
    for g_ in range(4):
        for r in range(4):
            for lb2 in range(2):
                lb = lbmap[g_][lb2]
                mchunks.append(4 * r + lb if lb < 4 else 16 + 4 * r + (lb - 4))
    gy_full = np.concatenate([aon, con])
    gy_perm = np.concatenate([gy_full[m * P:(m + 1) * P] for m in mchunks])
    shared["gy_col"] = col_layout(gy_perm)
    shared["w_out_perm"] = np.ascontiguousarray(np.concatenate([w_out[m * P:(m + 1) * P] for m in mchunks], axis=0))
    maps = []
    for core in range(NCORES):
        b, g = divmod(core, 4)
        m = dict(shared)
        m["x_full"] = np.ascontiguousarray(x[b])
        m["x_chunk"] = np.ascontiguousarray(x[b, g * TOK2:(g + 1) * TOK2])
        m["c_col"] = col_layout(c[b])
        sl = lambda base: w_in[:, base + 512 * g: base + 512 * g + 512]
        wq, wk, wv = sl(0), sl(2048), sl(4096)
        wf = w_in[:, 6144 + 4 * g: 6144 + 4 * g + 4]
        wgb, wgc, wu = sl(6160), sl(6160 + 2048), sl(6160 + 4096)
        m["w1"] = np.ascontiguousarray(np.concatenate([wq, wk, wgb, wgc, wu, wv, wf], axis=1))
        m["wf_col"] = np.ascontiguousarray(wf.reshape(32, P, 4).transpose(1, 0, 2).reshape(P, 128))
        m["bf_row"] = np.ascontiguousarray(b_f[4 * g:4 * g + 4][None, :])
        cw = conv_w[:, 512 * g:512 * g + 512]
        m["convw_col"] = np.ascontiguousarray(cw.reshape(3, 4, P).transpose(2, 1, 0).reshape(P, 12))
        maps.append(m)
    return maps


_NC_CACHE = {}


def kernel(**inputs):
    if "nc" not in _NC_CACHE:
        _NC_CACHE["nc"] = build_nc()
    nc = _NC_CACHE["nc"]
    in_maps = make_in_maps(inputs)
    res = run_bass_kernel_spmd(nc, in_maps, core_ids=list(range(NCORES)))
    outp = np.empty((2, S, D), dtype=np.float32)
    for core in range(NCORES):
        b, g = divmod(core, 4)
        outp[b, g * TOK2:(g + 1) * TOK2] = np.asarray(res.results[core]["out"])
    return outp
```

```python
import os
import numpy as np
import concourse.bass as bass
import concourse.mybir as mybir
from concourse.bass_utils import run_bass_kernel_spmd

F32 = mybir.dt.float32
BF16 = mybir.dt.bfloat16
AF = mybir.ActivationFunctionType
ALU = mybir.AluOpType

D = 4096
S = 8192
NCORES = 8
DFF = 11008
NFC = DFF // 128
EPS = 1e-6
P = 128
TT1 = 1024
NT1 = S // TT1
TT2 = 512
TOK2 = 2048
NT2 = TOK2 // TT2
STAGE_BF16 = True
NW1 = 3072 + 4

ENG = ['sync', 'scalar', 'vector', 'gpsimd', 'tensor']
COMPUTE = ['scalar', 'vector', 'gpsimd', 'tensor']


class Tr:
    def __init__(s, nc):
        s.nc = nc
        s.prog = {e: [] for e in ENG}
        s.esem = {e: [nc.alloc_semaphore("prog_" + e), 0] for e in COMPUTE}
        s.dsem = {}
        s.res = {}
        s.waited = {e: {} for e in ENG}

    def _handle(s, semkey):
        kind, k = semkey
        return s.esem[k][0] if kind == 'e' else s.dsem[k][0]

    def _need(s, e, toks):
        best = {}
        for (semkey, v) in toks:
            if semkey == ('e', 'tensor') and e == 'tensor':
                continue
            if v > best.get(semkey, 0):
                best[semkey] = v
        for semkey, v in best.items():
            if s.waited[e].get(semkey, 0) >= v:
                continue
            s.waited[e][semkey] = v
            h = s._handle(semkey)
            s.prog[e].append(lambda eng, h=h, v=v: eng.wait_ge(h, v))

    def op(s, e, fn, reads=(), writes=(), dma=None):
        writes = list(writes) + [r for r in reads if r.startswith('bk') and r not in writes]
        toks = []
        for r in reads:
            st = s.res.get(r)
            if st and st[0]:
                toks.append(st[0])
        for w in writes:
            st = s.res.get(w)
            if st:
                if st[0]:
                    toks.append(st[0])
                toks.extend(st[1].items())
        s._need(e, toks)
        if dma is not None:
            if dma not in s.dsem:
                s.dsem[dma] = [s.nc.alloc_semaphore("d_" + str(dma)), 0]
            d = s.dsem[dma]
            d[1] += 16
            tok = (('d', dma), d[1])
            h, inc = d[0], 16
        else:
            d = s.esem[e]
            d[1] += 1
            tok = (('e', e), d[1])
            h, inc = d[0], 1
        s.prog[e].append(lambda eng, fn=fn, h=h, inc=inc: fn(eng).then_inc(h, inc))
        for w in writes:
            s.res[w] = [tok, {}]
        for r in reads:
            st = s.res.setdefault(r, [None, {}])
            if tok[1] > st[1].get(tok[0], 0):
                st[1][tok[0]] = tok[1]
        return tok

    def barrier(s):
        toks = [(('e', f), s.esem[f][1]) for f in COMPUTE if s.esem[f][1] > 0]
        toks += [(('d', k), v[1]) for k, v in s.dsem.items()]
        for e in ENG:
            for semkey, v in toks:
                if s.waited[e].get(semkey, 0) >= v:
                    continue
                s.waited[e][semkey] = v
                h = s._handle(semkey)
                s.prog[e].append(lambda eng, h=h, v=v: eng.wait_ge(h, v))
        s.res = {}

    def coll(s, fn, reads=(), writes=()):
        e = 'gpsimd'
        toks = []
        for r in reads:
            st = s.res.get(r)
            if st and st[0]:
                toks.append(st[0])
        for w in writes:
            st = s.res.get(w)
            if st:
                if st[0]:
                    toks.append(st[0])
                toks.extend(st[1].items())
        s._need(e, toks)
        if 'cc' not in s.dsem:
            s.dsem['cc'] = [s.nc.alloc_semaphore("cc"), 0]
        d = s.dsem['cc']
        d[1] += 1
        tok = (('d', 'cc'), d[1])
        h = d[0]
        s.prog[e].append(lambda eng: fn(eng).then_inc(h))
        for w in writes:
            s.res[w] = [tok, {}]
        return tok

    def emit(s, block):
        def mk(e):
            def body(eng):
                for fn in s.prog[e]:
                    fn(eng)
            return body
        block.sync(mk('sync'))
        block.scalar(mk('scalar'))
        block.vector(mk('vector'))
        block.gpsimd(mk('gpsimd'))
        block.tensor(mk('tensor'))


def build_nc(stop_after=9, debug=False):
    nc = bass.Bass("TRN2", target_bir_lowering=False)

    def din(name, shape, dt=F32):
        return nc.dram_tensor(name, shape, dt, kind="ExternalInput").ap()

    x_full = din("x_full", [S, D])
    x_chunk = din("x_chunk", [TOK2, D])
    c_col_in = din("c_col", [P, 32])
    w_ada = din("w_ada", [D, 6 * D])
    b_ada = din("b_ada", [1, 6 * D])
    pre1_in = din("pre1_col", [P, 32])
    pre2_in = din("pre2_col", [P, 32])
    post1_in = din("post1_row", [1, D])
    post2_in = din("post2_row", [1, D])
    w1 = din("w1", [D, NW1])
    bf_in = din("bf_row", [1, 4])
    wf_in = din("wf_col", [P, 128])
    convw_in = din("convw_col", [P, 12])
    gy_in = din("gy_col", [P, 32])
    w_out = din("w_out_perm", [D, D])
    w_gate = din("w_gate", [D, DFF])
    w_up = din("w_up", [D, DFF])
    w_down = din("w_down", [DFF, D])
    out = nc.dram_tensor("out", [TOK2, D], F32, kind="ExternalOutput").ap()
    if debug:
        dbg = nc.dram_tensor("dbg", [P, 8192], F32, kind="ExternalOutput").ap()

    Gs = nc.dram_tensor("Gs", [2, D], F32)
    qTs = nc.dram_tensor("qTs", [4 * P, S], BF16)
    kTs = nc.dram_tensor("kTs", [4 * P, S], BF16)
    Vs = nc.dram_tensor("Vs", [S, 512], BF16)
    ybuf = nc.dram_tensor("ybuf", [16384, 512], BF16)
    yslots = nc.dram_tensor("yslots", [16384, 512], BF16)
    ymine = nc.dram_tensor("ymine", [16384, 512], BF16)
    wd_s = nc.dram_tensor("wd_s", [48 * P, 8192], BF16)
    bar_in = nc.dram_tensor("bar_in", [16, 512], BF16)
    bar_out = nc.dram_tensor("bar_out", [64, 512], BF16)
    x1s = nc.dram_tensor("x1s", [TOK2, D], F32)
    accs = nc.dram_tensor("accs", [TOK2, D], F32)

    BASE = 16512
    LIMIT = 229344

    class Arena:
        def __init__(self, start):
            self.off = start
            self.n = 0

        def alloc(self, shape, dt, name=None):
            nbytes = int(np.prod(shape[1:])) * (4 if dt == F32 else 2)
            nbytes = (nbytes + 63) // 64 * 64
            self.n += 1
            t = nc.alloc_sbuf_tensor_at(name or ("t%d_%d" % (self.off, self.n)), list(shape), dt, offset=self.off)
            self.off += nbytes
            assert self.off <= LIMIT, (self.off, LIMIT)
            return t

    A0 = Arena(BASE)
    ident_bf = A0.alloc([P, P], BF16, "ident_bf")
    ident_f = A0.alloc([P, P], F32, "ident_f")
    tri_bf = A0.alloc([P, P], BF16, "tri_bf")
    tri_f = A0.alloc([P, P], F32, "tri_f")
    ones_bf = A0.alloc([P, P], BF16, "ones_bf")
    ones_f = A0.alloc([P, P], F32, "ones_f")
    gain1c = A0.alloc([P, 32], F32, "gain1c")
    shift1c = A0.alloc([P, 32], F32, "shift1c")
    gain2c = A0.alloc([P, 32], F32, "gain2c")
    shift2c = A0.alloc([P, 32], F32, "shift2c")
    sc1c = A0.alloc([P, 32], F32, "sc1c")
    sc2c = A0.alloc([P, 32], F32, "sc2c")
    pre1c = A0.alloc([P, 32], F32, "pre1c")
    pre2c = A0.alloc([P, 32], F32, "pre2c")
    gyc = A0.alloc([P, 32], F32, "gyc")
    ccol = A0.alloc([P, 32], F32, "ccol")
    cact = A0.alloc([P, 32], F32, "cact")
    SP = A0.alloc([P, 256], F32, "SP")
    wf_sb = A0.alloc([P, 32, 4], BF16, "wf_sb")
    bfrep = A0.alloc([P, 32], F32, "bfrep")
    convw = A0.alloc([P, 12], F32, "convw")
    halo = A0.alloc([P, 4, 2], F32, "halo")
    stat = A0.alloc([P, 64], F32, "stat")
    junk_bf = A0.alloc([P, 512], BF16, "junk_bf")
    junk_f = A0.alloc([P, P], F32, "junk_f")
    P0 = BASE + 8192
    assert A0.off <= P0, A0.off

    pp = [nc.alloc_psum_tensor("pp%d" % i, [P, 1024], F32) for i in range(4)]
    ps = [pp[i // 2][:, (i % 2) * 512:(i % 2 + 1) * 512] for i in range(8)]
    psn = ['bk%d' % i for i in range(8)]

    T = Tr(nc)
    op = T.op

    def mk_const(t, val, cmp_op=None):
        op('gpsimd', lambda e: e.memset(t[:], val), writes=[t.name])
        if cmp_op is not None:
            op('gpsimd', lambda e: e.affine_select(out=t[:], in_=t[:], pattern=[[1, P]], compare_op=cmp_op,
                                                   fill=0.0, base=0, channel_multiplier=-1),
               reads=[t.name], writes=[t.name])

    mk_const(ident_bf, 1.0, ALU.is_equal)
    mk_const(ident_f, 1.0, ALU.is_equal)
    mk_const(tri_bf, 1.0, ALU.is_ge)
    mk_const(tri_f, 1.0, ALU.is_ge)
    mk_const(ones_bf, 1.0)
    mk_const(ones_f, 1.0)
    op('gpsimd', lambda e: e.memset(halo[:], 0.0), writes=['halo'])

    def small_load(dst, src, key, name):
        op('sync', lambda e: e.dma_start(out=dst, in_=src), writes=[name], dma=key)

    small_load(ccol[:], c_col_in[:, :], 'm0', 'ccol')
    small_load(pre1c[:], pre1_in[:, :], 'm1', 'pre1c')
    small_load(pre2c[:], pre2_in[:, :], 'm2', 'pre2c')
    small_load(gyc[:], gy_in[:, :], 'm3', 'gyc')
    small_load(convw[:], convw_in[:, :], 'm4', 'convw')
    for s8 in range(8):
        small_load(bfrep[:, s8 * 4:(s8 + 1) * 4], bf_in[0:1, :].partition_broadcast(P), 'm5', 'bfrep%d' % s8)
    op('gpsimd', lambda e: e.dma_start(out=wf_sb[:].rearrange("p k n -> p (k n)"), in_=wf_in[:, :]), writes=['wf_sb'], dma='m6')

    def emit_rstd(dst, src, mul, rname, wname):
        op('vector', lambda e: e.tensor_scalar(out=dst, in0=src, scalar1=mul, scalar2=EPS, op0=ALU.mult, op1=ALU.add),
           reads=rname, writes=[wname])
        op('scalar', lambda e: e.activation(out=dst, in_=dst, func=AF.Sqrt), reads=[wname], writes=[wname])
        op('vector', lambda e: e.reciprocal(out=dst, in_=dst), reads=[wname], writes=[wname])

    A = Arena(P0)
    cB = A.alloc([P, 32, P], BF16, "cB")
    ring0 = [A.alloc([P, 16, 512], BF16, "ring0_%d" % i) for i in range(3)]
    bt = [A.alloc([P, 512], F32, "bt%d" % i) for i in range(2)]
    postr = [A.alloc([P, 512], F32, "postr%d" % i) for i in range(2)]
    modblk = [A.alloc([P, 512], F32, "modblk%d" % i) for i in range(2)]
    gblk = [A.alloc([P, 512], F32, "gblk%d" % i) for i in range(2)]

    op('scalar', lambda e: e.activation(out=cact[:], in_=ccol[:], func=AF.Silu), reads=['ccol'], writes=['cact'])
    for k in range(32):
        op('vector', lambda e, k=k: e.tensor_scalar(out=cB[:, k, :], in0=ones_bf[:], scalar1=cact[:, k:k + 1],
                                                    scalar2=None, op0=ALU.mult),
           reads=['cact', 'ones_bf'], writes=['cB%d' % k])
    cBn = ['cB%d' % k for k in range(32)]

    rcount = [0]

    cv_jobs = []
    for db in range(8):
        for kg in range(6):
            nk = min(16, NFC - kg * 16)
            blk = db * 6 + kg
            cv_jobs.append((wd_s[blk * P:(blk + 1) * P, 0:nk * 512].rearrange("p (k n) -> p k n", n=512),
                            w_down[kg * 2048:kg * 2048 + nk * P, db * 512:(db + 1) * 512].rearrange("(k p) n -> p k n", p=P)))
    cv_pos = [0]

    def cv_issue(n=1):
        if not STAGE_BF16:
            return
        for _ in range(n):
            if cv_pos[0] >= len(cv_jobs):
                return
            dst, src = cv_jobs[cv_pos[0]]
            key = 'cv%d' % (cv_pos[0] % 4)
            cv_pos[0] += 1
            op('gpsimd', lambda e, dst=dst, src=src: e.dma_start(out=dst, in_=src), writes=['cvs_' + key], dma=key)

    def ring_load(rings, view_fn, src_ap, nslots, cv=0):
        i = rcount[0] % nslots
        rcount[0] += 1
        rn = 'ring%d' % i
        dst = view_fn(rings[i])
        op('gpsimd', lambda e: e.dma_start(out=dst, in_=src_ap), writes=[rn], dma=rn)
        cv_issue(cv)
        return rings[i], rn

    seg_cols = {0: shift1c, 1: sc1c, 3: shift2c, 4: sc2c}

    def mod_load(cb, rings, nslots):
        c0 = cb * 512
        return [ring_load(rings, lambda t: t[:], w_ada[kg * 2048:(kg + 1) * 2048, c0:c0 + 512].rearrange("(k p) n -> p k n", p=P), nslots)
                for kg in range(2)]

    def mod_block(cb, rings, nslots, cBt, pb, pbn, b_t, m_t, p_t, g_t, loaded=None):
        seg = cb // 8
        c0 = cb * 512
        cBn_ = ['cB%d' % k for k in range(32)]
        if loaded is None:
            loaded = mod_load(cb, rings, nslots)
        for kg in range(2):
            rt, rn = loaded[kg]

            def mm(e, rt=rt, kg=kg, pb=pb):
                ins = None
                for k in range(16):
                    ins = e.matmul(pb, lhsT=cBt[:, kg * 16 + k, :], rhs=rt[:, k, :],
                                   start=(kg == 0 and k == 0), stop=(kg == 1 and k == 15))
                return ins
            op('tensor', mm, reads=[rn] + cBn_, writes=[pbn])
        op('sync', lambda e, b_t=b_t, c0=c0: e.dma_start(out=b_t[:], in_=b_ada[0:1, c0:c0 + 512].partition_broadcast(P)),
           writes=[b_t.name], dma=b_t.name)
        op('vector', lambda e, m_t=m_t, pb=pb, b_t=b_t: e.tensor_tensor(out=m_t[:], in0=pb, in1=b_t[:], op=ALU.add),
           reads=[pbn, b_t.name], writes=[m_t.name])
        if seg in seg_cols:
            colt = seg_cols[seg]
            for i in range(4):
                cidx = (cb % 8) * 4 + i
                op('vector', lambda e, m_t=m_t, i=i, colt=colt, cidx=cidx: e.scalar_tensor_tensor(
                    out=junk_f[:], in0=m_t[:, i * P:(i + 1) * P], scalar=1.0, in1=ident_f[:],
                    op0=ALU.mult, op1=ALU.mult, accum_out=colt[:, cidx:cidx + 1]),
                   reads=[m_t.name], writes=['junk_f', colt.name + str(cidx)])
        else:
            which = 0 if seg == 2 else 1
            prow = post1_in if which == 0 else post2_in
            cc0 = (cb % 8) * 512
            op('sync', lambda e, p_t=p_t, prow=prow, cc0=cc0: e.dma_start(
                out=p_t[:], in_=prow[0:1, cc0:cc0 + 512].partition_broadcast(P)), writes=[p_t.name], dma=p_t.name)
            op('vector', lambda e, g_t=g_t, m_t=m_t, p_t=p_t: e.tensor_tensor(out=g_t[:], in0=m_t[:], in1=p_t[:], op=ALU.mult),
               reads=[m_t.name, p_t.name], writes=[g_t.name])
            op('sync', lambda e, g_t=g_t, which=which, cc0=cc0: e.dma_start(out=Gs[which:which + 1, cc0:cc0 + 512], in_=g_t[0:1, :]),
               reads=[g_t.name], writes=['Gs%d_%d' % (which, cb % 8)], dma=g_t.name + 'st')

    for cb in range(16):
        mod_block(cb, ring0, 3, cB, ps[cb % 2], psn[cb % 2], bt[cb % 2], modblk[cb % 2], postr[cb % 2], gblk[cb % 2])
    allc = lambda t: [t.name + str(i) for i in range(32)]
    op('vector', lambda e: e.scalar_tensor_tensor(out=gain1c[:], in0=sc1c[:], scalar=1.0, in1=pre1c[:], op0=ALU.add, op1=ALU.mult),
       reads=allc(sc1c) + ['pre1c'], writes=['gain1c'])
    shift1n = allc(shift1c)
    shift2n = allc(shift2c)
    GsN = [['Gs%d_%d' % (w, i) for i in range(8)] for w in range(2)]

    def dbg_dump(items, reads):
        dt_ = nc.alloc_sbuf_tensor_at("dbgt", [P, 8192], F32, offset=LIMIT - 8192 * 4 - 64)
        op('vector', lambda e: e.memset(dt_[:], 0.0), writes=['dbgt'])
        for (off, n, src) in items:
            op('vector', lambda e, off=off, n=n, src=src: e.tensor_copy(out=dt_[:, off:off + n], in_=src),
               reads=list(reads) + ['dbgt'], writes=['dbgt'])
        op('sync', lambda e: e.dma_start(out=dbg[:, :], in_=dt_[:]), reads=['dbgt'], dma='dbgs')

    def finish():
        T.barrier()
        with nc.Block() as block:
            T.emit(block)
        return nc

    if stop_after == 0:
        if debug:
            dbg_dump([(0, 32, gain1c[:]), (32, 32, shift1c[:]), (64, 32, gain2c[:]), (96, 32, shift2c[:])],
                     ['gain1c', 'gain2c'] + shift1n + shift2n)
        return finish()

    T.barrier()
    A = Arena(P0)
    hT = A.alloc([P, 32, TT1], BF16, "hT")
    ring1 = [A.alloc([P, 32, 256], BF16, "ring1_%d" % i) for i in range(3)]
    xs = [A.alloc([P, D], F32, "xs%d" % i) for i in range(2)]
    xn = [A.alloc([P, D], BF16, "xn%d" % i) for i in range(2)]
    qst = [A.alloc([P, TT1], BF16, "qst%d" % i) for i in range(2)]
    vst = A.alloc([P, 8, 256], BF16, "vst")
    gb_sb = A.alloc([P, 2, TT1], F32, "gb_sb")
    gc_sb = A.alloc([P, 2, TT1], F32, "gc_sb")
    vbuf = A.alloc([P, 2, TT1 + 2], F32, "vbuf")
    ysb = [A.alloc([P, TT1], BF16, "ysb%d" % i) for i in range(2)]
    fsm = A0.alloc([P, 32], F32, "fsm")

    hTn = ['hT%d' % k for k in range(32)]
    ybn = []
    tbf = pp[0][:].bitcast(BF16)
    QSCALE = 1.0 / float(np.sqrt(128.0))
    blocks = [('q', 0, 0), ('q', 256, 1), ('k', 512, 0), ('k', 768, 1),
              ('gb', 1024, 0), ('gc', 1536, 0), ('u', 2048, 0),
              ('gb', 1280, 1), ('gc', 1792, 1), ('u', 2304, 1),
              ('v', 2560, 0), ('v', 2816, 1)]
    cnt = {'xs': 0, 'st': 0, 'pair': 0, 'bank': 0, 'qst': 0, 'ysb': 0, 'ev': 0}
    NT1_RUN = NT1 if stop_after != 1 or not debug else 2
    if int(os.environ.get('K_CUT', '99')) == 0:
        NT1_RUN = 0

    for ti in range(NT1_RUN):
        for s8 in range(8):
            i = cnt['xs'] % 2
            cnt['xs'] += 1
            xt, xnt = xs[i], xn[i]
            r0 = ti * TT1 + s8 * P
            op('sync', lambda e, xt=xt, r0=r0: e.dma_start(out=xt[:], in_=x_full[r0:r0 + P, :]), writes=[xt.name], dma=xt.name)
            sc = cnt['st'] % 32
            cnt['st'] += 1
            stc = stat[:, sc:sc + 1]
            stn = 'stat%d' % sc
            op('scalar', lambda e, xt=xt, xnt=xnt, stc=stc: e.activation(out=xnt[:], in_=xt[:], func=AF.Square, accum_out=stc),
               reads=[xt.name], writes=[xnt.name, stn])
            emit_rstd(stc, stc, 1.0 / D, [stn], stn)
            op('vector', lambda e, xt=xt, xnt=xnt, stc=stc: e.tensor_scalar(out=xnt[:], in0=xt[:], scalar1=stc, scalar2=None, op0=ALU.mult),
               reads=[xt.name, stn], writes=[xnt.name])
            for g4 in range(4):
                half = g4 % 2
                tv = tbf[:, half * 1024:(half + 1) * 1024]
                bn = psn[half]

                def tp(e, xnt=xnt, g4=g4, tv=tv):
                    ins = None
                    for kk in range(8):
                        k = g4 * 8 + kk
                        ins = e.transpose(out=tv[:, kk * P:(kk + 1) * P], in_=xnt[:, k * P:(k + 1) * P], identity=ident_bf[:])
                    return ins
                op('tensor', tp, reads=[xnt.name], writes=[bn])

                def ev_act(e, g4=g4, tv=tv, s8=s8):
                    ins = None
                    for kk in range(8):
                        k = g4 * 8 + kk
                        ins = e.activation(out=hT[:, k, s8 * P:(s8 + 1) * P], in_=tv[:, kk * P:(kk + 1) * P], func=AF.Identity,
                                           scale=gain1c[:, k:k + 1], bias=shift1c[:, k:k + 1])
                    return ins

                def ev_dve(e, g4=g4, tv=tv, s8=s8):
                    ins = None
                    for kk in range(8):
                        k = g4 * 8 + kk
                        ins = e.tensor_scalar(out=hT[:, k, s8 * P:(s8 + 1) * P], in0=tv[:, kk * P:(kk + 1) * P],
                                              scalar1=gain1c[:, k:k + 1], scalar2=shift1c[:, k:k + 1], op0=ALU.mult, op1=ALU.add)
                    return ins
                if g4 % 2 == 0:
                    op('scalar', ev_act, reads=[bn, 'gain1c'] + shift1n, writes=[hTn[g4 * 8 + kk] for kk in range(8)])
                else:
                    op('vector', ev_dve, reads=[bn, 'gain1c'] + shift1n, writes=[hTn[g4 * 8 + kk] for kk in range(8)])

        KCUT = int(os.environ.get("K_CUT", "99"))
        if KCUT <= 1:
            continue
        def mmf(e):
            ins = None
            for s8 in range(8):
                for k in range(32):
                    ins = e.matmul(ps[7][:, s8 * 4:(s8 + 1) * 4], lhsT=hT[:, k, s8 * P:(s8 + 1) * P], rhs=wf_sb[:, k, :],
                                   start=(k == 0), stop=(k == 31))
            return ins
        op('tensor', mmf, reads=hTn + ['wf_sb'], writes=[psn[7]])
        op('vector', lambda e: e.tensor_tensor(out=fsm[:], in0=ps[7][:, 0:32], in1=bfrep[:], op=ALU.add),
           reads=[psn[7]] + ['bfrep%d' % i for i in range(8)], writes=['fsm'])
        op('scalar', lambda e: e.activation(out=fsm[:], in_=fsm[:], func=AF.Exp, scale=-1.0), reads=['fsm'], writes=['fsm'])
        op('scalar', lambda e, ti=ti: e.activation(out=SP[:, ti * 32:(ti + 1) * 32], in_=fsm[:], func=AF.Ln, bias=1.0),
           reads=['fsm'], writes=['SP%d' % ti])

        for bidx, (kind, col0, idx) in enumerate(blocks):
            if bidx >= KCUT - 2:
                break
            rt, rn = ring_load(ring1, lambda t: t[:], w1[:, col0:col0 + 256].rearrange("(k p) n -> p k n", p=P), 3, cv=(1 if bidx % 4 == 1 else 0))
            if kind == 'v':
                for s8 in range(8):
                    bi = 2 + cnt['bank'] % 6
                    cnt['bank'] += 1

                    def mmv(e, rt=rt, s8=s8, bi=bi):
                        ins = None
                        for k in range(32):
                            ins = e.matmul(ps[bi][:, 0:256], lhsT=hT[:, k, s8 * P:(s8 + 1) * P], rhs=rt[:, k, :],
                                           start=(k == 0), stop=(k == 31))
                        return ins
                    op('tensor', mmv, reads=hTn + [rn], writes=[psn[bi]])
                    eng = 'scalar' if s8 % 2 == 0 else 'vector'
                    if eng == 'scalar':
                        op('scalar', lambda e, s8=s8, bi=bi: e.activation(out=vst[:, s8, :], in_=ps[bi][:, 0:256], func=AF.Copy),
                           reads=[psn[bi]], writes=['vst%d' % s8])
                    else:
                        op('vector', lambda e, s8=s8, bi=bi: e.tensor_copy(out=vst[:, s8, :], in_=ps[bi][:, 0:256]),
                           reads=[psn[bi]], writes=['vst%d' % s8])
                op('sync', lambda e, ti=ti, idx=idx: e.dma_start(
                    out=Vs[ti * TT1:(ti + 1) * TT1, idx * 256:(idx + 1) * 256].rearrange("(s p) c -> p s c", p=P), in_=vst[:]),
                   reads=['vst%d' % i for i in range(8)], writes=['Vs_%d_%d' % (ti, idx)], dma='vst')
                continue
            for cch in range(2):
                pi = 1 + cnt['pair'] % 3
                cnt['pair'] += 1
                pair = pp[pi]
                pn = [psn[2 * pi], psn[2 * pi + 1]]

                def mmq(e, rt=rt, cch=cch, pair=pair):
                    ins = None
                    for k in range(32):
                        for half in range(2):
                            ins = e.matmul(pair[:, half * 512:(half + 1) * 512], lhsT=rt[:, k, cch * P:(cch + 1) * P],
                                           rhs=hT[:, k, half * 512:(half + 1) * 512], start=(k == 0), stop=(k == 31))
                    return ins
                op('tensor', mmq, reads=hTn + [rn], writes=pn)
                if kind in ('q', 'k'):
                    hl = idx * 2 + cch
                    qi = cnt['qst'] % 2
                    cnt['qst'] += 1
                    qt = qst[qi]
                    dst = qTs if kind == 'q' else kTs
                    if kind == 'q':
                        op('scalar', lambda e, qt=qt, pair=pair: e.activation(out=qt[:], in_=pair[:], func=AF.Copy, scale=QSCALE),
                           reads=pn, writes=[qt.name])
                    else:
                        op('vector', lambda e, qt=qt, pair=pair: e.tensor_copy(out=qt[:], in_=pair[:]), reads=pn, writes=[qt.name])
                    op('sync', lambda e, qt=qt, dst=dst, hl=hl, ti=ti: e.dma_start(
                        out=dst[hl * P:(hl + 1) * P, ti * TT1:(ti + 1) * TT1], in_=qt[:]),
                       reads=[qt.name], writes=['%sTs_%d_%d' % (kind, hl, ti)], dma=qt.name)
                elif kind == 'gb':
                    op('scalar', lambda e, cch=cch, pair=pair: e.activation(out=gb_sb[:, cch, :], in_=pair[:], func=AF.Copy),
                       reads=pn, writes=['gb%d' % cch])
                elif kind == 'gc':
                    op('vector', lambda e, cch=cch, pair=pair: e.tensor_copy(out=gc_sb[:, cch, :], in_=pair[:]),
                       reads=pn, writes=['gc%d' % cch])
                else:
                    cc = idx * 2 + cch
                    vn = 'vb%d' % cch
                    hn = 'halo%d' % cc
                    gcn = 'gc%d' % cch
                    yi = cnt['ysb'] % 2
                    cnt['ysb'] += 1
                    yt = ysb[yi]
                    op('vector', lambda e, cch=cch, cc=cc: e.tensor_copy(out=vbuf[:, cch, 0:2], in_=halo[:, cc, :]),
                       reads=['halo', hn], writes=[vn])
                    op('vector', lambda e, cch=cch, pair=pair: e.tensor_tensor(out=vbuf[:, cch, 2:2 + TT1], in0=pair[:], in1=gc_sb[:, cch, :], op=ALU.mult),
                       reads=pn + [gcn, vn], writes=[vn])
                    op('vector', lambda e, cch=cch, cc=cc: e.tensor_copy(out=halo[:, cc, :], in_=vbuf[:, cch, TT1:TT1 + 2]),
                       reads=[vn], writes=[hn])
                    op('vector', lambda e, cch=cch, cc=cc: e.tensor_scalar(out=gc_sb[:, cch, :], in0=vbuf[:, cch, 2:2 + TT1],
                                                                        scalar1=convw[:, cc * 3 + 2:cc * 3 + 3], scalar2=None, op0=ALU.mult),
                       reads=[vn, 'convw'], writes=[gcn])
                    op('vector', lambda e, cch=cch, cc=cc: e.scalar_tensor_tensor(out=gc_sb[:, cch, :], in0=vbuf[:, cch, 1:1 + TT1],
                                                                               scalar=convw[:, cc * 3 + 1:cc * 3 + 2], in1=gc_sb[:, cch, :],
                                                                               op0=ALU.mult, op1=ALU.add),
                       reads=[vn, gcn], writes=[gcn])
                    op('vector', lambda e, cch=cch, cc=cc: e.scalar_tensor_tensor(out=gc_sb[:, cch, :], in0=vbuf[:, cch, 0:TT1],
                                                                               scalar=convw[:, cc * 3:cc * 3 + 1], in1=gc_sb[:, cch, :],
                                                                               op0=ALU.mult, op1=ALU.add),
                       reads=[vn, gcn], writes=[gcn])
                    op('vector', lambda e, cch=cch, yt=yt: e.tensor_tensor(out=yt[:], in0=gc_sb[:, cch, :], in1=gb_sb[:, cch, :], op=ALU.mult),
                       reads=[gcn, 'gb%d' % cch], writes=[yt.name])
                    for hf in range(2):
                        rrow = ((((ti // 2) * 4 + cc // 2) * 4 + (ti % 2) * 2 + hf) * 2 + cc % 2) * P
                        yn_ = 'yb_c_%d_%d_%d' % (ti, cc, hf)
                        ybn.append(yn_)
                        op('sync', lambda e, yt=yt, rrow=rrow, hf=hf: e.dma_start(out=ybuf[rrow:rrow + P, :], in_=yt[:, hf * 512:(hf + 1) * 512]),
                           reads=[yt.name], writes=[yn_], dma=yt.name + '_%d' % hf)

    if stop_after == 1:
        if debug:
            T.barrier()
            dt_ = nc.alloc_sbuf_tensor_at("dbgt", [P, 8192], F32, offset=LIMIT - 8192 * 4 - 64)
            op('vector', lambda e: e.memset(dt_[:], 0.0), writes=['dbgt'])
            op('vector', lambda e: e.tensor_copy(out=dt_[:, 0:256], in_=SP[:]), reads=['dbgt'], writes=['dbgt'])
            op('vector', lambda e: e.tensor_copy(out=dt_[:, 256:384], in_=hT[:, 0, 896:1024]), reads=['dbgt'], writes=['dbgt'])
            op('vector', lambda e: e.tensor_copy(out=dt_[:, 384:512], in_=hT[:, 31, 0:128]), reads=['dbgt'], writes=['dbgt'])
            qd = nc.alloc_sbuf_tensor_at("qd", [P, 4, 1024], BF16, offset=LIMIT - 8192 * 4 - 64 - 8192)
            op('sync', lambda e: e.dma_start(out=qd[:, 0, :], in_=qTs[0:P, 0:1024]), writes=['qd'], dma='dq0')
            op('sync', lambda e: e.dma_start(out=qd[:, 1, :], in_=kTs[P:2 * P, 1024:2048]), writes=['qd1'], dma='dq1')
            op('sync', lambda e: e.dma_start(out=qd[:, 2, 0:512], in_=Vs[1024:1024 + P, 0:512]), writes=['qd2'], dma='dq2')
            op('sync', lambda e: e.dma_start(out=qd[:, 3, :], in_=ybuf[5 * P:6 * P, 0:1024]), writes=['qd3'], dma='dq3')
            op('vector', lambda e: e.tensor_copy(out=dt_[:, 1024:2048], in_=qd[:, 0, :]), reads=['qd', 'dbgt'], writes=['dbgt'])
            op('vector', lambda e: e.tensor_copy(out=dt_[:, 2048:3072], in_=qd[:, 1, :]), reads=['qd1', 'dbgt'], writes=['dbgt'])
            op('vector', lambda e: e.tensor_copy(out=dt_[:, 3072:3584], in_=qd[:, 2, 0:512]), reads=['qd2', 'dbgt'], writes=['dbgt'])
            op('vector', lambda e: e.tensor_copy(out=dt_[:, 4096:5120], in_=qd[:, 3, :]), reads=['qd3', 'dbgt'], writes=['dbgt'])
            op('sync', lambda e: e.dma_start(out=dbg[:, :], in_=dt_[:]), reads=['dbgt'], dma='dbgs')
        return finish()

    T.barrier()
    A = Arena(P0)
    qT = [A.alloc([P, S], BF16, "qT%d" % i) for i in range(2)]
    kT = [A.alloc([P, S], BF16, "kT%d" % i) for i in range(2)]
    Vt = [A.alloc([P, 64, P], BF16, "Vt%d" % i) for i in range(2)]
    Bm = [A.alloc([P, 64, 64], F32, "Bm%d" % i) for i in range(2)]
    PT = [A.alloc([P, 512], BF16, "PT%d" % i) for i in range(3)]
    rl = [A.alloc([P, 512], F32, "rl%d" % i) for i in range(2)]
    ost = [A.alloc([P, 512], BF16, "ost%d" % i) for i in range(2)]
    CumH = A.alloc([P, 4, 64], F32, "CumH")
    TotH = A.alloc([P, 4, 64], F32, "TotH")
    EndH = A.alloc([P, 4, 64], F32, "EndH")
    CumF = A.alloc([P, 4, 64], F32, "CumF")
    SPn = ['SP%d' % i for i in range(NT1)]
    cB2 = A.alloc([P, 32, P], BF16, "cB2")
    ringA = [A.alloc([P, 16, 512], BF16, "ringA_%d" % i) for i in range(2)]
    bt1 = A.alloc([P, 512], F32, "bt1")
    postr1 = A.alloc([P, 512], F32, "postr1")
    modblk1 = A.alloc([P, 512], F32, "modblk1")
    gblk1 = A.alloc([P, 512], F32, "gblk1")
    for k in range(32):
        op('vector', lambda e, k=k: e.tensor_scalar(out=cB2[:, k, :], in0=ones_bf[:], scalar1=cact[:, k:k + 1],
                                                    scalar2=None, op0=ALU.mult), writes=['cB%d' % k])

    pst = {}

    def gp_j(e):
        if 'j' not in pst:
            pst['j'] = e.partition_id() % 4
        return pst['j']

    def exchange_group(g, reads):
        for jp in range(4):
            r0 = (jp * 4 + g) * 1024
            T.coll(lambda e, r0=r0, jp=jp: e.collective_compute(
                "AllGather", ALU.bypass, replica_groups=[[0, 1, 2, 3], [4, 5, 6, 7]],
                ins=[ybuf[r0:r0 + 1024, :]], outs=[yslots[jp * 4096:(jp + 1) * 4096, :]]),
                reads=reads if jp == 0 else [], writes=['ysl%d' % jp, 'ccchain'])

        def cp(e, g=g):
            j = gp_j(e)
            return e.dma_start(out=ymine[g * 4096:(g + 1) * 4096, :], in_=yslots[bass.ds(j * 4096, 4096), :])
        op('gpsimd', cp, reads=['ysl%d' % i for i in range(4)], writes=['ym%d' % g], dma='ymc')
        T.coll(lambda e: e.collective_compute("AllGather", ALU.bypass, replica_groups=[[0, 1, 2, 3], [4, 5, 6, 7]],
                                              ins=[bar_in[:, :]], outs=[bar_out[:, :]]),
               reads=['ym%d' % g], writes=['ysl%d' % i for i in range(4)] + ['ccchain'])

    exchange_group(0, [])
    exchange_group(1, [])

    op('tensor', lambda e: e.matmul(ps[6][:, 0:256], lhsT=tri_f[:], rhs=SP[:], start=True, stop=True), reads=SPn + ['tri_f'], writes=[psn[6]])
    op('tensor', lambda e: e.matmul(ps[7][:, 0:256], lhsT=ones_f[:], rhs=SP[:], start=True, stop=True), reads=SPn + ['ones_f'], writes=[psn[7]])
    op('vector', lambda e: e.tensor_copy(out=CumH[:], in_=ps[6][:, 0:256].rearrange("p (kb h) -> p h kb", h=4)), reads=[psn[6]], writes=['CumH'])
    op('vector', lambda e: e.tensor_copy(out=TotH[:], in_=ps[7][:, 0:256].rearrange("p (kb h) -> p h kb", h=4)), reads=[psn[7]], writes=['TotH'])
    for h in range(4):
        op('vector', lambda e, h=h: e.tensor_tensor_scan(out=EndH[:, h, :], data0=ones_f[:, 0:64], data1=TotH[:, h, :], initial=0.0,
                                                         op0=ALU.mult, op1=ALU.add), reads=['TotH', 'ones_f'], writes=['EndH%d' % h])
    EndN = ['EndH%d' % h for h in range(4)]
    op('vector', lambda e: e.tensor_tensor(out=CumF[:], in0=CumH[:], in1=EndH[:], op=ALU.add), reads=['CumH'] + EndN, writes=['CumF'])
    op('vector', lambda e: e.tensor_tensor(out=CumF[:], in0=CumF[:], in1=TotH[:], op=ALU.subtract), reads=['CumF', 'TotH'], writes=['CumF'])

    cntb = {'s': 0, 'pt': 0, 'o': 0, 'ost': 0}
    NH_RUN = 4 if not (debug and stop_after == 2) else 1
    PTx = PT + [A.alloc([P, 512], BF16, "PT3")]
    SB = [0, 1, 6]
    LOOK = 2

    def head_setup(hl):
        sl = hl % 2
        q_t, k_t, v_t, b_t = qT[sl], kT[sl], Vt[sl], Bm[sl]
        op('sync', lambda e, q_t=q_t, hl=hl: e.dma_start(out=q_t[:], in_=qTs[hl * P:(hl + 1) * P, :]),
           reads=['qTs_%d_%d' % (hl, t) for t in range(NT1)], writes=[q_t.name], dma=q_t.name)
        op('sync', lambda e, k_t=k_t, hl=hl: e.dma_start(out=k_t[:], in_=kTs[hl * P:(hl + 1) * P, :]),
           reads=['kTs_%d_%d' % (hl, t) for t in range(NT1)], writes=[k_t.name], dma=k_t.name)
        op('sync', lambda e, v_t=v_t, hl=hl: e.dma_start(out=v_t[:], in_=Vs[:, hl * P:(hl + 1) * P].rearrange("(kb p) d -> p kb d", p=P)),
           reads=['Vs_%d_%d' % (t, hl // 2) for t in range(NT1)], writes=[v_t.name], dma=v_t.name)

        def mkB(e, b_t=b_t, hl=hl):
            ins = None
            for kb in range(64):
                ins = e.tensor_scalar(out=b_t[:, kb, :], in0=EndH[:, hl, :], scalar1=-1.0, scalar2=CumF[:, hl, kb:kb + 1],
                                      op0=ALU.mult, op1=ALU.add)
            return ins
        op('vector', mkB, reads=EndN + ['CumF'], writes=[b_t.name])

    blks = []
    for hl in range(NH_RUN):
        for qg in range(16):
            for kb in range(4 * qg + 4):
                blks.append((hl, qg, kb))
    info = {}

    def emit_S(i):
        hl, qg, kb = blks[i]
        sl = hl % 2
        q_t, k_t = qT[sl], kT[sl]
        c0 = max(0, kb - 4 * qg) * P
        bi = SB[i % 3]
        bS, bSn = ps[bi], psn[bi]
        op('tensor', lambda e, bS=bS, c0=c0, k_t=k_t, q_t=q_t, kb=kb, qg=qg: e.matmul(
            bS[:, c0:512], lhsT=k_t[:, kb * P:(kb + 1) * P], rhs=q_t[:, qg * 512 + c0:(qg + 1) * 512], start=True, stop=True),
           reads=[k_t.name, q_t.name], writes=[bSn])
        info[i] = (bS, bSn)

    head_setup(0)
    if NH_RUN > 1:
        head_setup(1)
    for i in range(min(LOOK, len(blks))):
        emit_S(i)
    mb_next = [16]
    mb_loaded = [None]
    i_ex2 = 2 * 544
    trig = set()
    if NH_RUN == 4:
        ra = list(range(600, i_ex2 - 20))
        rb = list(range(i_ex2 + 300, len(blks) - 40))
        na = 12
        nb_ = 20
        trig = set(ra[(t * len(ra)) // na] for t in range(na)) | set(rb[(t * len(rb)) // nb_] for t in range(nb_))
    for i, (hl, qg, kb) in enumerate(blks):
        sl = hl % 2
        v_t, b_t = Vt[sl], Bm[sl]
        nkb = 4 * qg + 4
        if i == 560 and NH_RUN == 4:
            mb_loaded[0] = mod_load(mb_next[0], ringA, 2)
        if i in trig and mb_next[0] < 48:
            mod_block(mb_next[0], ringA, 2, cB2, ps[7], psn[7], bt1, modblk1, postr1, gblk1, loaded=mb_loaded[0])
            mb_next[0] += 1
            cv_issue(2)
            mb_loaded[0] = mod_load(mb_next[0], ringA, 2) if mb_next[0] < 48 else None
        if NH_RUN == 4 and hl == 2 and qg == 0 and kb == 0:
            exchange_group(2, [n_ for n_ in ybn if n_.startswith('yb_a_0_') or n_.startswith('yb_a_1_')])
        if qg == 0 and kb == 0 and hl >= 1 and hl + 1 < NH_RUN:
            head_setup(hl + 1)
        if i + LOOK < len(blks):
            emit_S(i + LOOK)
        if kb == 0:
            oi = cntb['o'] % 2
            cntb['o'] += 1
            info['o'] = (ps[2 + oi], ps[4 + oi], psn[2 + oi], psn[4 + oi])
        bO, bL, bOn, bLn = info['o']
        bS, bSn = info.pop(i)
        j0 = max(0, kb - 4 * qg)
        c0 = j0 * P
        p_t = PTx[i % 4]

        def ex(e, bS=bS, p_t=p_t, b_t=b_t, j0=j0, kb=kb, qg=qg):
            ins = None
            for j in range(j0, 4):
                ins = e.activation(out=p_t[:, j * P:(j + 1) * P], in_=bS[:, j * P:(j + 1) * P], func=AF.Exp,
                                   bias=b_t[:, kb, 4 * qg + j:4 * qg + j + 1], scale=1.0)
            return ins
        op('scalar', ex, reads=[bSn, b_t.name], writes=[p_t.name])
        if kb >= 4 * qg:
            op('vector', lambda e, p_t=p_t, c0=c0: e.tensor_tensor(out=p_t[:, c0:c0 + P], in0=p_t[:, c0:c0 + P], in1=tri_bf[:], op=ALU.mult),
               reads=[p_t.name, 'tri_bf'], writes=[p_t.name])

        def pv(e, bO=bO, bL=bL, p_t=p_t, v_t=v_t, kb=kb, c0=c0, nkb=nkb):
            e.matmul(bO[:, c0:512], lhsT=v_t[:, kb, :], rhs=p_t[:, c0:512], start=(kb == 0), stop=(kb == nkb - 1))
            return e.matmul(bL[:, c0:512], lhsT=ones_bf[:], rhs=p_t[:, c0:512], start=(kb == 0), stop=(kb == nkb - 1))
        op('tensor', pv, reads=[p_t.name, v_t.name, 'ones_bf'], writes=[bOn, bLn])
        if kb == nkb - 1:
            oi2 = cntb['ost'] % 2
            cntb['ost'] += 1
            r_t, o_t = rl[oi2], ost[oi2]
            op('vector', lambda e, r_t=r_t, bL=bL: e.reciprocal(out=r_t[:], in_=bL), reads=[bLn], writes=[r_t.name])
            op('vector', lambda e, r_t=r_t, o_t=o_t, bO=bO: e.tensor_tensor(out=o_t[:], in0=bO, in1=r_t[:], op=ALU.mult),
               reads=[bOn, r_t.name], writes=[o_t.name])
            rrow = ((((qg // 4) * 4 + 2 + hl // 2) * 4 + qg % 4) * 2 + hl % 2) * P
            yn_ = 'yb_a_%d_%d' % (hl, qg)
            ybn.append(yn_)
            op('sync', lambda e, o_t=o_t, rrow=rrow: e.dma_start(out=ybuf[rrow:rrow + P, :], in_=o_t[:]),
               reads=[o_t.name], writes=[yn_], dma=o_t.name)

    while NH_RUN == 4 and mb_next[0] < 48:
        mod_block(mb_next[0], ringA, 2, cB2, ps[7], psn[7], bt1, modblk1, postr1, gblk1, loaded=mb_loaded[0])
        mb_loaded[0] = None
        mb_next[0] += 1
    cv_issue(len(cv_jobs))
    op('vector', lambda e: e.scalar_tensor_tensor(out=gain2c[:], in0=sc2c[:], scalar=1.0, in1=pre2c[:], op0=ALU.add, op1=ALU.mult),
       reads=allc(sc2c) + ['pre2c'], writes=['gain2c'])

    if stop_after == 2:
        if debug:
            dt_ = nc.alloc_sbuf_tensor_at("dbgt", [P, 8192], F32, offset=LIMIT - 8192 * 4 - 64)
            qd = nc.alloc_sbuf_tensor_at("qd", [P, 2, 2048], BF16, offset=LIMIT - 8192 * 4 - 64 - 8192)
            T.barrier()
            op('vector', lambda e: e.memset(dt_[:], 0.0), writes=['dbgt'])
            op('sync', lambda e: e.dma_start(out=qd[:, 0, :], in_=ybuf[0:P, :]), writes=['qd'], dma='dq0')
            op('sync', lambda e: e.dma_start(out=qd[:, 1, :], in_=ybuf[24 * P:25 * P, :]), writes=['qd1'], dma='dq1')
            op('vector', lambda e: e.tensor_copy(out=dt_[:, 0:2048], in_=qd[:, 0, :]), reads=['qd', 'dbgt'], writes=['dbgt'])
            op('vector', lambda e: e.tensor_copy(out=dt_[:, 2048:4096], in_=qd[:, 1, :]), reads=['qd1', 'dbgt'], writes=['dbgt'])
            op('vector', lambda e: e.tensor_copy(out=dt_[:, 4096:4352], in_=CumF[:].rearrange("p h kb -> p (h kb)")), reads=['dbgt'], writes=['dbgt'])
            op('sync', lambda e: e.dma_start(out=dbg[:, :], in_=dt_[:]), reads=['dbgt'], dma='dbgs')
        return finish()

    exchange_group(3, [n_ for n_ in ybn if n_.startswith('yb_a_2_') or n_.startswith('yb_a_3_')])
    T.barrier()

    A = Arena(P0)
    aT = A.alloc([P, NFC, TT2], BF16, "aT")
    zs = [nc.alloc_sbuf_tensor_at("zs%d" % i, [P, D], F32, offset=P0 + i * D * 4) for i in range(4)]
    actT = A.alloc([P, 32, TT2], BF16, "actT")
    ring2 = [A.alloc([P, 8192], BF16, "ring2_%d" % i) for i in range(2)]
    v16 = lambda t: t[:].rearrange("p (k n) -> p k n", n=512)
    v32 = lambda t: t[:].rearrange("p (k n) -> p k n", n=256)
    xs2 = A.alloc([P, D], F32, "xs2")
    Grow = A.alloc([P, D], F32, "Grow")
    xn2 = A.alloc([P, D], BF16, "xn2")
    scr = [A.alloc([P, 512], F32, "scr%d" % i) for i in range(2)]
    sq = A.alloc([P, 512], BF16, "sq")
    sg = [A.alloc([P, 512], F32, "sg%d" % i) for i in range(2)]
    ssq = A0.alloc([P, 32], F32, "ssq")
    actn = ['act%d' % k for k in range(32)]
    aTn = ['aT%d' % k for k in range(NFC)]
    c2 = {'ring': 0, 'sg': 0, 'scr': 0}

    def ring2_load(view, src_ap):
        i = c2['ring'] % 2
        c2['ring'] += 1
        rn = 'ring%d' % i
        dst = view(ring2[i])
        op('gpsimd', lambda e: e.dma_start(out=dst, in_=src_ap), writes=[rn], dma=rn)
        return ring2[i], rn

    NT2_RUN = NT2 if not (debug and stop_after == 3) else 1
    for tt in range(NT2_RUN):
        t0 = tt * TT2
        for gr in range(16):
            rr0 = gr * 1024 + tt * 256
            op('sync', lambda e, gr=gr, rr0=rr0: e.dma_start(out=actT[:, gr * 2:gr * 2 + 2, :],
                                                           in_=ymine[rr0:rr0 + 256, :].rearrange("(l p) t -> p l t", p=P)),
               writes=actn[gr * 2:gr * 2 + 2], dma='yl%d' % (gr % 4))
        op('sync', lambda e: e.dma_start(out=Grow[:], in_=Gs[0:1, :].partition_broadcast(P)), reads=GsN[0], writes=['Grow'], dma='grow')
        for grp in range(2):
            bnk, bnkn = ps[grp], psn[grp]
            kks = list(range(16, 32)) if grp == 0 else list(range(0, 16))
            for n_, kk in enumerate(kks):
                op('vector', lambda e, kk=kk: e.tensor_tensor(out=sq[:], in0=actT[:, kk, :], in1=actT[:, kk, :], op=ALU.mult),
                   reads=[actn[kk]], writes=['sq'])
                op('tensor', lambda e, bnk=bnk, n_=n_: e.matmul(bnk, lhsT=ones_bf[:], rhs=sq[:], start=(n_ == 0), stop=(n_ == 15)),
                   reads=['sq', 'ones_bf'], writes=[bnkn])
            emit_rstd(scr[grp][:], bnk, 1.0 / 2048, [bnkn], scr[grp].name)
        for kk in range(32):
            grp = 0 if kk >= 16 else 1
            op('vector', lambda e, kk=kk, grp=grp: e.scalar_tensor_tensor(out=actT[:, kk, :], in0=actT[:, kk, :], scalar=gyc[:, kk:kk + 1],
                                                                         in1=scr[grp][:], op0=ALU.mult, op1=ALU.mult),
               reads=[actn[kk], scr[grp].name, 'gyc'], writes=[actn[kk]])
        for db in range(8):
            bset = (db % 2) * 4
            for kg in range(2):
                if True:
                    rt, rn = ring2_load(v16, w_out[kg * 2048:(kg + 1) * 2048, db * 512:(db + 1) * 512].rearrange("(k p) n -> p k n", p=P))

                def mmo(e, rt=v16(rt), kg=kg, bset=bset):
                    ins = None
                    for s4 in range(4):
                        for k in range(16):
                            ins = e.matmul(ps[bset + s4], lhsT=actT[:, kg * 16 + k, s4 * P:(s4 + 1) * P], rhs=rt[:, k, :],
                                           start=(kg == 0 and k == 0), stop=(kg == 1 and k == 15))
                    return ins
                op('tensor', mmo, reads=actn + [rn], writes=[psn[bset + s4] for s4 in range(4)])
            for s4 in range(4):
                bk, bkn = ps[bset + s4], psn[bset + s4]
                op('scalar', lambda e, bk=bk, s4=s4, db=db: e.activation(out=junk_bf[:], in_=bk, func=AF.Square, accum_out=ssq[:, s4 * 8 + db:s4 * 8 + db + 1]),
                   reads=[bkn], writes=['junk_bf', 'ssq%d_%d' % (s4, db)])
                op('vector', lambda e, bk=bk, s4=s4, db=db: e.tensor_copy(out=zs[s4][:, db * 512:(db + 1) * 512], in_=bk),
                   reads=[bkn], writes=['zs%d_%d' % (s4, db)])
        for s4 in range(4):
            r0 = t0 + s4 * P
            op('sync', lambda e, r0=r0: e.dma_start(out=xs2[:], in_=x_chunk[r0:r0 + P, :]), writes=['xs2'], dma='xs2')
            ssn = ['ssq%d_%d' % (s4, db) for db in range(8)]
            st1 = stat[:, s4:s4 + 1]
            op('vector', lambda e, s4=s4, st1=st1: e.tensor_reduce(out=st1, in_=ssq[:, s4 * 8:(s4 + 1) * 8], axis=mybir.AxisListType.X, op=ALU.add),
               reads=ssn, writes=['st1_%d' % s4])
            emit_rstd(st1, st1, 1.0 / D, ['st1_%d' % s4], 'st1_%d' % s4)
            zn = ['zs%d_%d' % (s4, db) for db in range(8)]
            op('vector', lambda e, s4=s4, st1=st1: e.scalar_tensor_tensor(out=zs[s4][:], in0=zs[s4][:], scalar=st1, in1=Grow[:], op0=ALU.mult, op1=ALU.mult),
               reads=zn + ['st1_%d' % s4, 'Grow'], writes=['zs%d' % s4])
            op('vector', lambda e, s4=s4: e.tensor_tensor(out=xs2[:], in0=xs2[:], in1=zs[s4][:], op=ALU.add),
               reads=['xs2', 'zs%d' % s4], writes=['xs2'])
            op('sync', lambda e, r0=r0: e.dma_start(out=x1s[r0:r0 + P, :], in_=xs2[:]), reads=['xs2'], writes=['x1s_%d' % (tt * 4 + s4)], dma='x1st')
            st2 = stat[:, 8 + s4:9 + s4]
            op('scalar', lambda e, st2=st2: e.activation(out=xn2[:], in_=xs2[:], func=AF.Square, accum_out=st2),
               reads=['xs2'], writes=['xn2', 'st2_%d' % s4])
            emit_rstd(st2, st2, 1.0 / D, ['st2_%d' % s4], 'st2_%d' % s4)
            op('vector', lambda e, st2=st2: e.tensor_scalar(out=xn2[:], in0=xs2[:], scalar1=st2, scalar2=None, op0=ALU.mult),
               reads=['xs2', 'st2_%d' % s4], writes=['xn2'])
            for g4 in range(4):
                half = g4 % 2
                tv = tbf[:, half * 1024:(half + 1) * 1024]
                bn = psn[half]

                def tp2(e, g4=g4, tv=tv):
                    ins = None
                    for kk in range(8):
                        k = g4 * 8 + kk
                        ins = e.transpose(out=tv[:, kk * P:(kk + 1) * P], in_=xn2[:, k * P:(k + 1) * P], identity=ident_bf[:])
                    return ins
                op('tensor', tp2, reads=['xn2'], writes=[bn])

                def ev2a(e, g4=g4, tv=tv, s4=s4):
                    ins = None
                    for kk in range(8):
                        k = g4 * 8 + kk
                        ins = e.activation(out=actT[:, k, s4 * P:(s4 + 1) * P], in_=tv[:, kk * P:(kk + 1) * P], func=AF.Identity,
                                           scale=gain2c[:, k:k + 1], bias=shift2c[:, k:k + 1])
                    return ins

                def ev2v(e, g4=g4, tv=tv, s4=s4):
                    ins = None
                    for kk in range(8):
                        k = g4 * 8 + kk
                        ins = e.tensor_scalar(out=actT[:, k, s4 * P:(s4 + 1) * P], in0=tv[:, kk * P:(kk + 1) * P],
                                              scalar1=gain2c[:, k:k + 1], scalar2=shift2c[:, k:k + 1], op0=ALU.mult, op1=ALU.add)
                    return ins
                if g4 % 2 == 0:
                    op('scalar', ev2a, reads=[bn, 'gain2c'] + shift2n, writes=[actn[g4 * 8 + kk] for kk in range(8)])
                else:
                    op('vector', ev2v, reads=[bn, 'gain2c'] + shift2n, writes=[actn[g4 * 8 + kk] for kk in range(8)])
        op('sync', lambda e: e.dma_start(out=Grow[:], in_=Gs[1:2, :].partition_broadcast(P)), reads=GsN[1], writes=['Grow'], dma='grow')
        for fb in range(NFC // 2):
            f0 = fb * 256
            rg, rgn = ring2_load(v32, w_gate[:, f0:f0 + 256].rearrange("(k p) n -> p k n", p=P))
            ru, run = ring2_load(v32, w_up[:, f0:f0 + 256].rearrange("(k p) n -> p k n", p=P))
            rgv = v32(rg)
            ruv = v32(ru)
            bset = (fb % 2) * 4

            def mmg(e, rv=rgv, bset=bset, o=0):
                ins = None
                for cch in range(2):
                    for k in range(32):
                        ins = e.matmul(ps[bset + o + cch], lhsT=rv[:, k, cch * P:(cch + 1) * P], rhs=actT[:, k, :], start=(k == 0), stop=(k == 31))
                return ins
            op('tensor', mmg, reads=actn + [rgn], writes=[psn[bset], psn[bset + 1]])
            op('tensor', lambda e, rv=ruv, bset=bset: mmg(e, rv, bset, 2), reads=actn + [run], writes=[psn[bset + 2], psn[bset + 3]])
            for cch in range(2):
                fc = fb * 2 + cch
                si = c2['sg'] % 2
                c2['sg'] += 1
                s_t = sg[si]
                bg, bu = ps[bset + cch], ps[bset + 2 + cch]
                op('scalar', lambda e, s_t=s_t, bg=bg: e.activation(out=s_t[:], in_=bg, func=AF.Silu), reads=[psn[bset + cch]], writes=[s_t.name])
                op('vector', lambda e, s_t=s_t, bu=bu, fc=fc: e.tensor_tensor(out=aT[:, fc, :], in0=bu, in1=s_t[:], op=ALU.mult),
                   reads=[psn[bset + 2 + cch], s_t.name], writes=[aTn[fc]])
        for db in range(8):
            bset = (db % 2) * 4
            ngr = (NFC + 15) // 16
            for kg in range(ngr):
                nk = min(16, NFC - kg * 16)
                if STAGE_BF16:
                    rt, rn = ring2_load(lambda t, nk=nk: t[:, 0:nk * 512], wd_s[(db * 6 + kg) * P:(db * 6 + kg + 1) * P, 0:nk * 512])
                else:
                    rt, rn = ring2_load(lambda t, nk=nk: v16(t)[:, 0:nk, :],
                                        w_down[kg * 2048:kg * 2048 + nk * P, db * 512:(db + 1) * 512].rearrange("(k p) n -> p k n", p=P))

                def mmd(e, rt=v16(rt), kg=kg, nk=nk, bset=bset, ngr=ngr):
                    ins = None
                    for s4 in range(4):
                        for k in range(nk):
                            ins = e.matmul(ps[bset + s4], lhsT=aT[:, kg * 16 + k, s4 * P:(s4 + 1) * P], rhs=rt[:, k, :],
                                           start=(kg == 0 and k == 0), stop=(kg == ngr - 1 and k == nk - 1))
                    return ins
                op('tensor', mmd, reads=aTn[kg * 16:kg * 16 + nk] + [rn], writes=[psn[bset + s4] for s4 in range(4)])
            for s4 in range(4):
                bk, bkn = ps[bset + s4], psn[bset + s4]
                op('scalar', lambda e, bk=bk, s4=s4, db=db: e.activation(out=junk_bf[:], in_=bk, func=AF.Square, accum_out=ssq[:, s4 * 8 + db:s4 * 8 + db + 1]),
                   reads=[bkn], writes=['junk_bf', 'ssq%d_%d' % (s4, db)])
                ci = c2['scr'] % 2
                c2['scr'] += 1
                f_t = scr[ci]
                op('vector', lambda e, bk=bk, f_t=f_t, db=db: e.tensor_tensor(out=f_t[:], in0=bk, in1=Grow[:, db * 512:(db + 1) * 512], op=ALU.mult),
                   reads=[bkn, 'Grow'], writes=[f_t.name])
                r0 = t0 + s4 * P
                op('sync', lambda e, f_t=f_t, r0=r0, db=db: e.dma_start(out=accs[r0:r0 + P, db * 512:(db + 1) * 512], in_=f_t[:]),
                   reads=[f_t.name], writes=['acc_%d_%d' % (tt * 4 + s4, db)], dma=f_t.name + 'st')
        for s4 in range(4):
            r0 = t0 + s4 * P
            a_t = zs[s4 % 2]
            an = 'zs%d' % (s4 % 2)
            op('sync', lambda e, a_t=a_t, r0=r0: e.dma_start(out=a_t[:], in_=accs[r0:r0 + P, :]),
               reads=['acc_%d_%d' % (tt * 4 + s4, db) for db in range(8)], writes=[an] + ['zs%d_%d' % (s4 % 2, db) for db in range(8)], dma=an + 'ld')
            op('sync', lambda e, r0=r0: e.dma_start(out=xs2[:], in_=x1s[r0:r0 + P, :]), reads=['x1s_%d' % (tt * 4 + s4)], writes=['xs2'], dma='xs2')
            ssn = ['ssq%d_%d' % (s4, db) for db in range(8)]
            st3 = stat[:, 16 + s4:17 + s4]
            op('vector', lambda e, s4=s4, st3=st3: e.tensor_reduce(out=st3, in_=ssq[:, s4 * 8:(s4 + 1) * 8], axis=mybir.AxisListType.X, op=ALU.add),
               reads=ssn, writes=['st3_%d' % s4])
            emit_rstd(st3, st3, 1.0 / D, ['st3_%d' % s4], 'st3_%d' % s4)
            op('vector', lambda e, a_t=a_t, st3=st3: e.scalar_tensor_tensor(out=xs2[:], in0=a_t[:], scalar=st3, in1=xs2[:], op0=ALU.mult, op1=ALU.add),
               reads=[an, 'xs2', 'st3_%d' % s4] + ['zs%d_%d' % (s4 % 2, db) for db in range(8)], writes=['xs2'])
            op('sync', lambda e, r0=r0: e.dma_start(out=out[r0:r0 + P, :], in_=xs2[:]), reads=['xs2'], writes=['out_%d' % (tt * 4 + s4)], dma='outst')

    if debug and stop_after == 3:
        pass
    return finish()


def col_layout(v):
    return np.ascontiguousarray(np.asarray(v, dtype=np.float32).reshape(-1, P).T)


def make_in_maps(inputs):
    x = np.asarray(inputs["x"], dtype=np.float32)
    c = np.asarray(inputs["c"], dtype=np.float32)
    w_in = np.asarray(inputs["w_in"], dtype=np.float32)[0]
    w_out = np.asarray(inputs["w_out"], dtype=np.float32)[0]
    conv_w = np.asarray(inputs["conv_w"], dtype=np.float32)[0]
    b_f = np.asarray(inputs["b_f"], dtype=np.float32)[0]
    aon = np.asarray(inputs["attn_out_norm"], dtype=np.float32)[0]
    con = np.asarray(inputs["conv_out_norm"], dtype=np.float32)[0]
    shared = {
        "w_ada": np.ascontiguousarray(np.asarray(inputs["w_ada"], dtype=np.float32)[0]),
        "b_ada": np.ascontiguousarray(np.asarray(inputs["b_ada"], dtype=np.float32)[0][None, :]),
        "pre1_col": col_layout(inputs["pre_norm_mix"][0]),
        "pre2_col": col_layout(inputs["pre_norm_ffn"][0]),
        "post1_row": np.ascontiguousarray(np.asarray(inputs["post_norm_mix"], dtype=np.float32)[0][None, :]),
        "post2_row": np.ascontiguousarray(np.asarray(inputs["post_norm_ffn"], dtype=np.float32)[0][None, :]),
        "w_gate": np.ascontiguousarray(np.asarray(inputs["w_gate"], dtype=np.float32)[0]),
        "w_up": np.ascontiguousarray(np.asarray(inputs["w_up"], dtype=np.float32)[0]),
        "w_down": np.ascontiguousarray(np.asarray(inputs["w_down"], dtype=np.float32)[0]),
    }
    mchunks = []
    lbmap = {0: (4, 5), 1: (6, 7), 2: (0, 1), 3: (2, 3)}
    for g_ in range(4):
        for r in range(4):
            for lb2 in range(2):
                lb = lbmap[g_][lb2]
                mchunks.append(4 * r + lb if lb < 4 else 16 + 4 * r + (lb - 4))
    gy_full = np.concatenate([aon, con])
    gy_perm = np.concatenate([gy_full[m * P:(m + 1) * P] for m in mchunks])
    shared["gy_col"] = col_layout(gy_perm)
    shared["w_out_perm"] = np.ascontiguousarray(np.concatenate([w_out[m * P:(m + 1) * P] for m in mchunks], axis=0))
    maps = []
    for core in range(NCORES):
        b, g = divmod(core, 4)
        m = dict(shared)
        m["x_full"] = np.ascontiguousarray(x[b])
        m["x_chunk"] = np.ascontiguousarray(x[b, g * TOK2:(g + 1) * TOK2])
        m["c_col"] = col_layout(c[b])
        sl = lambda base: w_in[:, base + 512 * g: base + 512 * g + 512]
        wq, wk, wv = sl(0), sl(2048), sl(4096)
        wf = w_in[:, 6144 + 4 * g: 6144 + 4 * g + 4]
        wgb, wgc, wu = sl(6160), sl(6160 + 2048), sl(6160 + 4096)
        m["w1"] = np.ascontiguousarray(np.concatenate([wq, wk, wgb, wgc, wu, wv, wf], axis=1))
        m["wf_col"] = np.ascontiguousarray(wf.reshape(32, P, 4).transpose(1, 0, 2).reshape(P, 128))
        m["bf_row"] = np.ascontiguousarray(b_f[4 * g:4 * g + 4][None, :])
        cw = conv_w[:, 512 * g:512 * g + 512]
        m["convw_col"] = np.ascontiguousarray(cw.reshape(3, 4, P).transpose(2, 1, 0).reshape(P, 12))
        maps.append(m)
    return maps


_NC_CACHE = {}


def kernel(**inputs):
    if "nc" not in _NC_CACHE:
        _NC_CACHE["nc"] = build_nc()
    nc = _NC_CACHE["nc"]
    in_maps = make_in_maps(inputs)
    res = run_bass_kernel_spmd(nc, in_maps, core_ids=list(range(NCORES)))
    outp = np.empty((2, S, D), dtype=np.float32)
    for core in range(NCORES):
        b, g = divmod(core, 4)
        outp[b, g * TOK2:(g + 1) * TOK2] = np.asarray(res.results[core]["out"])
    return outp
```

```python
import os
import numpy as np
import concourse.bass as bass
import concourse.mybir as mybir
from concourse.bass_utils import run_bass_kernel_spmd

F32 = mybir.dt.float32
BF16 = mybir.dt.bfloat16
AF = mybir.ActivationFunctionType
ALU = mybir.AluOpType

D = 4096
S = 8192
NCORES = 8
DFF = 11008
NFC = DFF // 128
EPS = 1e-6
P = 128
TT1 = 1024
NT1 = S // TT1
TT2 = 512
TOK2 = 2048
NT2 = TOK2 // TT2
STAGE_BF16 = True
NW1 = 3072 + 4

ENG = ['sync', 'scalar', 'vector', 'gpsimd', 'tensor']
COMPUTE = ['scalar', 'vector', 'gpsimd', 'tensor']


class Tr:
    def __init__(s, nc):
        s.nc = nc
        s.prog = {e: [] for e in ENG}
        s.esem = {e: [nc.alloc_semaphore("prog_" + e), 0] for e in COMPUTE}
        s.dsem = {}
        s.res = {}
        s.waited = {e: {} for e in ENG}

    def _handle(s, semkey):
        kind, k = semkey
        return s.esem[k][0] if kind == 'e' else s.dsem[k][0]

    def _need(s, e, toks):
        best = {}
        for (semkey, v) in toks:
            if semkey == ('e', 'tensor') and e == 'tensor':
                continue
            if v > best.get(semkey, 0):
                best[semkey] = v
        for semkey, v in best.items():
            if s.waited[e].get(semkey, 0) >= v:
                continue
            s.waited[e][semkey] = v
            h = s._handle(semkey)
            s.prog[e].append(lambda eng, h=h, v=v: eng.wait_ge(h, v))

    def op(s, e, fn, reads=(), writes=(), dma=None):
        writes = list(writes) + [r for r in reads if r.startswith('bk') and r not in writes]
        toks = []
        for r in reads:
            st = s.res.get(r)
            if st and st[0]:
                toks.append(st[0])
        for w in writes:
            st = s.res.get(w)
            if st:
                if st[0]:
                    toks.append(st[0])
                toks.extend(st[1].items())
        s._need(e, toks)
        if dma is not None:
            if dma not in s.dsem:
                s.dsem[dma] = [s.nc.alloc_semaphore("d_" + str(dma)), 0]
            d = s.dsem[dma]
            d[1] += 16
            tok = (('d', dma), d[1])
            h, inc = d[0], 16
        else:
            d = s.esem[e]
            d[1] += 1
            tok = (('e', e), d[1])
            h, inc = d[0], 1
        s.prog[e].append(lambda eng, fn=fn, h=h, inc=inc: fn(eng).then_inc(h, inc))
        for w in writes:
            s.res[w] = [tok, {}]
        for r in reads:
            st = s.res.setdefault(r, [None, {}])
            if tok[1] > st[1].get(tok[0], 0):
                st[1][tok[0]] = tok[1]
        return tok

    def barrier(s):
        toks = [(('e', f), s.esem[f][1]) for f in COMPUTE if s.esem[f][1] > 0]
        toks += [(('d', k), v[1]) for k, v in s.dsem.items()]
        for e in ENG:
            for semkey, v in toks:
                if s.waited[e].get(semkey, 0) >= v:
                    continue
                s.waited[e][semkey] = v
                h = s._handle(semkey)
                s.prog[e].append(lambda eng, h=h, v=v: eng.wait_ge(h, v))
        s.res = {}

    def coll(s, fn, reads=(), writes=()):
        e = 'gpsimd'
        toks = []
        for r in reads:
            st = s.res.get(r)
            if st and st[0]:
                toks.append(st[0])
        for w in writes:
            st = s.res.get(w)
            if st:
                if st[0]:
                    toks.append(st[0])
                toks.extend(st[1].items())
        s._need(e, toks)
        if 'cc' not in s.dsem:
            s.dsem['cc'] = [s.nc.alloc_semaphore("cc"), 0]
        d = s.dsem['cc']
        d[1] += 1
        tok = (('d', 'cc'), d[1])
        h = d[0]
        s.prog[e].append(lambda eng: fn(eng).then_inc(h))
        for w in writes:
            s.res[w] = [tok, {}]
        return tok

    def emit(s, block):
        def mk(e):
            def body(eng):
                for fn in s.prog[e]:
                    fn(eng)
            return body
        block.sync(mk('sync'))
        block.scalar(mk('scalar'))
        block.vector(mk('vector'))
        block.gpsimd(mk('gpsimd'))
        block.tensor(mk('tensor'))


def build_nc(stop_after=9, debug=False):
    nc = bass.Bass("TRN2", target_bir_lowering=False)

    def din(name, shape, dt=F32):
        return nc.dram_tensor(name, shape, dt, kind="ExternalInput").ap()

    x_full = din("x_full", [S, D])
    x_chunk = din("x_chunk", [TOK2, D])
    c_col_in = din("c_col", [P, 32])
    w_ada = din("w_ada", [D, 6 * D])
    b_ada = din("b_ada", [1, 6 * D])
    pre1_in = din("pre1_col", [P, 32])
    pre2_in = din("pre2_col", [P, 32])
    post1_in = din("post1_row", [1, D])
    post2_in = din("post2_row", [1, D])
    w1 = din("w1", [D, NW1])
    bf_in = din("bf_row", [1, 4])
    wf_in = din("wf_col", [P, 128])
    convw_in = din("convw_col", [P, 12])
    gy_in = din("gy_col", [P, 32])
    w_out = din("w_out_perm", [D, D])
    w_gate = din("w_gate", [D, DFF])
    w_up = din("w_up", [D, DFF])
    w_down = din("w_down", [DFF, D])
    out = nc.dram_tensor("out", [TOK2, D], F32, kind="ExternalOutput").ap()
    if debug:
        dbg = nc.dram_tensor("dbg", [P, 8192], F32, kind="ExternalOutput").ap()

    Gs = nc.dram_tensor("Gs", [2, D], F32)
    qTs = nc.dram_tensor("qTs", [4 * P, S], BF16)
    kTs = nc.dram_tensor("kTs", [4 * P, S], BF16)
    Vs = nc.dram_tensor("Vs", [S, 512], BF16)
    ybuf = nc.dram_tensor("ybuf", [16384, 512], BF16)
    yslots = nc.dram_tensor("yslots", [16384, 512], BF16)
    ymine = nc.dram_tensor("ymine", [16384, 512], BF16)
    wd_s = nc.dram_tensor("wd_s", [48 * P, 8192], BF16)
    bar_in = nc.dram_tensor("bar_in", [16, 512], BF16)
    bar_out = nc.dram_tensor("bar_out", [64, 512], BF16)
    x1s = nc.dram_tensor("x1s", [TOK2, D], F32)
    accs = nc.dram_tensor("accs", [TOK2, D], F32)

    BASE = 16512
    LIMIT = 229344

    class Arena:
        def __init__(self, start):
            self.off = start
            self.n = 0

        def alloc(self, shape, dt, name=None):
            nbytes = int(np.prod(shape[1:])) * (4 if dt == F32 else 2)
            nbytes = (nbytes + 63) // 64 * 64
            self.n += 1
            t = nc.alloc_sbuf_tensor_at(name or ("t%d_%d" % (self.off, self.n)), list(shape), dt, offset=self.off)
            self.off += nbytes
            assert self.off <= LIMIT, (self.off, LIMIT)
            return t

    A0 = Arena(BASE)
    ident_bf = A0.alloc([P, P], BF16, "ident_bf")
    ident_f = A0.alloc([P, P], F32, "ident_f")
    tri_bf = A0.alloc([P, P], BF16, "tri_bf")
    tri_f = A0.alloc([P, P], F32, "tri_f")
    ones_bf = A0.alloc([P, P], BF16, "ones_bf")
    ones_f = A0.alloc([P, P], F32, "ones_f")
    gain1c = A0.alloc([P, 32], F32, "gain1c")
    shift1c = A0.alloc([P, 32], F32, "shift1c")
    gain2c = A0.alloc([P, 32], F32, "gain2c")
    shift2c = A0.alloc([P, 32], F32, "shift2c")
    sc1c = A0.alloc([P, 32], F32, "sc1c")
    sc2c = A0.alloc([P, 32], F32, "sc2c")
    pre1c = A0.alloc([P, 32], F32, "pre1c")
    pre2c = A0.alloc([P, 32], F32, "pre2c")
    gyc = A0.alloc([P, 32], F32, "gyc")
    ccol = A0.alloc([P, 32], F32, "ccol")
    cact = A0.alloc([P, 32], F32, "cact")
    SP = A0.alloc([P, 256], F32, "SP")
    wf_sb = A0.alloc([P, 32, 4], BF16, "wf_sb")
    bfrep = A0.alloc([P, 32], F32, "bfrep")
    convw = A0.alloc([P, 12], F32, "convw")
    halo = A0.alloc([P, 4, 2], F32, "halo")
    stat = A0.alloc([P, 64], F32, "stat")
    junk_bf = A0.alloc([P, 512], BF16, "junk_bf")
    junk_f = A0.alloc([P, P], F32, "junk_f")
    P0 = BASE + 8192
    assert A0.off <= P0, A0.off

    pp = [nc.alloc_psum_tensor("pp%d" % i, [P, 1024], F32) for i in range(4)]
    ps = [pp[i // 2][:, (i % 2) * 512:(i % 2 + 1) * 512] for i in range(8)]
    psn = ['bk%d' % i for i in range(8)]

    T = Tr(nc)
    op = T.op

    def mk_const(t, val, cmp_op=None):
        op('gpsimd', lambda e: e.memset(t[:], val), writes=[t.name])
        if cmp_op is not None:
            op('gpsimd', lambda e: e.affine_select(out=t[:], in_=t[:], pattern=[[1, P]], compare_op=cmp_op,
                                                   fill=0.0, base=0, channel_multiplier=-1),
               reads=[t.name], writes=[t.name])

    mk_const(ident_bf, 1.0, ALU.is_equal)
    mk_const(ident_f, 1.0, ALU.is_equal)
    mk_const(tri_bf, 1.0, ALU.is_ge)
    mk_const(tri_f, 1.0, ALU.is_ge)
    mk_const(ones_bf, 1.0)
    mk_const(ones_f, 1.0)
    op('gpsimd', lambda e: e.memset(halo[:], 0.0), writes=['halo'])

    def small_load(dst, src, key, name):
        op('sync', lambda e: e.dma_start(out=dst, in_=src), writes=[name], dma=key)

    small_load(ccol[:], c_col_in[:, :], 'm0', 'ccol')
    small_load(pre1c[:], pre1_in[:, :], 'm1', 'pre1c')
    small_load(pre2c[:], pre2_in[:, :], 'm2', 'pre2c')
    small_load(gyc[:], gy_in[:, :], 'm3', 'gyc')
    small_load(convw[:], convw_in[:, :], 'm4', 'convw')
    for s8 in range(8):
        small_load(bfrep[:, s8 * 4:(s8 + 1) * 4], bf_in[0:1, :].partition_broadcast(P), 'm5', 'bfrep%d' % s8)
    op('gpsimd', lambda e: e.dma_start(out=wf_sb[:].rearrange("p k n -> p (k n)"), in_=wf_in[:, :]), writes=['wf_sb'], dma='m6')

    def emit_rstd(dst, src, mul, rname, wname):
        op('vector', lambda e: e.tensor_scalar(out=dst, in0=src, scalar1=mul, scalar2=EPS, op0=ALU.mult, op1=ALU.add),
           reads=rname, writes=[wname])
        op('scalar', lambda e: e.activation(out=dst, in_=dst, func=AF.Sqrt), reads=[wname], writes=[wname])
        op('vector', lambda e: e.reciprocal(out=dst, in_=dst), reads=[wname], writes=[wname])

    A = Arena(P0)
    cB = A.alloc([P, 32, P], BF16, "cB")
    ring0 = [A.alloc([P, 16, 512], BF16, "ring0_%d" % i) for i in range(3)]
    bt = [A.alloc([P, 512], F32, "bt%d" % i) for i in range(2)]
    postr = [A.alloc([P, 512], F32, "postr%d" % i) for i in range(2)]
    modblk = [A.alloc([P, 512], F32, "modblk%d" % i) for i in range(2)]
    gblk = [A.alloc([P, 512], F32, "gblk%d" % i) for i in range(2)]

    op('scalar', lambda e: e.activation(out=cact[:], in_=ccol[:], func=AF.Silu), reads=['ccol'], writes=['cact'])
    for k in range(32):
        op('vector', lambda e, k=k: e.tensor_scalar(out=cB[:, k, :], in0=ones_bf[:], scalar1=cact[:, k:k + 1],
                                                    scalar2=None, op0=ALU.mult),
           reads=['cact', 'ones_bf'], writes=['cB%d' % k])
    cBn = ['cB%d' % k for k in range(32)]

    rcount = [0]

    cv_jobs = []
    for db in range(8):
        for kg in range(6):
            nk = min(16, NFC - kg * 16)
            blk = db * 6 + kg
            cv_jobs.append((wd_s[blk * P:(blk + 1) * P, 0:nk * 512].rearrange("p (k n) -> p k n", n=512),
                            w_down[kg * 2048:kg * 2048 + nk * P, db * 512:(db + 1) * 512].rearrange("(k p) n -> p k n", p=P)))
    cv_pos = [0]

    def cv_issue(n=1):
        if not STAGE_BF16:
            return
        for _ in range(n):
            if cv_pos[0] >= len(cv_jobs):
                return
            dst, src = cv_jobs[cv_pos[0]]
            key = 'cv%d' % (cv_pos[0] % 4)
            cv_pos[0] += 1
            op('gpsimd', lambda e, dst=dst, src=src: e.dma_start(out=dst, in_=src), writes=['cvs_' + key], dma=key)

    def ring_load(rings, view_fn, src_ap, nslots, cv=0):
        i = rcount[0] % nslots
        rcount[0] += 1
        rn = 'ring%d' % i
        dst = view_fn(rings[i])
        op('gpsimd', lambda e: e.dma_start(out=dst, in_=src_ap), writes=[rn], dma=rn)
        cv_issue(cv)
        return rings[i], rn

    seg_cols = {0: shift1c, 1: sc1c, 3: shift2c, 4: sc2c}

    def mod_load(cb, rings, nslots):
        c0 = cb * 512
        return [ring_load(rings, lambda t: t[:], w_ada[kg * 2048:(kg + 1) * 2048, c0:c0 + 512].rearrange("(k p) n -> p k n", p=P), nslots)
                for kg in range(2)]

    def mod_block(cb, rings, nslots, cBt, pb, pbn, b_t, m_t, p_t, g_t, loaded=None):
        seg = cb // 8
        c0 = cb * 512
        cBn_ = ['cB%d' % k for k in range(32)]
        if loaded is None:
            loaded = mod_load(cb, rings, nslots)
        for kg in range(2):
            rt, rn = loaded[kg]

            def mm(e, rt=rt, kg=kg, pb=pb):
                ins = None
                for k in range(16):
                    ins = e.matmul(pb, lhsT=cBt[:, kg * 16 + k, :], rhs=rt[:, k, :],
                                   start=(kg == 0 and k == 0), stop=(kg == 1 and k == 15))
                return ins
            op('tensor', mm, reads=[rn] + cBn_, writes=[pbn])
        op('sync', lambda e, b_t=b_t, c0=c0: e.dma_start(out=b_t[:], in_=b_ada[0:1, c0:c0 + 512].partition_broadcast(P)),
           writes=[b_t.name], dma=b_t.name)
        op('vector', lambda e, m_t=m_t, pb=pb, b_t=b_t: e.tensor_tensor(out=m_t[:], in0=pb, in1=b_t[:], op=ALU.add),
           reads=[pbn, b_t.name], writes=[m_t.name])
        if seg in seg_cols:
            colt = seg_cols[seg]
            for i in range(4):
                cidx = (cb % 8) * 4 + i
                op('vector', lambda e, m_t=m_t, i=i, colt=colt, cidx=cidx: e.scalar_tensor_tensor(
                    out=junk_f[:], in0=m_t[:, i * P:(i + 1) * P], scalar=1.0, in1=ident_f[:],
                    op0=ALU.mult, op1=ALU.mult, accum_out=colt[:, cidx:cidx + 1]),
                   reads=[m_t.name], writes=['junk_f', colt.name + str(cidx)])
        else:
            which = 0 if seg == 2 else 1
            prow = post1_in if which == 0 else post2_in
            cc0 = (cb % 8) * 512
            op('sync', lambda e, p_t=p_t, prow=prow, cc0=cc0: e.dma_start(
                out=p_t[:], in_=prow[0:1, cc0:cc0 + 512].partition_broadcast(P)), writes=[p_t.name], dma=p_t.name)
            op('vector', lambda e, g_t=g_t, m_t=m_t, p_t=p_t: e.tensor_tensor(out=g_t[:], in0=m_t[:], in1=p_t[:], op=ALU.mult),
               reads=[m_t.name, p_t.name], writes=[g_t.name])
            op('sync', lambda e, g_t=g_t, which=which, cc0=cc0: e.dma_start(out=Gs[which:which + 1, cc0:cc0 + 512], in_=g_t[0:1, :]),
               reads=[g_t.name], writes=['Gs%d_%d' % (which, cb % 8)], dma=g_t.name + 'st')

    for cb in range(16):
        mod_block(cb, ring0, 3, cB, ps[cb % 2], psn[cb % 2], bt[cb % 2], modblk[cb % 2], postr[cb % 2], gblk[cb % 2])
    allc = lambda t: [t.name + str(i) for i in range(32)]
    op('vector', lambda e: e.scalar_tensor_tensor(out=gain1c[:], in0=sc1c[:], scalar=1.0, in1=pre1c[:], op0=ALU.add, op1=ALU.mult),
       reads=allc(sc1c) + ['pre1c'], writes=['gain1c'])
    shift1n = allc(shift1c)
    shift2n = allc(shift2c)
    GsN = [['Gs%d_%d' % (w, i) for i in range(8)] for w in range(2)]

    def dbg_dump(items, reads):
        dt_ = nc.alloc_sbuf_tensor_at("dbgt", [P, 8192], F32, offset=LIMIT - 8192 * 4 - 64)
        op('vector', lambda e: e.memset(dt_[:], 0.0), writes=['dbgt'])
        for (off, n, src) in items:
            op('vector', lambda e, off=off, n=n, src=src: e.tensor_copy(out=dt_[:, off:off + n], in_=src),
               reads=list(reads) + ['dbgt'], writes=['dbgt'])
        op('sync', lambda e: e.dma_start(out=dbg[:, :], in_=dt_[:]), reads=['dbgt'], dma='dbgs')

    def finish():
        T.barrier()
        with nc.Block() as block:
            T.emit(block)
        return nc

    if stop_after == 0:
        if debug:
            dbg_dump([(0, 32, gain1c[:]), (32, 32, shift1c[:]), (64, 32, gain2c[:]), (96, 32, shift2c[:])],
                     ['gain1c', 'gain2c'] + shift1n + shift2n)
        return finish()

    T.barrier()
    A = Arena(P0)
    hT = A.alloc([P, 32, TT1], BF16, "hT")
    ring1 = [A.alloc([P, 32, 256], BF16, "ring1_%d" % i) for i in range(3)]
    xs = [A.alloc([P, D], F32, "xs%d" % i) for i in range(2)]
    xn = [A.alloc([P, D], BF16, "xn%d" % i) for i in range(2)]
    qst = [A.alloc([P, TT1], BF16, "qst%d" % i) for i in range(2)]
    vst = A.alloc([P, 8, 256], BF16, "vst")
    gb_sb = A.alloc([P, 2, TT1], F32, "gb_sb")
    gc_sb = A.alloc([P, 2, TT1], F32, "gc_sb")
    vbuf = A.alloc([P, 2, TT1 + 2], F32, "vbuf")
    ysb = [A.alloc([P, TT1], BF16, "ysb%d" % i) for i in range(2)]
    fsm = A0.alloc([P, 32], F32, "fsm")

    hTn = ['hT%d' % k for k in range(32)]
    ybn = []
    tbf = pp[0][:].bitcast(BF16)
    QSCALE = 1.0 / float(np.sqrt(128.0))
    blocks = [('q', 0, 0), ('q', 256, 1), ('k', 512, 0), ('k', 768, 1),
              ('gb', 1024, 0), ('gc', 1536, 0), ('u', 2048, 0),
              ('gb', 1280, 1), ('gc', 1792, 1), ('u', 2304, 1),
              ('v', 2560, 0), ('v', 2816, 1)]
    cnt = {'xs': 0, 'st': 0, 'pair': 0, 'bank': 0, 'qst': 0, 'ysb': 0, 'ev': 0}
    NT1_RUN = NT1 if stop_after != 1 or not debug else 2
    if int(os.environ.get('K_CUT', '99')) == 0:
        NT1_RUN = 0

    for ti in range(NT1_RUN):
        for s8 in range(8):
            i = cnt['xs'] % 2
            cnt['xs'] += 1
            xt, xnt = xs[i], xn[i]
            r0 = ti * TT1 + s8 * P
            op('sync', lambda e, xt=xt, r0=r0: e.dma_start(out=xt[:], in_=x_full[r0:r0 + P, :]), writes=[xt.name], dma=xt.name)
            sc = cnt['st'] % 32
            cnt['st'] += 1
            stc = stat[:, sc:sc + 1]
            stn = 'stat%d' % sc
            op('scalar', lambda e, xt=xt, xnt=xnt, stc=stc: e.activation(out=xnt[:], in_=xt[:], func=AF.Square, accum_out=stc),
               reads=[xt.name], writes=[xnt.name, stn])
            emit_rstd(stc, stc, 1.0 / D, [stn], stn)
            op('vector', lambda e, xt=xt, xnt=xnt, stc=stc: e.tensor_scalar(out=xnt[:], in0=xt[:], scalar1=stc, scalar2=None, op0=ALU.mult),
               reads=[xt.name, stn], writes=[xnt.name])
            for g4 in range(4):
                half = g4 % 2
                tv = tbf[:, half * 1024:(half + 1) * 1024]
                bn = psn[half]

                def tp(e, xnt=xnt, g4=g4, tv=tv):
                    ins = None
                    for kk in range(8):
                        k = g4 * 8 + kk
                        ins = e.transpose(out=tv[:, kk * P:(kk + 1) * P], in_=xnt[:, k * P:(k + 1) * P], identity=ident_bf[:])
                    return ins
                op('tensor', tp, reads=[xnt.name], writes=[bn])

                def ev_act(e, g4=g4, tv=tv, s8=s8):
                    ins = None
                    for kk in range(8):
                        k = g4 * 8 + kk
                        ins = e.activation(out=hT[:, k, s8 * P:(s8 + 1) * P], in_=tv[:, kk * P:(kk + 1) * P], func=AF.Identity,
                                           scale=gain1c[:, k:k + 1], bias=shift1c[:, k:k + 1])
                    return ins

                def ev_dve(e, g4=g4, tv=tv, s8=s8):
                    ins = None
                    for kk in range(8):
                        k = g4 * 8 + kk
                        ins = e.tensor_scalar(out=hT[:, k, s8 * P:(s8 + 1) * P], in0=tv[:, kk * P:(kk + 1) * P],
                                              scalar1=gain1c[:, k:k + 1], scalar2=shift1c[:, k:k + 1], op0=ALU.mult, op1=ALU.add)
                    return ins
                if g4 % 2 == 0:
                    op('scalar', ev_act, reads=[bn, 'gain1c'] + shift1n, writes=[hTn[g4 * 8 + kk] for kk in range(8)])
                else:
                    op('vector', ev_dve, reads=[bn, 'gain1c'] + shift1n, writes=[hTn[g4 * 8 + kk] for kk in range(8)])

        KCUT = int(os.environ.get("K_CUT", "99"))
        if KCUT <= 1:
            continue
        def mmf(e):
            ins = None
            for s8 in range(8):
                for k in range(32):
                    ins = e.matmul(ps[7][:, s8 * 4:(s8 + 1) * 4], lhsT=hT[:, k, s8 * P:(s8 + 1) * P], rhs=wf_sb[:, k, :],
                                   start=(k == 0), stop=(k == 31))
            return ins
        op('tensor', mmf, reads=hTn + ['wf_sb'], writes=[psn[7]])
        op('vector', lambda e: e.tensor_tensor(out=fsm[:], in0=ps[7][:, 0:32], in1=bfrep[:], op=ALU.add),
           reads=[psn[7]] + ['bfrep%d' % i for i in range(8)], writes=['fsm'])
        op('scalar', lambda e: e.activation(out=fsm[:], in_=fsm[:], func=AF.Exp, scale=-1.0), reads=['fsm'], writes=['fsm'])
        op('scalar', lambda e, ti=ti: e.activation(out=SP[:, ti * 32:(ti + 1) * 32], in_=fsm[:], func=AF.Ln, bias=1.0),
           reads=['fsm'], writes=['SP%d' % ti])

        for bidx, (kind, col0, idx) in enumerate(blocks):
            if bidx >= KCUT - 2:
                break
            rt, rn = ring_load(ring1, lambda t: t[:], w1[:, col0:col0 + 256].rearrange("(k p) n -> p k n", p=P), 3, cv=(1 if bidx % 4 == 1 else 0))
            if kind == 'v':
                for s8 in range(8):
                    bi = 2 + cnt['bank'] % 6
                    cnt['bank'] += 1

                    def mmv(e, rt=rt, s8=s8, bi=bi):
                        ins = None
                        for k in range(32):
                            ins = e.matmul(ps[bi][:, 0:256], lhsT=hT[:, k, s8 * P:(s8 + 1) * P], rhs=rt[:, k, :],
                                           start=(k == 0), stop=(k == 31))
                        return ins
                    op('tensor', mmv, reads=hTn + [rn], writes=[psn[bi]])
                    eng = 'scalar' if s8 % 2 == 0 else 'vector'
                    if eng == 'scalar':
                        op('scalar', lambda e, s8=s8, bi=bi: e.activation(out=vst[:, s8, :], in_=ps[bi][:, 0:256], func=AF.Copy),
                           reads=[psn[bi]], writes=['vst%d' % s8])
                    else:
                        op('vector', lambda e, s8=s8, bi=bi: e.tensor_copy(out=vst[:, s8, :], in_=ps[bi][:, 0:256]),
                           reads=[psn[bi]], writes=['vst%d' % s8])
                op('sync', lambda e, ti=ti, idx=idx: e.dma_start(
                    out=Vs[ti * TT1:(ti + 1) * TT1, idx * 256:(idx + 1) * 256].rearrange("(s p) c -> p s c", p=P), in_=vst[:]),
                   reads=['vst%d' % i for i in range(8)], writes=['Vs_%d_%d' % (ti, idx)], dma='vst')
                continue
            for cch in range(2):
                pi = 1 + cnt['pair'] % 3
                cnt['pair'] += 1
                pair = pp[pi]
                pn = [psn[2 * pi], psn[2 * pi + 1]]

                def mmq(e, rt=rt, cch=cch, pair=pair):
                    ins = None
                    for k in range(32):
                        for half in range(2):
                            ins = e.matmul(pair[:, half * 512:(half + 1) * 512], lhsT=rt[:, k, cch * P:(cch + 1) * P],
                                           rhs=hT[:, k, half * 512:(half + 1) * 512], start=(k == 0), stop=(k == 31))
                    return ins
                op('tensor', mmq, reads=hTn + [rn], writes=pn)
                if kind in ('q', 'k'):
                    hl = idx * 2 + cch
                    qi = cnt['qst'] % 2
                    cnt['qst'] += 1
                    qt = qst[qi]
                    dst = qTs if kind == 'q' else kTs
                    if kind == 'q':
                        op('scalar', lambda e, qt=qt, pair=pair: e.activation(out=qt[:], in_=pair[:], func=AF.Copy, scale=QSCALE),
                           reads=pn, writes=[qt.name])
                    else:
                        op('vector', lambda e, qt=qt, pair=pair: e.tensor_copy(out=qt[:], in_=pair[:]), reads=pn, writes=[qt.name])
                    op('sync', lambda e, qt=qt, dst=dst, hl=hl, ti=ti: e.dma_start(
                        out=dst[hl * P:(hl + 1) * P, ti * TT1:(ti + 1) * TT1], in_=qt[:]),
                       reads=[qt.name], writes=['%sTs_%d_%d' % (kind, hl, ti)], dma=qt.name)
                elif kind == 'gb':
                    op('scalar', lambda e, cch=cch, pair=pair: e.activation(out=gb_sb[:, cch, :], in_=pair[:], func=AF.Copy),
                       reads=pn, writes=['gb%d' % cch])
                elif kind == 'gc':
                    op('vector', lambda e, cch=cch, pair=pair: e.tensor_copy(out=gc_sb[:, cch, :], in_=pair[:]),
                       reads=pn, writes=['gc%d' % cch])
                else:
                    cc = idx * 2 + cch
                    vn = 'vb%d' % cch
                    hn = 'halo%d' % cc
                    gcn = 'gc%d' % cch
                    yi = cnt['ysb'] % 2
                    cnt['ysb'] += 1
                    yt = ysb[yi]
                    op('vector', lambda e, cch=cch, cc=cc: e.tensor_copy(out=vbuf[:, cch, 0:2], in_=halo[:, cc, :]),
                       reads=['halo', hn], writes=[vn])
                    op('vector', lambda e, cch=cch, pair=pair: e.tensor_tensor(out=vbuf[:, cch, 2:2 + TT1], in0=pair[:], in1=gc_sb[:, cch, :], op=ALU.mult),
                       reads=pn + [gcn, vn], writes=[vn])
                    op('vector', lambda e, cch=cch, cc=cc: e.tensor_copy(out=halo[:, cc, :], in_=vbuf[:, cch, TT1:TT1 + 2]),
                       reads=[vn], writes=[hn])
                    op('vector', lambda e, cch=cch, cc=cc: e.tensor_scalar(out=gc_sb[:, cch, :], in0=vbuf[:, cch, 2:2 + TT1],
                                                                        scalar1=convw[:, cc * 3 + 2:cc * 3 + 3], scalar2=None, op0=ALU.mult),
                       reads=[vn, 'convw'], writes=[gcn])
                    op('vector', lambda e, cch=cch, cc=cc: e.scalar_tensor_tensor(out=gc_sb[:, cch, :], in0=vbuf[:, cch, 1:1 + TT1],
                                                                               scalar=convw[:, cc * 3 + 1:cc * 3 + 2], in1=gc_sb[:, cch, :],
                                                                               op0=ALU.mult, op1=ALU.add),
                       reads=[vn, gcn], writes=[gcn])
                    op('vector', lambda e, cch=cch, cc=cc: e.scalar_tensor_tensor(out=gc_sb[:, cch, :], in0=vbuf[:, cch, 0:TT1],
                                                                               scalar=convw[:, cc * 3:cc * 3 + 1], in1=gc_sb[:, cch, :],
                                                                               op0=ALU.mult, op1=ALU.add),
                       reads=[vn, gcn], writes=[gcn])
                    op('vector', lambda e, cch=cch, yt=yt: e.tensor_tensor(out=yt[:], in0=gc_sb[:, cch, :], in1=gb_sb[:, cch, :], op=ALU.mult),
                       reads=[gcn, 'gb%d' % cch], writes=[yt.name])
                    for hf in range(2):
                        rrow = ((((ti // 2) * 4 + cc // 2) * 4 + (ti % 2) * 2 + hf) * 2 + cc % 2) * P
                        yn_ = 'yb_c_%d_%d_%d' % (ti, cc, hf)
                        ybn.append(yn_)
                        op('sync', lambda e, yt=yt, rrow=rrow, hf=hf: e.dma_start(out=ybuf[rrow:rrow + P, :], in_=yt[:, hf * 512:(hf + 1) * 512]),
                           reads=[yt.name], writes=[yn_], dma=yt.name + '_%d' % hf)

    if stop_after == 1:
        if debug:
            T.barrier()
            dt_ = nc.alloc_sbuf_tensor_at("dbgt", [P, 8192], F32, offset=LIMIT - 8192 * 4 - 64)
            op('vector', lambda e: e.memset(dt_[:], 0.0), writes=['dbgt'])
            op('vector', lambda e: e.tensor_copy(out=dt_[:, 0:256], in_=SP[:]), reads=['dbgt'], writes=['dbgt'])
            op('vector', lambda e: e.tensor_copy(out=dt_[:, 256:384], in_=hT[:, 0, 896:1024]), reads=['dbgt'], writes=['dbgt'])
            op('vector', lambda e: e.tensor_copy(out=dt_[:, 384:512], in_=hT[:, 31, 0:128]), reads=['dbgt'], writes=['dbgt'])
            qd = nc.alloc_sbuf_tensor_at("qd", [P, 4, 1024], BF16, offset=LIMIT - 8192 * 4 - 64 - 8192)
            op('sync', lambda e: e.dma_start(out=qd[:, 0, :], in_=qTs[0:P, 0:1024]), writes=['qd'], dma='dq0')
            op('sync', lambda e: e.dma_start(out=qd[:, 1, :], in_=kTs[P:2 * P, 1024:2048]), writes=['qd1'], dma='dq1')
            op('sync', lambda e: e.dma_start(out=qd[:, 2, 0:512], in_=Vs[1024:1024 + P, 0:512]), writes=['qd2'], dma='dq2')
            op('sync', lambda e: e.dma_start(out=qd[:, 3, :], in_=ybuf[5 * P:6 * P, 0:1024]), writes=['qd3'], dma='dq3')
            op('vector', lambda e: e.tensor_copy(out=dt_[:, 1024:2048], in_=qd[:, 0, :]), reads=['qd', 'dbgt'], writes=['dbgt'])
            op('vector', lambda e: e.tensor_copy(out=dt_[:, 2048:3072], in_=qd[:, 1, :]), reads=['qd1', 'dbgt'], writes=['dbgt'])
            op('vector', lambda e: e.tensor_copy(out=dt_[:, 3072:3584], in_=qd[:, 2, 0:512]), reads=['qd2', 'dbgt'], writes=['dbgt'])
            op('vector', lambda e: e.tensor_copy(out=dt_[:, 4096:5120], in_=qd[:, 3, :]), reads=['qd3', 'dbgt'], writes=['dbgt'])
            op('sync', lambda e: e.dma_start(out=dbg[:, :], in_=dt_[:]), reads=['dbgt'], dma='dbgs')
        return finish()

    T.barrier()
    A = Arena(P0)
    qT = [A.alloc([P, S], BF16, "qT%d" % i) for i in range(2)]
    kT = [A.alloc([P, S], BF16, "kT%d" % i) for i in range(2)]
    Vt = [A.alloc([P, 64, P], BF16, "Vt%d" % i) for i in range(2)]
    Bm = [A.alloc([P, 64, 64], F32, "Bm%d" % i) for i in range(2)]
    PT = [A.alloc([P, 512], BF16, "PT%d" % i) for i in range(3)]
    rl = [A.alloc([P, 512], F32, "rl%d" % i) for i in range(2)]
    ost = [A.alloc([P, 512], BF16, "ost%d" % i) for i in range(2)]
    CumH = A.alloc([P, 4, 64], F32, "CumH")
    TotH = A.alloc([P, 4, 64], F32, "TotH")
    EndH = A.alloc([P, 4, 64], F32, "EndH")
    CumF = A.alloc([P, 4, 64], F32, "CumF")
    SPn = ['SP%d' % i for i in range(NT1)]
    cB2 = A.alloc([P, 32, P], BF16, "cB2")
    ringA = [A.alloc([P, 16, 512], BF16, "ringA_%d" % i) for i in range(2)]
    bt1 = A.alloc([P, 512], F32, "bt1")
    postr1 = A.alloc([P, 512], F32, "postr1")
    modblk1 = A.alloc([P, 512], F32, "modblk1")
    gblk1 = A.alloc([P, 512], F32, "gblk1")
    for k in range(32):
        op('vector', lambda e, k=k: e.tensor_scalar(out=cB2[:, k, :], in0=ones_bf[:], scalar1=cact[:, k:k + 1],
                                                    scalar2=None, op0=ALU.mult), writes=['cB%d' % k])

    pst = {}

    def gp_j(e):
        if 'j' not in pst:
            pst['j'] = e.partition_id() % 4
        return pst['j']

    def exchange_group(g, reads):
        for jp in range(4):
            r0 = (jp * 4 + g) * 1024
            T.coll(lambda e, r0=r0, jp=jp: e.collective_compute(
                "AllGather", ALU.bypass, replica_groups=[[0, 1, 2, 3], [4, 5, 6, 7]],
                ins=[ybuf[r0:r0 + 1024, :]], outs=[yslots[jp * 4096:(jp + 1) * 4096, :]]),
                reads=reads if jp == 0 else [], writes=['ysl%d' % jp, 'ccchain'])

        for _ in range(2):
            T.coll(lambda e: e.collective_compute("AllGather", ALU.bypass, replica_groups=[[0, 1, 2, 3], [4, 5, 6, 7]],
                                                  ins=[bar_in[:, :]], outs=[bar_out[:, :]]),
                   reads=[], writes=['ccchain'])

        def cp(e, g=g):
            j = gp_j(e)
            return e.dma_start(out=ymine[g * 4096:(g + 1) * 4096, :], in_=yslots[bass.ds(j * 4096, 4096), :])
        op('gpsimd', cp, reads=['ysl%d' % i for i in range(4)] + ['ccchain'], writes=['ym%d' % g], dma='ymc')
        T.coll(lambda e: e.collective_compute("AllGather", ALU.bypass, replica_groups=[[0, 1, 2, 3], [4, 5, 6, 7]],
                                              ins=[bar_in[:, :]], outs=[bar_out[:, :]]),
               reads=['ym%d' % g], writes=['ysl%d' % i for i in range(4)] + ['ccchain'])

    exchange_group(0, [])
    exchange_group(1, [])

    op('tensor', lambda e: e.matmul(ps[6][:, 0:256], lhsT=tri_f[:], rhs=SP[:], start=True, stop=True), reads=SPn + ['tri_f'], writes=[psn[6]])
    op('tensor', lambda e: e.matmul(ps[7][:, 0:256], lhsT=ones_f[:], rhs=SP[:], start=True, stop=True), reads=SPn + ['ones_f'], writes=[psn[7]])
    op('vector', lambda e: e.tensor_copy(out=CumH[:], in_=ps[6][:, 0:256].rearrange("p (kb h) -> p h kb", h=4)), reads=[psn[6]], writes=['CumH'])
    op('vector', lambda e: e.tensor_copy(out=TotH[:], in_=ps[7][:, 0:256].rearrange("p (kb h) -> p h kb", h=4)), reads=[psn[7]], writes=['TotH'])
    for h in range(4):
        op('vector', lambda e, h=h: e.tensor_tensor_scan(out=EndH[:, h, :], data0=ones_f[:, 0:64], data1=TotH[:, h, :], initial=0.0,
                                                         op0=ALU.mult, op1=ALU.add), reads=['TotH', 'ones_f'], writes=['EndH%d' % h])
    EndN = ['EndH%d' % h for h in range(4)]
    op('vector', lambda e: e.tensor_tensor(out=CumF[:], in0=CumH[:], in1=EndH[:], op=ALU.add), reads=['CumH'] + EndN, writes=['CumF'])
    op('vector', lambda e: e.tensor_tensor(out=CumF[:], in0=CumF[:], in1=TotH[:], op=ALU.subtract), reads=['CumF', 'TotH'], writes=['CumF'])

    cntb = {'s': 0, 'pt': 0, 'o': 0, 'ost': 0}
    NH_RUN = 4 if not (debug and stop_after == 2) else 1
    PTx = PT + [A.alloc([P, 512], BF16, "PT3")]
    SB = [0, 1, 6]
    LOOK = 2

    def head_setup(hl):
        sl = hl % 2
        q_t, k_t, v_t, b_t = qT[sl], kT[sl], Vt[sl], Bm[sl]
        op('sync', lambda e, q_t=q_t, hl=hl: e.dma_start(out=q_t[:], in_=qTs[hl * P:(hl + 1) * P, :]),
           reads=['qTs_%d_%d' % (hl, t) for t in range(NT1)], writes=[q_t.name], dma=q_t.name)
        op('sync', lambda e, k_t=k_t, hl=hl: e.dma_start(out=k_t[:], in_=kTs[hl * P:(hl + 1) * P, :]),
           reads=['kTs_%d_%d' % (hl, t) for t in range(NT1)], writes=[k_t.name], dma=k_t.name)
        op('sync', lambda e, v_t=v_t, hl=hl: e.dma_start(out=v_t[:], in_=Vs[:, hl * P:(hl + 1) * P].rearrange("(kb p) d -> p kb d", p=P)),
           reads=['Vs_%d_%d' % (t, hl // 2) for t in range(NT1)], writes=[v_t.name], dma=v_t.name)

        def mkB(e, b_t=b_t, hl=hl):
            ins = None
            for kb in range(64):
                ins = e.tensor_scalar(out=b_t[:, kb, :], in0=EndH[:, hl, :], scalar1=-1.0, scalar2=CumF[:, hl, kb:kb + 1],
                                      op0=ALU.mult, op1=ALU.add)
            return ins
        op('vector', mkB, reads=EndN + ['CumF'], writes=[b_t.name])

    blks = []
    for hl in range(NH_RUN):
        for qg in range(16):
            for kb in range(4 * qg + 4):
                blks.append((hl, qg, kb))
    info = {}

    def emit_S(i):
        hl, qg, kb = blks[i]
        sl = hl % 2
        q_t, k_t = qT[sl], kT[sl]
        c0 = max(0, kb - 4 * qg) * P
        bi = SB[i % 3]
        bS, bSn = ps[bi], psn[bi]
        op('tensor', lambda e, bS=bS, c0=c0, k_t=k_t, q_t=q_t, kb=kb, qg=qg: e.matmul(
            bS[:, c0:512], lhsT=k_t[:, kb * P:(kb + 1) * P], rhs=q_t[:, qg * 512 + c0:(qg + 1) * 512], start=True, stop=True),
           reads=[k_t.name, q_t.name], writes=[bSn])
        info[i] = (bS, bSn)

    head_setup(0)
    if NH_RUN > 1:
        head_setup(1)
    for i in range(min(LOOK, len(blks))):
        emit_S(i)
    mb_next = [16]
    mb_loaded = [None]
    i_ex2 = 2 * 544
    trig = set()
    if NH_RUN == 4:
        ra = list(range(600, i_ex2 - 20))
        rb = list(range(i_ex2 + 300, len(blks) - 40))
        na = 12
        nb_ = 20
        trig = set(ra[(t * len(ra)) // na] for t in range(na)) | set(rb[(t * len(rb)) // nb_] for t in range(nb_))
    for i, (hl, qg, kb) in enumerate(blks):
        sl = hl % 2
        v_t, b_t = Vt[sl], Bm[sl]
        nkb = 4 * qg + 4
        if i == 560 and NH_RUN == 4:
            mb_loaded[0] = mod_load(mb_next[0], ringA, 2)
        if i in trig and mb_next[0] < 48:
            mod_block(mb_next[0], ringA, 2, cB2, ps[7], psn[7], bt1, modblk1, postr1, gblk1, loaded=mb_loaded[0])
            mb_next[0] += 1
            cv_issue(2)
            mb_loaded[0] = mod_load(mb_next[0], ringA, 2) if mb_next[0] < 48 else None
        if NH_RUN == 4 and hl == 2 and qg == 0 and kb == 0:
            exchange_group(2, [n_ for n_ in ybn if n_.startswith('yb_a_0_') or n_.startswith('yb_a_1_')])
        if qg == 0 and kb == 0 and hl >= 1 and hl + 1 < NH_RUN:
            head_setup(hl + 1)
        if i + LOOK < len(blks):
            emit_S(i + LOOK)
        if kb == 0:
            oi = cntb['o'] % 2
            cntb['o'] += 1
            info['o'] = (ps[2 + oi], ps[4 + oi], psn[2 + oi], psn[4 + oi])
        bO, bL, bOn, bLn = info['o']
        bS, bSn = info.pop(i)
        j0 = max(0, kb - 4 * qg)
        c0 = j0 * P
        p_t = PTx[i % 4]

        def ex(e, bS=bS, p_t=p_t, b_t=b_t, j0=j0, kb=kb, qg=qg):
            ins = None
            for j in range(j0, 4):
                ins = e.activation(out=p_t[:, j * P:(j + 1) * P], in_=bS[:, j * P:(j + 1) * P], func=AF.Exp,
                                   bias=b_t[:, kb, 4 * qg + j:4 * qg + j + 1], scale=1.0)
            return ins
        op('scalar', ex, reads=[bSn, b_t.name], writes=[p_t.name])
        if kb >= 4 * qg:
            op('vector', lambda e, p_t=p_t, c0=c0: e.tensor_tensor(out=p_t[:, c0:c0 + P], in0=p_t[:, c0:c0 + P], in1=tri_bf[:], op=ALU.mult),
               reads=[p_t.name, 'tri_bf'], writes=[p_t.name])

        def pv(e, bO=bO, bL=bL, p_t=p_t, v_t=v_t, kb=kb, c0=c0, nkb=nkb):
            e.matmul(bO[:, c0:512], lhsT=v_t[:, kb, :], rhs=p_t[:, c0:512], start=(kb == 0), stop=(kb == nkb - 1))
            return e.matmul(bL[:, c0:512], lhsT=ones_bf[:], rhs=p_t[:, c0:512], start=(kb == 0), stop=(kb == nkb - 1))
        op('tensor', pv, reads=[p_t.name, v_t.name, 'ones_bf'], writes=[bOn, bLn])
        if kb == nkb - 1:
            oi2 = cntb['ost'] % 2
            cntb['ost'] += 1
            r_t, o_t = rl[oi2], ost[oi2]
            op('vector', lambda e, r_t=r_t, bL=bL: e.reciprocal(out=r_t[:], in_=bL), reads=[bLn], writes=[r_t.name])
            op('vector', lambda e, r_t=r_t, o_t=o_t, bO=bO: e.tensor_tensor(out=o_t[:], in0=bO, in1=r_t[:], op=ALU.mult),
               reads=[bOn, r_t.name], writes=[o_t.name])
            rrow = ((((qg // 4) * 4 + 2 + hl // 2) * 4 + qg % 4) * 2 + hl % 2) * P
            yn_ = 'yb_a_%d_%d' % (hl, qg)
            ybn.append(yn_)
            op('sync', lambda e, o_t=o_t, rrow=rrow: e.dma_start(out=ybuf[rrow:rrow + P, :], in_=o_t[:]),
               reads=[o_t.name], writes=[yn_], dma=o_t.name)

    while NH_RUN == 4 and mb_next[0] < 48:
        mod_block(mb_next[0], ringA, 2, cB2, ps[7], psn[7], bt1, modblk1, postr1, gblk1, loaded=mb_loaded[0])
        mb_loaded[0] = None
        mb_next[0] += 1
    cv_issue(len(cv_jobs))
    op('vector', lambda e: e.scalar_tensor_tensor(out=gain2c[:], in0=sc2c[:], scalar=1.0, in1=pre2c[:], op0=ALU.add, op1=ALU.mult),
       reads=allc(sc2c) + ['pre2c'], writes=['gain2c'])

    if stop_after == 2:
        if debug:
            dt_ = nc.alloc_sbuf_tensor_at("dbgt", [P, 8192], F32, offset=LIMIT - 8192 * 4 - 64)
            qd = nc.alloc_sbuf_tensor_at("qd", [P, 2, 2048], BF16, offset=LIMIT - 8192 * 4 - 64 - 8192)
            T.barrier()
            op('vector', lambda e: e.memset(dt_[:], 0.0), writes=['dbgt'])
            op('sync', lambda e: e.dma_start(out=qd[:, 0, :], in_=ybuf[0:P, :]), writes=['qd'], dma='dq0')
            op('sync', lambda e: e.dma_start(out=qd[:, 1, :], in_=ybuf[24 * P:25 * P, :]), writes=['qd1'], dma='dq1')
            op('vector', lambda e: e.tensor_copy(out=dt_[:, 0:2048], in_=qd[:, 0, :]), reads=['qd', 'dbgt'], writes=['dbgt'])
            op('vector', lambda e: e.tensor_copy(out=dt_[:, 2048:4096], in_=qd[:, 1, :]), reads=['qd1', 'dbgt'], writes=['dbgt'])
            op('vector', lambda e: e.tensor_copy(out=dt_[:, 4096:4352], in_=CumF[:].rearrange("p h kb -> p (h kb)")), reads=['dbgt'], writes=['dbgt'])
            op('sync', lambda e: e.dma_start(out=dbg[:, :], in_=dt_[:]), reads=['dbgt'], dma='dbgs')
        return finish()

    exchange_group(3, [n_ for n_ in ybn if n_.startswith('yb_a_2_') or n_.startswith('yb_a_3_')])
    T.barrier()

    A = Arena(P0)
    aT = A.alloc([P, NFC, TT2], BF16, "aT")
    zs = [nc.alloc_sbuf_tensor_at("zs%d" % i, [P, D], F32, offset=P0 + i * D * 4) for i in range(4)]
    actT = A.alloc([P, 32, TT2], BF16, "actT")
    ring2 = [A.alloc([P, 8192], BF16, "ring2_%d" % i) for i in range(2)]
    v16 = lambda t: t[:].rearrange("p (k n) -> p k n", n=512)
    v32 = lambda t: t[:].rearrange("p (k n) -> p k n", n=256)
    xs2 = A.alloc([P, D], F32, "xs2")
    Grow = A.alloc([P, D], F32, "Grow")
    xn2 = A.alloc([P, D], BF16, "xn2")
    scr = [A.alloc([P, 512], F32, "scr%d" % i) for i in range(2)]
    sq = A.alloc([P, 512], BF16, "sq")
    sg = [A.alloc([P, 512], F32, "sg%d" % i) for i in range(2)]
    ssq = A0.alloc([P, 32], F32, "ssq")
    actn = ['act%d' % k for k in range(32)]
    aTn = ['aT%d' % k for k in range(NFC)]
    c2 = {'ring': 0, 'sg': 0, 'scr': 0}

    def ring2_load(view, src_ap):
        i = c2['ring'] % 2
        c2['ring'] += 1
        rn = 'ring%d' % i
        dst = view(ring2[i])
        op('gpsimd', lambda e: e.dma_start(out=dst, in_=src_ap), writes=[rn], dma=rn)
        return ring2[i], rn

    NT2_RUN = NT2 if not (debug and stop_after == 3) else 1
    for tt in range(NT2_RUN):
        t0 = tt * TT2
        for gr in range(16):
            rr0 = gr * 1024 + tt * 256
            op('sync', lambda e, gr=gr, rr0=rr0: e.dma_start(out=actT[:, gr * 2:gr * 2 + 2, :],
                                                           in_=ymine[rr0:rr0 + 256, :].rearrange("(l p) t -> p l t", p=P)),
               writes=actn[gr * 2:gr * 2 + 2], dma='yl%d' % (gr % 4))
        op('sync', lambda e: e.dma_start(out=Grow[:], in_=Gs[0:1, :].partition_broadcast(P)), reads=GsN[0], writes=['Grow'], dma='grow')
        for grp in range(2):
            bnk, bnkn = ps[grp], psn[grp]
            kks = list(range(16, 32)) if grp == 0 else list(range(0, 16))
            for n_, kk in enumerate(kks):
                op('vector', lambda e, kk=kk: e.tensor_tensor(out=sq[:], in0=actT[:, kk, :], in1=actT[:, kk, :], op=ALU.mult),
                   reads=[actn[kk]], writes=['sq'])
                op('tensor', lambda e, bnk=bnk, n_=n_: e.matmul(bnk, lhsT=ones_bf[:], rhs=sq[:], start=(n_ == 0), stop=(n_ == 15)),
                   reads=['sq', 'ones_bf'], writes=[bnkn])
            emit_rstd(scr[grp][:], bnk, 1.0 / 2048, [bnkn], scr[grp].name)
        for kk in range(32):
            grp = 0 if kk >= 16 else 1
            op('vector', lambda e, kk=kk, grp=grp: e.scalar_tensor_tensor(out=actT[:, kk, :], in0=actT[:, kk, :], scalar=gyc[:, kk:kk + 1],
                                                                         in1=scr[grp][:], op0=ALU.mult, op1=ALU.mult),
               reads=[actn[kk], scr[grp].name, 'gyc'], writes=[actn[kk]])
        for db in range(8):
            bset = (db % 2) * 4
            for kg in range(2):
                if True:
                    rt, rn = ring2_load(v16, w_out[kg * 2048:(kg + 1) * 2048, db * 512:(db + 1) * 512].rearrange("(k p) n -> p k n", p=P))

                def mmo(e, rt=v16(rt), kg=kg, bset=bset):
                    ins = None
                    for s4 in range(4):
                        for k in range(16):
                            ins = e.matmul(ps[bset + s4], lhsT=actT[:, kg * 16 + k, s4 * P:(s4 + 1) * P], rhs=rt[:, k, :],
                                           start=(kg == 0 and k == 0), stop=(kg == 1 and k == 15))
                    return ins
                op('tensor', mmo, reads=actn + [rn], writes=[psn[bset + s4] for s4 in range(4)])
            for s4 in range(4):
                bk, bkn = ps[bset + s4], psn[bset + s4]
                op('scalar', lambda e, bk=bk, s4=s4, db=db: e.activation(out=junk_bf[:], in_=bk, func=AF.Square, accum_out=ssq[:, s4 * 8 + db:s4 * 8 + db + 1]),
                   reads=[bkn], writes=['junk_bf', 'ssq%d_%d' % (s4, db)])
                op('vector', lambda e, bk=bk, s4=s4, db=db: e.tensor_copy(out=zs[s4][:, db * 512:(db + 1) * 512], in_=bk),
                   reads=[bkn], writes=['zs%d_%d' % (s4, db)])
        for s4 in range(4):
            r0 = t0 + s4 * P
            op('sync', lambda e, r0=r0: e.dma_start(out=xs2[:], in_=x_chunk[r0:r0 + P, :]), writes=['xs2'], dma='xs2')
            ssn = ['ssq%d_%d' % (s4, db) for db in range(8)]
            st1 = stat[:, s4:s4 + 1]
            op('vector', lambda e, s4=s4, st1=st1: e.tensor_reduce(out=st1, in_=ssq[:, s4 * 8:(s4 + 1) * 8], axis=mybir.AxisListType.X, op=ALU.add),
               reads=ssn, writes=['st1_%d' % s4])
            emit_rstd(st1, st1, 1.0 / D, ['st1_%d' % s4], 'st1_%d' % s4)
            zn = ['zs%d_%d' % (s4, db) for db in range(8)]
            op('vector', lambda e, s4=s4, st1=st1: e.scalar_tensor_tensor(out=zs[s4][:], in0=zs[s4][:], scalar=st1, in1=Grow[:], op0=ALU.mult, op1=ALU.mult),
               reads=zn + ['st1_%d' % s4, 'Grow'], writes=['zs%d' % s4])
            op('vector', lambda e, s4=s4: e.tensor_tensor(out=xs2[:], in0=xs2[:], in1=zs[s4][:], op=ALU.add),
               reads=['xs2', 'zs%d' % s4], writes=['xs2'])
            op('sync', lambda e, r0=r0: e.dma_start(out=x1s[r0:r0 + P, :], in_=xs2[:]), reads=['xs2'], writes=['x1s_%d' % (tt * 4 + s4)], dma='x1st')
            st2 = stat[:, 8 + s4:9 + s4]
            op('scalar', lambda e, st2=st2: e.activation(out=xn2[:], in_=xs2[:], func=AF.Square, accum_out=st2),
               reads=['xs2'], writes=['xn2', 'st2_%d' % s4])
            emit_rstd(st2, st2, 1.0 / D, ['st2_%d' % s4], 'st2_%d' % s4)
            op('vector', lambda e, st2=st2: e.tensor_scalar(out=xn2[:], in0=xs2[:], scalar1=st2, scalar2=None, op0=ALU.mult),
               reads=['xs2', 'st2_%d' % s4], writes=['xn2'])
            for g4 in range(4):
                half = g4 % 2
                tv = tbf[:, half * 1024:(half + 1) * 1024]
                bn = psn[half]

                def tp2(e, g4=g4, tv=tv):
                    ins = None
                    for kk in range(8):
                        k = g4 * 8 + kk
                        ins = e.transpose(out=tv[:, kk * P:(kk + 1) * P], in_=xn2[:, k * P:(k + 1) * P], identity=ident_bf[:])
                    return ins
                op('tensor', tp2, reads=['xn2'], writes=[bn])

                def ev2a(e, g4=g4, tv=tv, s4=s4):
                    ins = None
                    for kk in range(8):
                        k = g4 * 8 + kk
                        ins = e.activation(out=actT[:, k, s4 * P:(s4 + 1) * P], in_=tv[:, kk * P:(kk + 1) * P], func=AF.Identity,
                                           scale=gain2c[:, k:k + 1], bias=shift2c[:, k:k + 1])
                    return ins

                def ev2v(e, g4=g4, tv=tv, s4=s4):
                    ins = None
                    for kk in range(8):
                        k = g4 * 8 + kk
                        ins = e.tensor_scalar(out=actT[:, k, s4 * P:(s4 + 1) * P], in0=tv[:, kk * P:(kk + 1) * P],
                                              scalar1=gain2c[:, k:k + 1], scalar2=shift2c[:, k:k + 1], op0=ALU.mult, op1=ALU.add)
                    return ins
                if g4 % 2 == 0:
                    op('scalar', ev2a, reads=[bn, 'gain2c'] + shift2n, writes=[actn[g4 * 8 + kk] for kk in range(8)])
                else:
                    op('vector', ev2v, reads=[bn, 'gain2c'] + shift2n, writes=[actn[g4 * 8 + kk] for kk in range(8)])
        op('sync', lambda e: e.dma_start(out=Grow[:], in_=Gs[1:2, :].partition_broadcast(P)), reads=GsN[1], writes=['Grow'], dma='grow')
        for fb in range(NFC // 2):
            f0 = fb * 256
            rg, rgn = ring2_load(v32, w_gate[:, f0:f0 + 256].rearrange("(k p) n -> p k n", p=P))
            ru, run = ring2_load(v32, w_up[:, f0:f0 + 256].rearrange("(k p) n -> p k n", p=P))
            rgv = v32(rg)
            ruv = v32(ru)
            bset = (fb % 2) * 4

            def mmg(e, rv=rgv, bset=bset, o=0):
                ins = None
                for cch in range(2):
                    for k in range(32):
                        ins = e.matmul(ps[bset + o + cch], lhsT=rv[:, k, cch * P:(cch + 1) * P], rhs=actT[:, k, :], start=(k == 0), stop=(k == 31))
                return ins
            op('tensor', mmg, reads=actn + [rgn], writes=[psn[bset], psn[bset + 1]])
            op('tensor', lambda e, rv=ruv, bset=bset: mmg(e, rv, bset, 2), reads=actn + [run], writes=[psn[bset + 2], psn[bset + 3]])
            for cch in range(2):
                fc = fb * 2 + cch
                si = c2['sg'] % 2
                c2['sg'] += 1
                s_t = sg[si]
                bg, bu = ps[bset + cch], ps[bset + 2 + cch]
                op('scalar', lambda e, s_t=s_t, bg=bg: e.activation(out=s_t[:], in_=bg, func=AF.Silu), reads=[psn[bset + cch]], writes=[s_t.name])
                op('vector', lambda e, s_t=s_t, bu=bu, fc=fc: e.tensor_tensor(out=aT[:, fc, :], in0=bu, in1=s_t[:], op=ALU.mult),
                   reads=[psn[bset + 2 + cch], s_t.name], writes=[aTn[fc]])
        for db in range(8):
            bset = (db % 2) * 4
            ngr = (NFC + 15) // 16
            for kg in range(ngr):
                nk = min(16, NFC - kg * 16)
                if STAGE_BF16:
                    rt, rn = ring2_load(lambda t, nk=nk: t[:, 0:nk * 512], wd_s[(db * 6 + kg) * P:(db * 6 + kg + 1) * P, 0:nk * 512])
                else:
                    rt, rn = ring2_load(lambda t, nk=nk: v16(t)[:, 0:nk, :],
                                        w_down[kg * 2048:kg * 2048 + nk * P, db * 512:(db + 1) * 512].rearrange("(k p) n -> p k n", p=P))

                def mmd(e, rt=v16(rt), kg=kg, nk=nk, bset=bset, ngr=ngr):
                    ins = None
                    for s4 in range(4):
                        for k in range(nk):
                            ins = e.matmul(ps[bset + s4], lhsT=aT[:, kg * 16 + k, s4 * P:(s4 + 1) * P], rhs=rt[:, k, :],
                                           start=(kg == 0 and k == 0), stop=(kg == ngr - 1 and k == nk - 1))
                    return ins
                op('tensor', mmd, reads=aTn[kg * 16:kg * 16 + nk] + [rn], writes=[psn[bset + s4] for s4 in range(4)])
            for s4 in range(4):
                bk, bkn = ps[bset + s4], psn[bset + s4]
                op('scalar', lambda e, bk=bk, s4=s4, db=db: e.activation(out=junk_bf[:], in_=bk, func=AF.Square, accum_out=ssq[:, s4 * 8 + db:s4 * 8 + db + 1]),
                   reads=[bkn], writes=['junk_bf', 'ssq%d_%d' % (s4, db)])
                ci = c2['scr'] % 2
                c2['scr'] += 1
                f_t = scr[ci]
                op('vector', lambda e, bk=bk, f_t=f_t, db=db: e.tensor_tensor(out=f_t[:], in0=bk, in1=Grow[:, db * 512:(db + 1) * 512], op=ALU.mult),
                   reads=[bkn, 'Grow'], writes=[f_t.name])
                r0 = t0 + s4 * P
                op('sync', lambda e, f_t=f_t, r0=r0, db=db: e.dma_start(out=accs[r0:r0 + P, db * 512:(db + 1) * 512], in_=f_t[:]),
                   reads=[f_t.name], writes=['acc_%d_%d' % (tt * 4 + s4, db)], dma=f_t.name + 'st')
        for s4 in range(4):
            r0 = t0 + s4 * P
            a_t = zs[s4 % 2]
            an = 'zs%d' % (s4 % 2)
            op('sync', lambda e, a_t=a_t, r0=r0: e.dma_start(out=a_t[:], in_=accs[r0:r0 + P, :]),
               reads=['acc_%d_%d' % (tt * 4 + s4, db) for db in range(8)], writes=[an] + ['zs%d_%d' % (s4 % 2, db) for db in range(8)], dma=an + 'ld')
            op('sync', lambda e, r0=r0: e.dma_start(out=xs2[:], in_=x1s[r0:r0 + P, :]), reads=['x1s_%d' % (tt * 4 + s4)], writes=['xs2'], dma='xs2')
            ssn = ['ssq%d_%d' % (s4, db) for db in range(8)]
            st3 = stat[:, 16 + s4:17 + s4]
            op('vector', lambda e, s4=s4, st3=st3: e.tensor_reduce(out=st3, in_=ssq[:, s4 * 8:(s4 + 1) * 8], axis=mybir.AxisListType.X, op=ALU.add),
               reads=ssn, writes=['st3_%d' % s4])
            emit_rstd(st3, st3, 1.0 / D, ['st3_%d' % s4], 'st3_%d' % s4)
            op('vector', lambda e, a_t=a_t, st3=st3: e.scalar_tensor_tensor(out=xs2[:], in0=a_t[:], scalar=st3, in1=xs2[:], op0=ALU.mult, op1=ALU.add),
               reads=[an, 'xs2', 'st3_%d' % s4] + ['zs%d_%d' % (s4 % 2, db) for db in range(8)], writes=['xs2'])
            op('sync', lambda e, r0=r0: e.dma_start(out=out[r0:r0 + P, :], in_=xs2[:]), reads=['xs2'], writes=['out_%d' % (tt * 4 + s4)], dma='outst')

    if debug and stop_after == 3:
        pass
    return finish()


def col_layout(v):
    return np.ascontiguousarray(np.asarray(v, dtype=np.float32).reshape(-1, P).T)


def make_in_maps(inputs):
    x = np.asarray(inputs["x"], dtype=np.float32)
    c = np.asarray(inputs["c"], dtype=np.float32)
    w_in = np.asarray(inputs["w_in"], dtype=np.float32)[0]
    w_out = np.asarray(inputs["w_out"], dtype=np.float32)[0]
    conv_w = np.asarray(inputs["conv_w"], dtype=np.float32)[0]
    b_f = np.asarray(inputs["b_f"], dtype=np.float32)[0]
    aon = np.asarray(inputs["attn_out_norm"], dtype=np.float32)[0]
    con = np.asarray(inputs["conv_out_norm"], dtype=np.float32)[0]
    shared = {
        "w_ada": np.ascontiguousarray(np.asarray(inputs["w_ada"], dtype=np.float32)[0]),
        "b_ada": np.ascontiguousarray(np.asarray(inputs["b_ada"], dtype=np.float32)[0][None, :]),
        "pre1_col": col_layout(inputs["pre_norm_mix"][0]),
        "pre2_col": col_layout(inputs["pre_norm_ffn"][0]),
        "post1_row": np.ascontiguousarray(np.asarray(inputs["post_norm_mix"], dtype=np.float32)[0][None, :]),
        "post2_row": np.ascontiguousarray(np.asarray(inputs["post_norm_ffn"], dtype=np.float32)[0][None, :]),
        "w_gate": np.ascontiguousarray(np.asarray(inputs["w_gate"], dtype=np.float32)[0]),
        "w_up": np.ascontiguousarray(np.asarray(inputs["w_up"], dtype=np.float32)[0]),
        "w_down": np.ascontiguousarray(np.asarray(inputs["w_down"], dtype=np.float32)[0]),
    }
    mchunks = []
    lbmap = {0: (4, 5), 1: (6, 7), 2: (0, 1), 3: (2, 3)}
    for g_ in range(4):
        for r in range(4):
            for lb2 in range(2):
                lb = lbmap[g_][lb2]
                mchunks.append(4 * r + lb if lb < 4 else 16 + 4 * r + (lb - 4))
    gy_full = np.concatenate([aon, con])
    gy_perm = np.concatenate([gy_full[m * P:(m + 1) * P] for m in mchunks])
    shared["gy_col"] = col_layout(gy_perm)
    shared["w_out_perm"] = np.ascontiguousarray(np.concatenate([w_out[m * P:(m + 1) * P] for m in mchunks], axis=0))
    maps = []
    for core in range(NCORES):
        b, g = divmod(core, 4)
        m = dict(shared)
        m["x_full"] = np.ascontiguousarray(x[b])
        m["x_chunk"] = np.ascontiguousarray(x[b, g * TOK2:(g + 1) * TOK2])
        m["c_col"] = col_layout(c[b])
        sl = lambda base: w_in[:, base + 512 * g: base + 512 * g + 512]
        wq, wk, wv = sl(0), sl(2048), sl(4096)
        wf = w_in[:, 6144 + 4 * g: 6144 + 4 * g + 4]
        wgb, wgc, wu = sl(6160), sl(6160 + 2048), sl(6160 + 4096)
        m["w1"] = np.ascontiguousarray(np.concatenate([wq, wk, wgb, wgc, wu, wv, wf], axis=1))
        m["wf_col"] = np.ascontiguousarray(wf.reshape(32, P, 4).transpose(1, 0, 2).reshape(P, 128))
        m["bf_row"] = np.ascontiguousarray(b_f[4 * g:4 * g + 4][None, :])
        cw = conv_w[:, 512 * g:512 * g + 512]
        m["convw_col"] = np.ascontiguousarray(cw.reshape(3, 4, P).transpose(2, 1, 0).reshape(P, 12))
        maps.append(m)
    return maps


_NC_CACHE = {}


def kernel(**inputs):
    if "nc" not in _NC_CACHE:
        _NC_CACHE["nc"] = build_nc()
    nc = _NC_CACHE["nc"]
    in_maps = make_in_maps(inputs)
    res = run_bass_kernel_spmd(nc, in_maps, core_ids=list(range(NCORES)))
    outp = np.empty((2, S, D), dtype=np.float32)
    for core in range(NCORES):
        b, g = divmod(core, 4)
        outp[b, g * TOK2:(g + 1) * TOK2] = np.asarray(res.results[core]["out"])
    return outp
```
